# Optimizing a Trainium2 kernel written in Bass

```python
import jax, jax.numpy as jnp
from jax import lax
import numpy as np

D_MODEL = 1024
BATCH = 4
SEQ = 8192
DEPTH = 2

GRID_W = 64
CTX_LEN = 256
RMS_EPS = 1e-6
N_BRANCH = 3

F_WIDTH = D_MODEL // 4
F_GROUPS = 4
F_GROUP_DIM = F_WIDTH // F_GROUPS

S5_WIDTH = D_MODEL // 4
S5_GROUP_DIM = 16
S5_GROUPS = S5_WIDTH // S5_GROUP_DIM
S5_STATE = 64
S5_DT_MIN = 1e-3
S5_DT_MAX = 1e-1

NA_WIDTH = D_MODEL // 2
NA_HEAD_DIM = 64
NA_HEADS = NA_WIDTH // NA_HEAD_DIM
NA_KH = 8
NA_KW = 16

D_FF = 4 * D_MODEL

OFF_S5 = F_WIDTH
OFF_Q = OFF_S5 + S5_WIDTH
OFF_K = OFF_Q + NA_WIDTH
OFF_V = OFF_K + NA_WIDTH
OFF_G = OFF_V + NA_WIDTH
IN_WIDTH = OFF_G + N_BRANCH * D_MODEL

kernel_name = "hybrid_fnet_s5_natten_diffusion_block"


def rmsnorm(x, g):
    xf = x.astype(jnp.float32)
    y = xf * lax.rsqrt(jnp.mean(xf * xf, axis=-1, keepdims=True) + RMS_EPS)
    return (y * g.astype(jnp.float32)).astype(x.dtype)


def modulate(h, shift, scale):
    return h * (1 + scale) + shift


def sqrelu_mlp(h, w1, w2):
    return jnp.square(jax.nn.relu(h @ w1)) @ w2


def fourier_mix(z):
    b, l, _ = z.shape
    zg = z.astype(jnp.float32).reshape(b, l, F_GROUPS, F_GROUP_DIM)
    y = jnp.fft.fftn(zg, axes=(1, 3), norm="ortho").real
    return y.reshape(b, l, F_WIDTH).astype(z.dtype)


def _scan_op(e1, e2):
    a1, b1 = e1
    a2, b2 = e2
    return a1 * a2, a2 * b1 + b2


def s5_discretise(lam_re, lam_im, log_dt, b_re, b_im):
    lam = lax.complex(jnp.minimum(lam_re.astype(jnp.float32), -1e-4), lam_im.astype(jnp.float32))
    dt = jnp.exp(log_dt.astype(jnp.float32))[:, None]
    lbar = jnp.exp(lam * dt)
    bmat = lax.complex(b_re.astype(jnp.float32), b_im.astype(jnp.float32))
    bbar = ((lbar - 1) / lam)[..., None] * bmat
    return lbar, bbar


def s5_drive(u, bbar):
    return jnp.einsum('blgn,gpn->blgp', u.astype(jnp.complex64), bbar)


def linear_scan(bu, lbar, h0, reverse):
    if h0 is not None:
        idx = -1 if reverse else 0
        bu = bu.at[:, idx].add(lbar * h0)
    a = jnp.broadcast_to(lbar, bu.shape)
    _, h = lax.associative_scan(_scan_op, (a, bu), axis=1, reverse=reverse)
    return h


def s5_readout(h, cmat):
    return jnp.einsum('blgp,gnp->blgn', h, cmat).real


def s5_glu(y, w_glu):
    g = jax.nn.gelu(y)
    return g * jax.nn.sigmoid(g @ w_glu.astype(jnp.float32))


def s5_branch(zx, zc, lam_re, lam_im, log_dt, b_re, b_im, c_re, c_im, d_skip, w_glu, need_ctx_out):
    b, l, _ = zx.shape
    lc = zc.shape[1]
    ux = zx.astype(jnp.float32).reshape(b, l, S5_GROUPS, S5_GROUP_DIM)
    uc = zc.astype(jnp.float32).reshape(b, lc, S5_GROUPS, S5_GROUP_DIM)
    d = d_skip.astype(jnp.float32)
    yx = ux * d
    yc = uc * d if need_ctx_out else None
    for direction, reverse in ((0, False), (1, True)):
        lbar, bbar = s5_discretise(lam_re[direction], lam_im[direction], log_dt[direction],
                                   b_re[direction], b_im[direction])
        cmat = lax.complex(c_re[direction].astype(jnp.float32), c_im[direction].astype(jnp.float32))
        hc = linear_scan(s5_drive(uc, bbar), lbar, None, reverse)
        hc_last = hc[:, 0] if reverse else hc[:, -1]
        hx = linear_scan(s5_drive(ux, bbar), lbar, hc_last, reverse)
        yx = yx + s5_readout(hx, cmat)
        if need_ctx_out:
            yc = yc + s5_readout(hc, cmat)
    out_x = s5_glu(yx.reshape(b, l, S5_WIDTH), w_glu).astype(zx.dtype)
    out_c = s5_glu(yc.reshape(b, lc, S5_WIDTH), w_glu).astype(zc.dtype) if need_ctx_out else None
    return out_x, out_c


def na_latent(q, k, v, kc, vc, rpb):
    b, l, h, dh = q.shape
    rows = l // GRID_W
    kh = min(NA_KH, rows)
    qg = q.reshape(b, rows, GRID_W, h, dh)
    kg = k.reshape(b, rows, GRID_W, h, dh)
    vg = v.reshape(b, rows, GRID_W, h, dh)
    col = np.arange(GRID_W)
    col_start = np.clip(col - NA_KW // 2, 0, GRID_W - NA_KW)
    col_idx = col_start[:, None] + np.arange(NA_KW)[None, :]
    col_rel = col_idx - col[:, None] + NA_KW - 1
    rpb_cols = rpb.astype(jnp.float32)[:, :, col_rel]
    scale = dh ** -0.5

    def one_row(r):
        rs = jnp.clip(r - kh // 2, 0, rows - kh)
        k_band = lax.dynamic_slice_in_dim(kg, rs, kh, axis=1)
        v_band = lax.dynamic_slice_in_dim(vg, rs, kh, axis=1)
        k_win = k_band[:, :, col_idx]
        v_win = v_band[:, :, col_idx]
        q_row = lax.dynamic_index_in_dim(qg, r, axis=1, keepdims=False)
        row_rel = rs + jnp.arange(kh) - r + NA_KH - 1
        bias = jnp.take(rpb_cols, row_rel, axis=1).transpose(0, 2, 1, 3)
        s_loc = jnp.einsum('bwhd,bkwjhd->bhwkj', q_row, k_win).astype(jnp.float32) * scale + bias[None]
        s_ctx = jnp.einsum('bwhd,bmhd->bhwm', q_row, kc).astype(jnp.float32) * scale
        s = jnp.concatenate([s_loc.reshape(b, h, GRID_W, kh * NA_KW), s_ctx], axis=-1)
        p = jax.nn.softmax(s, axis=-1).astype(v.dtype)
        p_loc = p[..., :kh * NA_KW].reshape(b, h, GRID_W, kh, NA_KW)
        p_ctx = p[..., kh * NA_KW:]
        return (jnp.einsum('bhwkj,bkwjhd->bwhd', p_loc, v_win)
                + jnp.einsum('bhwm,bmhd->bwhd', p_ctx, vc))

    out = lax.map(one_row, jnp.arange(rows))
    return jnp.moveaxis(out, 0, 1).reshape(b, l, h * dh)


def ctx_attention(qc, kc, vc):
    b, lc, h, dh = qc.shape
    s = jnp.einsum('bmhd,bnhd->bhmn', qc, kc).astype(jnp.float32) * dh ** -0.5
    p = jax.nn.softmax(s, axis=-1).astype(vc.dtype)
    return jnp.einsum('bhmn,bnhd->bmhd', p, vc).reshape(b, lc, h * dh)


def gated_merge(fa, sb, nc, gate_logits, w_br_a, w_br_b, w_br_c, w_out):
    g = jax.nn.sigmoid(gate_logits)
    m = (g[..., :D_MODEL] * (fa @ w_br_a)
         + g[..., D_MODEL:2 * D_MODEL] * (sb @ w_br_b)
         + g[..., 2 * D_MODEL:] * (nc @ w_br_c))
    return m @ w_out


def mixer_sublayer(hx, hc, w_in, w_br_a, w_br_b, w_br_c, w_out, lam_re, lam_im, log_dt,
                   b_re, b_im, c_re, c_im, d_skip, w_glu, rpb, need_ctx_out):
    b, l, _ = hx.shape
    lc = hc.shape[1]
    heads = (NA_HEADS, NA_HEAD_DIM)
    zx = hx @ w_in
    zc_s5 = hc @ w_in[:, OFF_S5:OFF_Q]
    zc_kv = hc @ w_in[:, OFF_K:OFF_G]
    kc = zc_kv[..., :NA_WIDTH].reshape(b, lc, *heads)
    vc = zc_kv[..., NA_WIDTH:].reshape(b, lc, *heads)

    fa_x = fourier_mix(zx[..., :OFF_S5])
    sb_x, sb_c = s5_branch(zx[..., OFF_S5:OFF_Q], zc_s5, lam_re, lam_im, log_dt,
                           b_re, b_im, c_re, c_im, d_skip, w_glu, need_ctx_out)
    qx = zx[..., OFF_Q:OFF_K].reshape(b, l, *heads)
    kx = zx[..., OFF_K:OFF_V].reshape(b, l, *heads)
    vx = zx[..., OFF_V:OFF_G].reshape(b, l, *heads)
    nc_x = na_latent(qx, kx, vx, kc, vc, rpb)
    out_x = gated_merge(fa_x, sb_x, nc_x, zx[..., OFF_G:], w_br_a, w_br_b, w_br_c, w_out)

    out_c = None
    if need_ctx_out:
        fa_c = fourier_mix(hc @ w_in[:, :OFF_S5])
        qc = (hc @ w_in[:, OFF_Q:OFF_K]).reshape(b, lc, *heads)
        nc_c = ctx_attention(qc, kc, vc)
        out_c = gated_merge(fa_c, sb_c, nc_c, hc @ w_in[:, OFF_G:], w_br_a, w_br_b, w_br_c, w_out)
    return out_x, out_c


def setup_inputs(seed: int = 0) -> dict:
    key = jax.random.key(seed)
    ks = iter(jax.random.split(key, 32))
    f32 = jnp.float32

    def nrm(shape, scale):
        return jax.random.normal(next(ks), shape, f32) * scale

    L, G, P, N = DEPTH, S5_GROUPS, S5_STATE, S5_GROUP_DIM
    x = nrm((BATCH, SEQ, D_MODEL), 1.0)
    c = nrm((BATCH, D_MODEL), 1.0)
    ctx = nrm((BATCH, CTX_LEN, D_MODEL), 1.0)
    c_ctx = nrm((D_MODEL,), 1.0)
    w_mod = nrm((L, D_MODEL, 6 * D_MODEL), D_MODEL ** -0.5)
    b_mod = nrm((L, 6 * D_MODEL), 0.01)
    g_norm1 = 1.0 + nrm((L, D_MODEL), 0.02)
    g_norm2 = 1.0 + nrm((L, D_MODEL), 0.02)
    w_in = nrm((L, D_MODEL, IN_WIDTH), D_MODEL ** -0.5)
    w_br_a = nrm((L, F_WIDTH, D_MODEL), F_WIDTH ** -0.5)
    w_br_b = nrm((L, S5_WIDTH, D_MODEL), S5_WIDTH ** -0.5)
    w_br_c = nrm((L, NA_WIDTH, D_MODEL), NA_WIDTH ** -0.5)
    w_out = nrm((L, D_MODEL, D_MODEL), D_MODEL ** -0.5)
    s5_lam_re = -0.5 + nrm((L, 2, G, P), 0.01)
    s5_lam_im = jnp.pi * jnp.arange(P, dtype=f32) + nrm((L, 2, G, P), 0.01)
    s5_log_dt = jax.random.uniform(next(ks), (L, 2, G), f32,
                                   minval=math_log(S5_DT_MIN), maxval=math_log(S5_DT_MAX))
    s5_b_re = nrm((L, 2, G, P, N), (2 * N) ** -0.5)
    s5_b_im = nrm((L, 2, G, P, N), (2 * N) ** -0.5)
    s5_c_re = nrm((L, 2, G, N, P), (2 * P) ** -0.5)
    s5_c_im = nrm((L, 2, G, N, P), (2 * P) ** -0.5)
    s5_d = nrm((L, G, N), 1.0)
    s5_w_glu = nrm((L, S5_WIDTH, S5_WIDTH), S5_WIDTH ** -0.5)
    na_rpb = nrm((L, NA_HEADS, 2 * NA_KH - 1, 2 * NA_KW - 1), 0.02)
    w_ff1 = nrm((L, D_MODEL, D_FF), D_MODEL ** -0.5)
    w_ff2 = nrm((L, D_FF, D_MODEL), D_FF ** -0.5)
    g_final = 1.0 + nrm((D_MODEL,), 0.02)
    return {"x": x, "c": c, "ctx": ctx, "c_ctx": c_ctx, "w_mod": w_mod, "b_mod": b_mod,
            "g_norm1": g_norm1, "g_norm2": g_norm2, "w_in": w_in, "w_br_a": w_br_a,
            "w_br_b": w_br_b, "w_br_c": w_br_c, "w_out": w_out, "s5_lam_re": s5_lam_re,
            "s5_lam_im": s5_lam_im, "s5_log_dt": s5_log_dt, "s5_b_re": s5_b_re, "s5_b_im": s5_b_im,
            "s5_c_re": s5_c_re, "s5_c_im": s5_c_im, "s5_d": s5_d, "s5_w_glu": s5_w_glu,
            "na_rpb": na_rpb, "w_ff1": w_ff1, "w_ff2": w_ff2, "g_final": g_final}


def math_log(v):
    return float(np.log(v))


def reference(x, c, ctx, c_ctx, w_mod, b_mod, g_norm1, g_norm2, w_in, w_br_a, w_br_b, w_br_c, w_out,
              s5_lam_re, s5_lam_im, s5_log_dt, s5_b_re, s5_b_im, s5_c_re, s5_c_im, s5_d, s5_w_glu,
              na_rpb, w_ff1, w_ff2, g_final):
    for i in range(DEPTH):
        need_ctx_out = i < DEPTH - 1
        mod_x = jax.nn.silu(c) @ w_mod[i] + b_mod[i]
        sh1, sc1, gt1, sh2, sc2, gt2 = jnp.split(mod_x[:, None, :], 6, axis=-1)
        n_ctx_mod = 6 if need_ctx_out else 2
        mod_c = jax.nn.silu(c_ctx) @ w_mod[i][:, :n_ctx_mod * D_MODEL] + b_mod[i][:n_ctx_mod * D_MODEL]
        mod_c = jnp.split(mod_c, n_ctx_mod)

        hx = modulate(rmsnorm(x, g_norm1[i]), sh1, sc1)
        hc = modulate(rmsnorm(ctx, g_norm1[i]), mod_c[0], mod_c[1])
        ox, oc = mixer_sublayer(hx, hc, w_in[i], w_br_a[i], w_br_b[i], w_br_c[i], w_out[i],
                                s5_lam_re[i], s5_lam_im[i], s5_log_dt[i], s5_b_re[i], s5_b_im[i],
                                s5_c_re[i], s5_c_im[i], s5_d[i], s5_w_glu[i], na_rpb[i], need_ctx_out)
        x = x + gt1 * ox
        h2 = modulate(rmsnorm(x, g_norm2[i]), sh2, sc2)
        x = x + gt2 * sqrelu_mlp(h2, w_ff1[i], w_ff2[i])

        if need_ctx_out:
            ctx = ctx + mod_c[2] * oc
            h2c = modulate(rmsnorm(ctx, g_norm2[i]), mod_c[3], mod_c[4])
            ctx = ctx + mod_c[5] * sqrelu_mlp(h2c, w_ff1[i], w_ff2[i])
    return rmsnorm(x, g_final)
```

```python
import numpy as np
from contextlib import ExitStack
import concourse.bass as bass
import concourse.mybir as mybir
from concourse.bass_utils import run_bass_kernel_spmd

F32 = mybir.dt.float32
BF16 = mybir.dt.bfloat16
AF = mybir.ActivationFunctionType
ALU = mybir.AluOpType

D = 1024
L = 8192
LC = 256
T = L + LC
DEPTH = 2
DIN = 5120
DFF = 4096
EPS = 1e-6
NEG = -30000.0
CUT = 99
DEBUG = 0

COMPUTE = ("pe", "dve", "act", "pool")
NDMA_SEM = 12


class Sched:
    def __init__(self, nc, stack):
        self.nc = nc
        self.ops = {e: [] for e in ("pe", "dve", "act", "pool", "sp")}
        self.sem = {}
        self.cnt = {}
        for e in COMPUTE:
            self.sem[e] = stack.enter_context(nc.semaphore("s_" + e))
            self.cnt[e] = 0
        self.dsem = {}
        self.dcnt = {}
        self.dnext = {}
        for q in ("sp", "pool", "act"):
            self.dsem[q] = [stack.enter_context(nc.semaphore("d_%s%d" % (q, j))) for j in range(NDMA_SEM)]
            self.dcnt[q] = [0] * NDMA_SEM
            self.dnext[q] = 0
        self.seen = {e: {} for e in self.ops}
        self.last_w = {}
        self.reads = {}
        self.out_events = []
        self.nblock = 0

    def _need(self, eng, reads, writes):
        need = {}

        def add(ev):
            if ev is None:
                return
            k, v = ev
            if need.get(k, 0) < v:
                need[k] = v

        for r in reads:
            add(self.last_w.get(r))
        for w in writes:
            add(self.last_w.get(w))
            for ev in self.reads.get(w, ()):
                add(ev)
        waits = []
        for k, v in need.items():
            if k == eng and eng == "pe":
                continue
            if self.seen[eng].get(k, 0) >= v:
                continue
            self.seen[eng][k] = v
            waits.append((k, v))
        return waits

    def _semof(self, k):
        if isinstance(k, tuple):
            return self.dsem[k[0]][k[1]]
        return self.sem[k]

    def _commit(self, ev, reads, writes):
        for w in writes:
            self.last_w[w] = ev
            self.reads[w] = []
        for r in reads:
            self.reads.setdefault(r, []).append(ev)

    def op(self, eng, fn, reads=(), writes=()):
        waits = self._need(eng, reads, writes)
        self.cnt[eng] += 1
        ev = (eng, self.cnt[eng])
        self.ops[eng].append((waits, fn, self.sem[eng], 1))
        self._commit(ev, reads, writes)
        return ev

    def dma(self, q, fn, reads=(), writes=(), is_output=False):
        j = self.dnext[q]
        self.dnext[q] = (j + 1) % NDMA_SEM
        key = (q, j)
        waits = self._need(q, reads, writes)
        if self.dcnt[q][j] > 0 and self.seen[q].get(key, 0) < self.dcnt[q][j]:
            self.seen[q][key] = self.dcnt[q][j]
            waits.append((key, self.dcnt[q][j]))
        self.dcnt[q][j] += 16
        ev = (key, self.dcnt[q][j])
        self.ops[q].append((waits, fn, self.dsem[q][j], 16))
        self._commit(ev, reads, writes)
        return ev

    def flush(self):
        tail = {e: [] for e in self.ops}
        for e in self.ops:
            for k in COMPUTE:
                if k != e and self.cnt[k] > self.seen[e].get(k, 0):
                    self.seen[e][k] = self.cnt[k]
                    tail[e].append((k, self.cnt[k]))
            for q in self.dsem:
                for j in range(NDMA_SEM):
                    v = self.dcnt[q][j]
                    if v > self.seen[e].get((q, j), 0):
                        self.seen[e][(q, j)] = v
                        tail[e].append(((q, j), v))
        nc = self.nc
        with nc.Block() as block:
            def replay(name):
                def body(e):
                    for waits, fn, sem, inc in self.ops[name]:
                        for k, v in waits:
                            e.wait_ge(self._semof(k), v)
                        fn(e).then_inc(sem, inc)
                    for k, v in tail[name]:
                        e.wait_ge(self._semof(k), v)
                return body
            block.sync(replay("sp"))
            block.tensor(replay("pe"))
            block.vector(replay("dve"))
            block.scalar(replay("act"))
            block.gpsimd(replay("pool"))
        for e in self.ops:
            self.ops[e] = []
        self.last_w = {}
        self.reads = {}
        self.nblock += 1


def _consts():
    c = {}
    c["ident"] = np.eye(128, dtype=np.float32)
    ij = np.outer(np.arange(64), np.arange(64)) * (2 * np.pi / 64)
    cs = np.zeros((2, 128, 128), np.float32)
    for g in range(2):
        cs[0, g * 64:(g + 1) * 64, g * 64:(g + 1) * 64] = np.cos(ij)
        cs[1, g * 64:(g + 1) * 64, g * 64:(g + 1) * 64] = np.sin(ij)
    c["cs64"] = cs
    r = np.arange(128)[:, None, None]
    cc = np.arange(64)[None, :, None]
    k1 = np.arange(128)[None, None, :]
    ang = (2 * np.pi / 8192) * ((k1 * (64 * r + cc)) % 8192)
    c["fn_tc"] = np.cos(ang).astype(np.float32)
    c["fn_tsn"] = (-np.sin(ang)).astype(np.float32)
    sc = 1.0 / np.sqrt(8192.0 * 64.0)
    a64 = (2 * np.pi / 64) * ((np.arange(64)[:, None] * np.arange(64)[None, :]) % 64)
    c["fn_w64"] = np.stack([np.cos(a64) * sc, np.sin(a64) * sc, -np.sin(a64) * sc]).astype(np.float32)
    scc = 1.0 / np.sqrt(256.0 * 64.0)
    a256 = (2 * np.pi / 256) * ((np.arange(256)[:, None] * np.arange(256)[None, :]) % 256)
    c["fn_c256"] = np.stack([np.cos(a256) * scc, -np.sin(a256) * scc]).astype(np.float32)
    w = np.arange(64)
    cs0 = np.clip(w - 8, 0, 48)
    wp = np.arange(64)[:, None]
    inside = (wp >= cs0[None, :]) & (wp < cs0[None, :] + 16)
    cm = np.where(inside, 0.0, NEG).astype(np.float32)
    jm = np.zeros((128, 128), np.float32)
    for p in range(64):
        jm[p, 64 + p] = 1.0
        jm[64 + p, p] = -1.0
    c["s5_jm"] = jm
    sidx = np.repeat(np.arange(8), 16)
    c["s5_mask"] = np.stack([(sidx[None, :] >= sidx[:, None]), (sidx[None, :] <= sidx[:, None])]).astype(np.float32)
    selin = np.zeros((128, 8, 8, 128), np.float32)
    for g in range(8):
        for s_ in range(8):
            for m_ in range(16):
                selin[g * 16 + m_, g, s_, s_ * 16 + m_] = 1.0
    c["s5_selin"] = selin
    c["s5_selout"] = np.ascontiguousarray(np.transpose(selin, (3, 2, 1, 0)))
    c["na_cm"] = np.ascontiguousarray(np.broadcast_to(np.concatenate([cm, cm], 0)[:, None, :], (128, 15, 64))).astype(np.float32)
    return c


def na_gather_rpb(rpb):
    wp = np.arange(64)[:, None]
    w = np.arange(64)[None, :]
    idx = np.clip(wp - w + 15, 0, 30)
    g = rpb[:, :, :, idx]
    g = np.transpose(g, (0, 1, 3, 2, 4))
    return np.ascontiguousarray(np.concatenate([g, g], axis=2)).astype(np.float32)


def na_blocks():
    out = []
    for j in range(64):
        rs = [min(max(2 * j + b - 4, 0), 120) for b in range(2)]
        lo, hi = min(rs), max(rs) + 7
        blocks = list(range(lo // 2, hi // 2 + 1))
        pat = []
        for i in blocks:
            for a in range(2):
                for b in range(2):
                    rho = 2 * i + a
                    r = 2 * j + b
                    ok = rs[b] <= rho <= rs[b] + 7
                    pat.append((rho - r + 7) if ok else None)
        out.append((blocks, tuple(pat)))
    return out


SCRATCH = {
    "XT": ([D, T], F32), "XM": ([D, T], F32), "H2": ([D, T], BF16),
    "ZF": ([T, 256], BF16), "VV": ([T, 512], BF16), "UT": ([256, T], BF16),
    "QT": ([512, T], BF16), "KT": ([512, T], BF16), "GT": ([3072, T], BF16),
    "FA": ([512, T], BF16), "SB": ([256, T], BF16), "NCT": ([512, T], BF16),
    "AF": ([64, 128 * 512], BF16), "DBG": ([128, 2048], F32), "YT": ([256, T], F32),
}

W_SHAPES = {
    "w_mod": [DEPTH, D, 6 * D], "b_mod": [DEPTH, 6 * D], "g_norm1": [DEPTH, D], "g_norm2": [DEPTH, D],
    "w_in": [DEPTH, D, DIN], "w_br_a": [DEPTH, 256, D], "w_br_b": [DEPTH, 256, D], "w_br_c": [DEPTH, 512, D],
    "w_out": [DEPTH, D, D], "w_ff1": [DEPTH, D, DFF], "w_ff2": [DEPTH, DFF, D], "g_final": [D],
    "rpbg": [DEPTH, 8, 128, 15, 64],
    "s5_lam_re": [DEPTH, 2, 16, 64], "s5_lam_im": [DEPTH, 2, 16, 64], "s5_log_dt": [DEPTH, 2, 16],
    "s5_b_re": [DEPTH, 2, 16, 64, 16], "s5_b_im": [DEPTH, 2, 16, 64, 16],
    "s5_c_re": [DEPTH, 2, 16, 16, 64], "s5_c_im": [DEPTH, 2, 16, 16, 64],
    "s5_d": [DEPTH, 16, 16], "s5_w_glu": [DEPTH, 256, 256],
}


def tiles_of(n, with_ctx=True):
    out = []
    if with_ctx:
        t = 0
        while t < LC:
            m = min(n, LC - t)
            out.append((t, m, 1))
            t += m
    t = LC
    while t < T:
        m = min(n, T - t)
        out.append((t, m, 0))
        t += m
    return out


def build(phases, kinds=None, layers=(0, 1)):
    kinds = kinds or {}
    nc = bass.Bass("TRN2", target_bir_lowering=False)
    I = {}
    I["x"] = nc.dram_tensor("x", [L, D], F32, kind="ExternalInput").ap()
    I["ctx"] = nc.dram_tensor("ctx", [LC, D], F32, kind="ExternalInput").ap()
    I["cvec"] = nc.dram_tensor("cvec", [2, D], F32, kind="ExternalInput").ap()
    for k, shp in W_SHAPES.items():
        I[k] = nc.dram_tensor(k, shp, F32, kind="ExternalInput").ap()
    for k, v in _consts().items():
        I[k] = nc.dram_tensor(k, list(v.shape), F32, kind="ExternalInput").ap()
    out = nc.dram_tensor("out", [L, D], F32, kind="ExternalOutput").ap()
    R = {}
    for k, (shp, dt) in SCRATCH.items():
        R[k] = nc.dram_tensor("r_" + k.lower(), shp, dt, kind=kinds.get(k, "Internal")).ap()

    with ExitStack() as st:
        S = Sched(nc, st)
        pbig = [st.enter_context(nc.psum_tensor("pbig%d" % i, [128, 1024], F32)) for i in range(4)]
        pbanks = [pbig[i // 2][:, (i % 2) * 512:(i % 2 + 1) * 512] for i in range(8)]
        pctr = [0]

        def bank():
            i = pctr[0] % 8
            pctr[0] += 1
            return pbanks[i], "pb%d" % i

        evc = [0]

        def evac_eng():
            evc[0] += 1
            return "dve" if evc[0] % 2 else "act"

        def copy_op(eng, out_ap, in_ap):
            if eng == "act":
                return lambda e: e.activation(out=out_ap, in_=in_ap, func=AF.Copy)
            return lambda e: e.tensor_copy(out=out_ap, in_=in_ap)

        uidc = [0]

        def sb(stack, name, shape, dt):
            uidc[0] += 1
            return stack.enter_context(nc.sbuf_tensor("%s_u%d" % (name, uidc[0]), shape, dt))

        ident = sb(st, "ident", [128, 128], F32)
        ones_bf = sb(st, "ones_bf", [128, 128], BF16)
        mod = sb(st, "mod", [128, DEPTH, 48, 2], F32)
        gsc = sb(st, "gsc", [128, DEPTH, 2, 8, 2], F32)
        gfin = sb(st, "gfin", [128, 8], F32)

        NCDMA = dict(allow_slow_non_contiguous=True)
        S.dma("sp", lambda e: e.dma_start(out=ident[:], in_=I["ident"]), writes=["ident"])
        S.op("pool", lambda e: e.memset(ones_bf[:], 1.0), writes=["ones"])

        def phase0():
            with ExitStack() as ps:
                cT = sb(ps, "cT", [128, 8, 2], F32)
                sT = sb(ps, "sT", [128, 8, 2], F32)
                bm = sb(ps, "bm", [128, DEPTH, 48], F32)
                gn = sb(ps, "gn", [128, DEPTH, 2, 8], F32)
                wm = [sb(ps, "wm%d" % i, [128, 8, 512], F32) for i in range(2)]
                for j in range(2):
                    S.dma("sp", lambda e, j=j: e.dma_start(out=cT[:, :, j], in_=I["cvec"][j].rearrange("(k p) -> p k", p=128), **NCDMA), writes=["cT"])
                for l in range(DEPTH):
                    S.dma("sp", lambda e, l=l: e.dma_start(out=bm[:, l, :], in_=I["b_mod"][l].rearrange("(j p) -> p j", p=128), **NCDMA), writes=["bm"])
                    S.dma("sp", lambda e, l=l: e.dma_start(out=gn[:, l, 0, :], in_=I["g_norm1"][l].rearrange("(k p) -> p k", p=128), **NCDMA), writes=["gn0"])
                    S.dma("sp", lambda e, l=l: e.dma_start(out=gn[:, l, 1, :], in_=I["g_norm2"][l].rearrange("(k p) -> p k", p=128), **NCDMA), writes=["gn1"])
                S.dma("sp", lambda e: e.dma_start(out=gfin[:], in_=I["g_final"].rearrange("(k p) -> p k", p=128), **NCDMA), writes=["gfin"])
                S.op("act", lambda e: e.activation(out=sT[:], in_=cT[:], func=AF.Silu), reads=["cT"], writes=["sT"])
                for l in range(DEPTH):
                    pb, pbn = bank()
                    wv = I["w_mod"][l].rearrange("(k p) n -> p k n", p=128)
                    for blk in range(12):
                        w = wm[blk % 2]
                        wn = "wm%d" % (blk % 2)
                        S.dma("sp", lambda e, w=w, blk=blk, wv=wv: e.dma_start(out=w[:], in_=wv[:, :, blk * 512:(blk + 1) * 512]), writes=[wn])
                        for jj in range(4):
                            j = blk * 4 + jj
                            for k in range(8):
                                S.op("pe", lambda e, w=w, jj=jj, j=j, k=k, pb=pb: e.matmul(
                                    pb[:, j * 2:j * 2 + 2], lhsT=w[:, k, jj * 128:(jj + 1) * 128], rhs=sT[:, k, :],
                                    start=(k == 0), stop=(k == 7)), reads=[wn, "sT"], writes=[pbn])
                    for v in range(2):
                        S.op("dve", lambda e, l=l, v=v, pb=pb: e.tensor_tensor(
                            out=mod[:, l, :, v], in0=pb[:, 0:96].rearrange("p (j v) -> p j v", v=2)[:, :, v], in1=bm[:, l, :], op=ALU.add),
                            reads=[pbn, "bm"], writes=["mod"])
                    for nn in range(2):
                        for v in range(2):
                            j0 = 8 + 24 * nn
                            S.op("dve", lambda e, l=l, v=v, nn=nn, j0=j0: e.scalar_tensor_tensor(
                                out=gsc[:, l, nn, :, v], in0=mod[:, l, j0:j0 + 8, v], scalar=1.0, in1=gn[:, l, nn, :],
                                op0=ALU.add, op1=ALU.mult), reads=["mod", "gn%d" % nn], writes=["gsc"])
                S.flush()

        def norm_mod(l, nn, v, xT, n, sq, rstd, tmpa, tmps, hT, xname, hname, uid):
            sh_j0 = 24 * nn
            S.op("act", lambda e: e.activation(out=sq[:, :, :n], in_=xT[:, :, :n], func=AF.Square), reads=[xname], writes=["sq"])
            pb, pbn = bank()
            for c in range(8):
                S.op("pe", lambda e, c=c, pb=pb: e.matmul(pb[:, :n], lhsT=ones_bf[:], rhs=sq[:, c, :n], start=(c == 0), stop=(c == 7)),
                     reads=["sq", "ones"], writes=[pbn])
            S.op("act", lambda e, pb=pb: e.activation(out=tmpa[:, :n], in_=pb[:, :n], func=AF.Sqrt, scale=1.0 / D, bias=epsb[:, 0:1]),
                 reads=[pbn, "epsb"], writes=["tmpa"])
            S.op("dve", lambda e: e.reciprocal(out=rstd[:, :n], in_=tmpa[:, :n]), reads=["tmpa"], writes=["rstd"])
            for c in range(8):
                tt = tmps[c % 2]
                tn = "tmps%d" % (c % 2)
                S.op("dve", lambda e, c=c, tt=tt: e.tensor_tensor(out=tt[:, :n], in0=xT[:, c, :n], in1=rstd[:, :n], op=ALU.mult),
                     reads=[xname, "rstd"], writes=[tn])
                S.op("act", lambda e, c=c, tt=tt: e.activation(out=hT[:, c, :n], in_=tt[:, :n], func=AF.Identity,
                                                              scale=gsc[:, l, nn, c, v:v + 1], bias=mod[:, l, sh_j0 + c, v:v + 1]),
                     reads=[tn, "gsc", "mod"], writes=[hname])

        epsb = sb(st, "epsb", [128, 1], F32)
        S.op("pool", lambda e: e.memset(epsb[:], EPS), writes=["epsb"])

        def phase1(l):
            NT = 512
            tl = tiles_of(NT)
            with ExitStack() as ps:
                w_sb = sb(ps, "w_in_sb", [128, 8, DIN], BF16)
                wv = I["w_in"][l].rearrange("(k p) n -> p k n", p=128)
                for j in range(10):
                    S.dma("pool", lambda e, j=j: e.dma_start(out=w_sb[:, :, j * 512:(j + 1) * 512], in_=wv[:, :, j * 512:(j + 1) * 512]),
                          writes=["w%d" % j])
                xtok = sb(ps, "xtok", [128, 4, D], F32)
                xTs = [sb(ps, "xT%d" % i, [128, 8, NT], F32) for i in range(2)]
                hTs = [sb(ps, "hT%d" % i, [128, 8, NT], BF16) for i in range(2)]
                sq = sb(ps, "sq", [128, 8, NT], BF16)
                rstd = sb(ps, "rstd", [128, NT], F32)
                tmpa = sb(ps, "tmpa", [128, NT], F32)
                tmps = [sb(ps, "tmps%d" % i, [128, NT], F32) for i in range(2)]
                stg = [sb(ps, "stg%d" % i, [128, 4, NT], BF16) for i in range(3)]
                sctr = [0]

                def stage():
                    i = sctr[0] % 3
                    sctr[0] += 1
                    return stg[i], "stg%d" % i

                XTv = R["XT"].rearrange("(c p) t -> p c t", p=128)

                def load(i):
                    t0, n, v = tl[i]
                    xT = xTs[i % 2]
                    xn = "xT%d" % (i % 2)
                    if l == 0:
                        src = I["ctx"] if v else I["x"]
                        r0 = t0 if v else t0 - LC
                        ns = n // 128
                        S.dma("sp", lambda e: e.dma_start(out=xtok[:, :ns, :], in_=src[r0:r0 + n, :].rearrange("(s p) d -> p s d", p=128)),
                              writes=["xtok"])
                        for s in range(ns):
                            for half in range(2):
                                pb, pbn = bank()
                                for cc in range(4):
                                    c = half * 4 + cc
                                    S.op("pe", lambda e, s=s, c=c, cc=cc, pb=pb: e.transpose(
                                        out=pb[:, cc * 128:(cc + 1) * 128], in_=xtok[:, s, c * 128:(c + 1) * 128], identity=ident[:]),
                                        reads=["xtok", "ident"], writes=[pbn])
                                eg = evac_eng()
                                S.op(eg, copy_op(eg, xT[:, half * 4:half * 4 + 4, s * 128:(s + 1) * 128],
                                                 pb[:, :].rearrange("p (c t) -> p c t", c=4)), reads=[pbn], writes=[xn])
                        S.dma("sp", lambda e: e.dma_start(out=XTv[:, :, t0:t0 + n], in_=xT[:, :, :n]), reads=[xn], writes=["XT%d" % i])
                    else:
                        S.dma("sp", lambda e: e.dma_start(out=xT[:, :, :n], in_=XTv[:, :, t0:t0 + n]), writes=[xn])

                def compute(i):
                    t0, n, v = tl[i]
                    xT = xTs[i % 2]
                    xn = "xT%d" % (i % 2)
                    hT = hTs[i % 2]
                    hn = "hT%d" % (i % 2)
                    norm_mod(l, 0, v, xT, n, sq, rstd, tmpa, tmps, hT, xn, hn, i)
                    ns = n // 128
                    for s in range(ns):
                        sg, sgn = stage()
                        pb, pbn = bank()
                        for c in range(8):
                            S.op("pe", lambda e, s=s, c=c, pb=pb: e.matmul(pb[:, 0:256], lhsT=hT[:, c, s * 128:(s + 1) * 128], rhs=w_sb[:, c, 0:256],
                                                                         start=(c == 0), stop=(c == 7)), reads=[hn, "w0"], writes=[pbn])
                        eg = evac_eng()
                        S.op(eg, copy_op(eg, sg[:, 0, 0:256], pb[:, 0:256]), reads=[pbn], writes=[sgn])
                        S.dma("sp", lambda e, s=s, sg=sg: e.dma_start(out=R["ZF"][t0 + s * 128:t0 + (s + 1) * 128, :], in_=sg[:, 0, 0:256]),
                              reads=[sgn], writes=["ZF"])
                        pb, pbn = bank()
                        for c in range(8):
                            S.op("pe", lambda e, s=s, c=c, pb=pb: e.matmul(pb[:, :], lhsT=hT[:, c, s * 128:(s + 1) * 128], rhs=w_sb[:, c, 1536:2048],
                                                                         start=(c == 0), stop=(c == 7)), reads=[hn, "w3"], writes=[pbn])
                        eg = evac_eng()
                        S.op(eg, copy_op(eg, sg[:, 1, :], pb[:, :]), reads=[pbn], writes=[sgn])
                        S.dma("sp", lambda e, s=s, sg=sg: e.dma_start(out=R["VV"][t0 + s * 128:t0 + (s + 1) * 128, :], in_=sg[:, 1, :]),
                              reads=[sgn], writes=["VV"])
                    groups = [("UT", 256, 2, False), ("QT", 512, 4, False), ("KT", 1024, 4, False)]
                    groups += [("GT%d" % g, 2048 + g * 512, 4, True) for g in range(6)]
                    for name, col0, nch, sig in groups:
                        sg, sgn = stage()
                        for q in range(nch):
                            pb, pbn = bank()
                            cs = col0 + q * 128
                            for c in range(8):
                                S.op("pe", lambda e, c=c, cs=cs, pb=pb: e.matmul(pb[:, :n], lhsT=w_sb[:, c, cs:cs + 128], rhs=hT[:, c, :n],
                                                                               start=(c == 0), stop=(c == 7)), reads=[hn, "w%d" % (cs // 512)], writes=[pbn])
                            if sig:
                                S.op("act", lambda e, q=q, pb=pb, sg=sg: e.activation(out=sg[:, q, :n], in_=pb[:, :n], func=AF.Sigmoid),
                                     reads=[pbn], writes=[sgn])
                            else:
                                eg = evac_eng()
                                S.op(eg, copy_op(eg, sg[:, q, :n], pb[:, :n]), reads=[pbn], writes=[sgn])
                        if sig:
                            g = int(name[2:])
                            dst = R["GT"].rearrange("(c p) t -> p c t", p=128)[:, g * 4:(g + 1) * 4, t0:t0 + n]
                        else:
                            dst = R[name].rearrange("(c p) t -> p c t", p=128)[:, :, t0:t0 + n]
                        S.dma("sp", lambda e, sg=sg, dst=dst, nch=nch: e.dma_start(out=dst, in_=sg[:, :nch, :n]), reads=[sgn], writes=[name])

                load(0)
                for i in range(len(tl)):
                    if i + 1 < len(tl):
                        load(i + 1)
                    compute(i)
                S.flush()


        def phase2_fnet(l):
            with ExitStack() as ps:
                tc = sb(ps, "fn_tc", [128, 64, 128], BF16)
                tsn = sb(ps, "fn_tsn", [128, 64, 128], BF16)
                w64 = sb(ps, "fn_w64", [64, 3, 64], BF16)
                zf = sb(ps, "zf", [128, 64, 256], BF16)
                xo = sb(ps, "xo", [128, 4, L], BF16)
                ablk = [sb(ps, "ablk%d" % i, [64, 16, 512], BF16) for i in range(2)]
                stg = [sb(ps, "fstg%d" % i, [128, 512], BF16) for i in range(3)]
                for q4 in range(4):
                    S.dma("pool", lambda e, q4=q4: e.dma_start(out=tc[:, q4 * 16:(q4 + 1) * 16, :], in_=I["fn_tc"][:, q4 * 16:(q4 + 1) * 16, :]), writes=["tc%d" % q4])
                    S.dma("pool", lambda e, q4=q4: e.dma_start(out=tsn[:, q4 * 16:(q4 + 1) * 16, :], in_=I["fn_tsn"][:, q4 * 16:(q4 + 1) * 16, :]), writes=["tsn%d" % q4])
                S.dma("pool", lambda e: e.dma_start(out=w64[:], in_=I["fn_w64"].rearrange("a p m -> p a m")), writes=["w64"])
                zsrc = R["ZF"][LC:T, :].rearrange("(r c) ch -> r c ch", c=64)
                for q4 in range(4):
                    S.dma("sp", lambda e, q4=q4: e.dma_start(out=zf[:, q4 * 16:(q4 + 1) * 16, :], in_=zsrc[:, q4 * 16:(q4 + 1) * 16, :]), writes=["zf%d" % q4])
                AFv = R["AF"].rearrange("c (k m) -> c k m", m=512)
                if l < DEPTH - 1:
                    c256 = sb(ps, "c256", [128, 2, 2, 256], BF16)
                    zc = sb(ps, "zc", [128, 2, 256], BF16)
                    xc = sb(ps, "xc", [128, 4, 256], BF16)
                    for a in range(2):
                        S.dma("pool", lambda e, a=a: e.dma_start(out=c256[:, a, :, :], in_=I["fn_c256"][a].rearrange("(tc p) k -> p tc k", p=128)), writes=["c256"])
                    S.dma("sp", lambda e: e.dma_start(out=zc[:], in_=R["ZF"][0:LC, :].rearrange("(a p) ch -> p a ch", p=128)), writes=["zc"])
                    for ri in range(2):
                        for q in range(2):
                            pb, pbn = bank()
                            for t2 in range(2):
                                S.op("pe", lambda e, ri=ri, q=q, t2=t2, pb=pb: e.matmul(pb[:, 0:256], lhsT=zc[:, t2, q * 128:(q + 1) * 128], rhs=c256[:, ri, t2, :],
                                                                                     start=(t2 == 0), stop=(t2 == 1)), reads=["zc", "c256"], writes=[pbn])
                            eg = evac_eng()
                            S.op(eg, copy_op(eg, xc[:, ri * 2 + q, :], pb[:, 0:256]), reads=[pbn], writes=["xc"])
                    S.dma("sp", lambda e: e.dma_start(out=R["FA"].rearrange("(a p) t -> p a t", p=128)[:, :, 0:LC], in_=xc[:]), reads=["xc"], writes=["FAc"])
                for c in range(64):
                    pb, pbn = bank()
                    S.op("pe", lambda e, c=c, pb=pb: e.matmul(pb[:, 0:256], lhsT=tc[:, c, :], rhs=zf[:, c, :], start=True, stop=True),
                         reads=["tc%d" % (c // 16), "zf%d" % (c // 16)], writes=[pbn])
                    S.op("pe", lambda e, c=c, pb=pb: e.matmul(pb[:, 256:512], lhsT=tsn[:, c, :], rhs=zf[:, c, :], start=True, stop=True),
                         reads=["tsn%d" % (c // 16), "zf%d" % (c // 16)], writes=[pbn])
                    sg = stg[c % 3]
                    sgn = "fstg%d" % (c % 3)
                    eg = evac_eng()
                    S.op(eg, copy_op(eg, sg[:, :], pb[:, :]), reads=[pbn], writes=[sgn])
                    S.dma("sp", lambda e, c=c, sg=sg: e.dma_start(out=AFv[c], in_=sg[:, :]), reads=[sgn], writes=["AF"])
                xov = xo[:, :, :].rearrange("p a (k2 k1) -> p a k2 k1", k1=128)
                for kb in range(8):
                    ab = ablk[kb % 2]
                    abn = "ablk%d" % (kb % 2)
                    S.dma("sp", lambda e, kb=kb, ab=ab: e.dma_start(out=ab[:], in_=AFv[:, kb * 16:(kb + 1) * 16, :]), reads=["AF"], writes=[abn])
                    for k1l in range(16):
                        k1 = kb * 16 + k1l
                        pb, pbn = bank()
                        for q in range(2):
                            ar = ab[:, k1l, q * 128:(q + 1) * 128]
                            ai = ab[:, k1l, 256 + q * 128:256 + (q + 1) * 128]
                            o_r = pb[:, q * 64:(q + 1) * 64]
                            o_i = pb[:, (2 + q) * 64:(3 + q) * 64]
                            S.op("pe", lambda e, ar=ar, o_r=o_r: e.matmul(o_r, lhsT=ar, rhs=w64[:, 0, :], start=True, stop=False), reads=[abn, "w64"], writes=[pbn])
                            S.op("pe", lambda e, ai=ai, o_r=o_r: e.matmul(o_r, lhsT=ai, rhs=w64[:, 1, :], start=False, stop=True), reads=[abn, "w64"], writes=[pbn])
                            S.op("pe", lambda e, ai=ai, o_i=o_i: e.matmul(o_i, lhsT=ai, rhs=w64[:, 0, :], start=True, stop=False), reads=[abn, "w64"], writes=[pbn])
                            S.op("pe", lambda e, ar=ar, o_i=o_i: e.matmul(o_i, lhsT=ar, rhs=w64[:, 2, :], start=False, stop=True), reads=[abn, "w64"], writes=[pbn])
                        eg = evac_eng()
                        S.op(eg, copy_op(eg, xov[:, :, :, k1], pb[:, 0:256].rearrange("p (a k) -> p a k", a=4)), reads=[pbn], writes=["xo"])
                FAl = R["FA"].rearrange("(a p) t -> p a t", p=128)
                for a in range(4):
                    S.dma("sp", lambda e, a=a: e.dma_start(out=FAl[:, a, LC:T], in_=xo[:, a, :]), reads=["xo"], writes=["FA%d" % a])
                S.flush()

        def phase2_na(l):
            blocks = na_blocks()
            pats = {}
            for blks, pat in blocks:
                if pat not in pats:
                    pats[pat] = (len(pats), len(blks))
            npat = len(pats)
            with_ctx_q = (l < DEPTH - 1)
            scale = 0.125
            with ExitStack() as ps:
                cm = sb(ps, "na_cm", [128, 15, 64], F32)
                S.dma("sp", lambda e: e.dma_start(out=cm[:], in_=I["na_cm"]), writes=["cm"])
                gt = sb(ps, "na_g", [128, 15, 64], F32)
                cb = sb(ps, "na_cb", [128, 16, 64], F32)
                bias = sb(ps, "na_bias", [128, npat, 640], F32)
                kTs = [sb(ps, "kT%d" % i, [64, T], BF16) for i in range(2)]
                qTs = [sb(ps, "qT%d" % i, [64, T], BF16) for i in range(2)]
                v1s = [sb(ps, "v1_%d" % i, [128, 66, 65], BF16) for i in range(2)]
                ncs = [sb(ps, "ncT%d" % i, [64, T], BF16) for i in range(2)]
                scb = [sb(ps, "scb%d" % i, [128, 640], F32) for i in range(2)]
                PTs = [sb(ps, "PT%d" % i, [128, 7, 128], BF16) for i in range(2)]
                otk = [sb(ps, "otk%d" % i, [128, 64], F32) for i in range(2)]
                rec = [sb(ps, "rec%d" % i, [128, 1], F32) for i in range(2)]
                for i in range(2):
                    S.op("pool", lambda e, i=i: e.memset(v1s[i][:, :, 64:65], 1.0), writes=["v1o_%d" % i])
                S.op("pool", lambda e: e.memset(cb[:, 15, :], NEG), writes=["cbneg"])
                VVv = R["VV"].rearrange("(blk p) d -> p blk d", p=128)

                def load_head(h):
                    hb = h % 2
                    S.dma("sp", lambda e: e.dma_start(out=kTs[hb][:], in_=R["KT"][h * 64:(h + 1) * 64, :]), writes=["kT%d" % hb])
                    S.dma("sp", lambda e: e.dma_start(out=qTs[hb][:], in_=R["QT"][h * 64:(h + 1) * 64, :]), writes=["qT%d" % hb])
                    for part in range(3):
                        S.dma("sp", lambda e, part=part: e.dma_start(out=v1s[hb][:, part * 22:(part + 1) * 22, 0:64],
                                                                    in_=VVv[:, part * 22:(part + 1) * 22, h * 64:(h + 1) * 64]),
                              reads=["v1o_%d" % hb], writes=["v1_%d_%d" % (hb, part)])

                def head(h):
                    hb = h % 2
                    kT, qT, v1, ncT = kTs[hb], qTs[hb], v1s[hb], ncs[hb]
                    kn, qn, ncn = "kT%d" % hb, "qT%d" % hb, "ncT%d" % hb
                    vn = ["v1_%d_%d" % (hb, p) for p in range(3)]
                    S.dma("sp", lambda e: e.dma_start(out=gt[:], in_=I["rpbg"][l, h]), writes=["na_g"])
                    S.op("dve", lambda e: e.tensor_tensor(out=cb[:, 0:15, :], in0=gt[:], in1=cm[:], op=ALU.add), reads=["na_g", "cm"], writes=["cb"])
                    for pat, (pi, nb) in pats.items():
                        k = 0
                        for s_ in range(nb):
                            for a in range(2):
                                for b in range(2):
                                    d = pat[k]
                                    k += 1
                                    row = 15 if d is None else d
                                    eng = "dve" if (k % 2) else "pool"
                                    S.op(eng, lambda e, a=a, b=b, s_=s_, row=row, pi=pi: e.tensor_copy(
                                        out=bias[a * 64:(a + 1) * 64, pi, s_ * 128 + b * 64:s_ * 128 + (b + 1) * 64], in_=cb[a * 64:(a + 1) * 64, row, :]),
                                        reads=["cb", "cbneg"], writes=["bias"])
                    qblocks = ([0, 1] if with_ctx_q else []) + list(range(2, 66))
                    for qi, tq in enumerate(qblocks):
                        jb = qi % 2
                        if tq >= 2:
                            blks, pat = blocks[tq - 2]
                            pi, nb = pats[pat]
                        else:
                            blks, nb = [], 0
                        qs = qT[:, tq * 128:(tq + 1) * 128]
                        big = pbig[jb]
                        bign = ["pb%d" % (2 * jb), "pb%d" % (2 * jb + 1)]
                        for s_, i in enumerate(blks):
                            tk = 2 + i
                            S.op("pe", lambda e, s_=s_, tk=tk, big=big, qs=qs: e.matmul(big[:, s_ * 128:(s_ + 1) * 128], lhsT=kT[:, tk * 128:(tk + 1) * 128], rhs=qs,
                                                                                      start=True, stop=True), reads=[kn, qn], writes=bign)
                        psc = pbanks[4 + jb]
                        pscn = "pb%d" % (4 + jb)
                        for s_ in range(2):
                            S.op("pe", lambda e, s_=s_, psc=psc, qs=qs: e.matmul(psc[:, s_ * 128:(s_ + 1) * 128], lhsT=kT[:, s_ * 128:(s_ + 1) * 128], rhs=qs,
                                                                               start=True, stop=True), reads=[kn, qn], writes=[pscn])
                        PT = PTs[jb]
                        ptn = "PT%d" % jb
                        scbj, recj, otkj = scb[jb], rec[jb], otk[jb]
                        if nb:
                            S.op("dve", lambda e, big=big, nb=nb, pi=pi, scbj=scbj: e.scalar_tensor_tensor(out=scbj[:, :nb * 128], in0=big[:, :nb * 128], scalar=scale,
                                                                                                          in1=bias[:, pi, :nb * 128], op0=ALU.mult, op1=ALU.add),
                                 reads=bign + ["bias"], writes=["scb%d" % jb])
                            S.op("act", lambda e, nb=nb, PT=PT, scbj=scbj: e.activation(out=PT[:, 0:nb, :], in_=scbj[:, :nb * 128].rearrange("p (s q) -> p s q", q=128), func=AF.Exp),
                                 reads=["scb%d" % jb], writes=[ptn])
                        S.op("act", lambda e, psc=psc, PT=PT: e.activation(out=PT[:, 5:7, :], in_=psc[:, 0:256].rearrange("p (s q) -> p s q", q=128), func=AF.Exp, scale=scale),
                             reads=[pscn], writes=[ptn])
                        pso = pbanks[6 + jb]
                        pson = "pb%d" % (6 + jb)
                        klist = [(s_, 2 + i) for s_, i in enumerate(blks)] + [(5, 0), (6, 1)]
                        for n_, (slot, tk) in enumerate(klist):
                            first, lastk = (n_ == 0), (n_ == len(klist) - 1)
                            S.op("pe", lambda e, slot=slot, tk=tk, first=first, lastk=lastk, pso=pso, PT=PT: e.matmul(pso[:, 0:65], lhsT=PT[:, slot, :], rhs=v1[:, tk, :],
                                                                                                                    start=first, stop=lastk),
                                 reads=[ptn, vn[tk // 22], "v1o_%d" % hb], writes=[pson])
                        if DEBUG and h == 0 and qi == DEBUG - 1:
                            dbt = sb(ps, "dbt", [128, 2048], F32)
                            S.op("dve", lambda e, psc=psc: e.tensor_copy(out=dbt[:, 0:256], in_=psc[:, 0:256]), reads=[pscn], writes=["dbt"])
                            S.op("dve", lambda e, PT=PT: e.tensor_copy(out=dbt[:, 256:256 + 896], in_=PT[:, :, :].rearrange("p s q -> p (s q)")), reads=[ptn], writes=["dbt"])
                            S.op("dve", lambda e, pso=pso: e.tensor_copy(out=dbt[:, 1152:1152 + 65], in_=pso[:, 0:65]), reads=[pson], writes=["dbt"])
                            S.op("dve", lambda e, big=big: e.tensor_copy(out=dbt[:, 1280:1280 + 640], in_=big[:, 0:640]), reads=bign, writes=["dbt"])
                            S.dma("sp", lambda e: e.dma_start(out=R["DBG"], in_=dbt[:]), reads=["dbt"], writes=["DBG"])
                        S.op("dve", lambda e, pso=pso, recj=recj: e.reciprocal(out=recj[:], in_=pso[:, 64:65]), reads=[pson], writes=["rec%d" % jb])
                        S.op("dve", lambda e, pso=pso, recj=recj, otkj=otkj: e.tensor_scalar(out=otkj[:], in0=pso[:, 0:64], scalar1=recj[:, 0:1], scalar2=None, op0=ALU.mult),
                             reads=[pson, "rec%d" % jb], writes=["otk%d" % jb])
                        S.op("pe", lambda e, pso=pso, otkj=otkj: e.transpose(out=pso[0:64, 128:256], in_=otkj[:], identity=ident[:]), reads=["otk%d" % jb], writes=[pson])
                        eg = evac_eng()
                        S.op(eg, copy_op(eg, ncT[:, tq * 128:(tq + 1) * 128], pso[0:64, 128:256]), reads=[pson], writes=[ncn])
                    t_lo = 0 if with_ctx_q else LC
                    S.dma("sp", lambda e: e.dma_start(out=R["NCT"][h * 64:(h + 1) * 64, t_lo:T], in_=ncT[:, t_lo:T]), reads=[ncn], writes=["NCT%d" % h])

                load_head(0)
                for h in range(8):
                    if h + 1 < 8:
                        load_head(h + 1)
                    head(h)
                S.flush()


        def phase2_s5(l):
            Q = T // 8
            PCS = [(0, 352), (352, 704), (704, 1056)]
            TWO_PI = float(2 * np.pi)
            MAGIC = 12582912.0
            with ExitStack() as po:
                bblk = sb(po, "bblk", [128, 32, 128], BF16)
                cblk = sb(po, "cblk", [128, 32, 128], BF16)
                dsum = sb(po, "dsum", [128, 16, 128], BF16)
                MU = sb(po, "MU", [128, 11, 2, 32], F32)
                jm = sb(po, "jm", [128, 128], F32)
                S.dma("sp", lambda e: e.dma_start(out=jm[:], in_=I["s5_jm"]), writes=["jm"])
                with ExitStack() as ps:
                    def t32(name):
                        return sb(ps, name, [128, 32], F32)
                    lr, li, ldt, dtv, x1, mag, ang = [t32(n) for n in ("lr", "li", "ldt", "dtv", "x1", "mag", "ang")]
                    s0, tq, nn_, red, sinv, cosv = [t32(n) for n in ("s0", "tq", "nn", "red", "sinv", "cosv")]
                    am1, den, rden, wr, wi, u1, u2, u3, u4 = [t32(n) for n in ("am1", "den", "rden", "wr", "wi", "u1", "u2", "u3", "u4")]
                    PW = sb(ps, "PW", [128, 9, 2, 32], F32)
                    IPW = sb(ps, "IPW", [128, 8, 2, 32], F32)
                    Y1 = sb(ps, "Y1", [128, 32, 16], F32)
                    Y2 = sb(ps, "Y2", [128, 32, 16], F32)
                    BB1 = sb(ps, "BB1", [128, 32, 16], F32)
                    BB2 = sb(ps, "BB2", [128, 32, 16], F32)
                    CC1 = sb(ps, "CC1", [128, 4, 128], F32)
                    CC2 = sb(ps, "CC2", [128, 4, 128], F32)
                    X1 = sb(ps, "X1", [128, 32, 16], F32)
                    X2 = sb(ps, "X2", [128, 32, 16], F32)
                    e1 = sb(ps, "e1", [128, 32, 16], F32)
                    e2 = sb(ps, "e2", [128, 32, 16], F32)
                    EBt = sb(ps, "EBt", [128, 32, 8, 16], F32)
                    ENt = sb(ps, "ENt", [128, 32, 8, 16], F32)
                    GPt = sb(ps, "GPt", [128, 32, 8, 16], F32)
                    CBt = sb(ps, "CBt", [128, 32, 8, 16], F32)
                    Dm = sb(ps, "Dm", [128, 32, 128], F32)
                    msk = sb(ps, "msk", [128, 2, 128], F32)
                    dsk = sb(ps, "dsk", [128, 16], F32)
                    for hf in range(2):
                        hs = slice(hf * 64, (hf + 1) * 64)
                        S.dma("sp", lambda e, hs=hs: e.dma_start(out=lr[hs, :], in_=I["s5_lam_re"][l].rearrange("d g p -> p (d g)"), **NCDMA), writes=["lr"])
                        S.dma("sp", lambda e, hs=hs: e.dma_start(out=li[hs, :], in_=I["s5_lam_im"][l].rearrange("d g p -> p (d g)"), **NCDMA), writes=["li"])
                    S.dma("sp", lambda e: e.dma_start(out=ldt[:], in_=I["s5_log_dt"][l].rearrange("d g -> (d g)").partition_broadcast(128)), writes=["ldt"])
                    bre = I["s5_b_re"][l].rearrange("d g p m -> p (d g) m")
                    bim = I["s5_b_im"][l].rearrange("d g p m -> p (d g) m")
                    S.dma("sp", lambda e: e.dma_start(out=Y1[0:64], in_=bre), writes=["Y1"])
                    S.dma("sp", lambda e: e.dma_start(out=Y1[64:128], in_=bim), writes=["Y1"])
                    S.dma("sp", lambda e: e.dma_start(out=Y2[0:64], in_=bim), writes=["Y2"])
                    S.dma("sp", lambda e: e.dma_start(out=Y2[64:128], in_=bre), writes=["Y2"])
                    cre = I["s5_c_re"][l].rearrange("d g n p -> (d g n) p")
                    cim = I["s5_c_im"][l].rearrange("d g n p -> (d g n) p")
                    for ch in range(4):
                        rs_ = slice(ch * 128, (ch + 1) * 128)
                        S.dma("sp", lambda e, ch=ch, rs_=rs_: e.dma_start(out=CC1[:, ch, 0:64], in_=cre[rs_, :]), writes=["CC1"])
                        S.dma("sp", lambda e, ch=ch, rs_=rs_: e.dma_start(out=CC1[:, ch, 64:128], in_=cim[rs_, :]), writes=["CC1"])
                        S.dma("sp", lambda e, ch=ch, rs_=rs_: e.dma_start(out=CC2[:, ch, 0:64], in_=cim[rs_, :]), writes=["CC2"])
                        S.dma("sp", lambda e, ch=ch, rs_=rs_: e.dma_start(out=CC2[:, ch, 64:128], in_=cre[rs_, :]), writes=["CC2"])
                    S.dma("sp", lambda e: e.dma_start(out=msk[:], in_=I["s5_mask"].rearrange("d p m -> p d m")), writes=["msk"])
                    for s_ in range(8):
                        S.dma("sp", lambda e, s_=s_: e.dma_start(out=dsk[s_ * 16:(s_ + 1) * 16, :], in_=I["s5_d"][l].rearrange("g m -> m g"), **NCDMA), writes=["dsk"])

                    if CUT < 10:
                        S.flush()
                        return
                    def V(fn, reads, writes, eng="dve"):
                        S.op(eng, fn, reads=reads, writes=writes)

                    def tt(o, a, b, op, on, an, bn):
                        V(lambda e: e.tensor_tensor(out=o, in0=a, in1=b, op=op), [an, bn], [on])

                    def ts(o, a, s1, op0, on, an, s2=None, op1=None):
                        if op1 is None:
                            V(lambda e: e.tensor_scalar(out=o, in0=a, scalar1=s1, scalar2=None, op0=op0), [an], [on])
                        else:
                            V(lambda e: e.tensor_scalar(out=o, in0=a, scalar1=s1, scalar2=s2, op0=op0, op1=op1), [an], [on])

                    V(lambda e: e.tensor_scalar(out=Y2[0:64], in0=Y2[0:64], scalar1=-1.0, scalar2=None, op0=ALU.mult), ["Y2"], ["Y2"])
                    ts(lr[:], lr[:], -1e-4, ALU.min, "lr", "lr")
                    V(lambda e: e.activation(out=dtv[:], in_=ldt[:], func=AF.Exp), ["ldt"], ["dtv"], eng="act")
                    tt(x1[:], lr[:], dtv[:], ALU.mult, "x1", "lr", "dtv")
                    V(lambda e: e.activation(out=mag[:], in_=x1[:], func=AF.Exp), ["x1"], ["mag"], eng="act")
                    tt(ang[:], li[:], dtv[:], ALU.mult, "ang", "li", "dtv")

                    def sinred(dst, dn, shift):
                        ts(s0[:], ang[:], shift, ALU.add, "s0", "ang")
                        ts(tq[:], s0[:], 1.0 / TWO_PI, ALU.mult, "tq", "s0")
                        ts(nn_[:], tq[:], MAGIC, ALU.add, "nn", "tq")
                        ts(tq[:], nn_[:], -MAGIC, ALU.add, "tq", "nn")
                        V(lambda e: e.scalar_tensor_tensor(out=red[:], in0=tq[:], scalar=-TWO_PI, in1=s0[:], op0=ALU.mult, op1=ALU.add), ["tq", "s0"], ["red"])
                        ts(red[:], red[:], -3.1415925, ALU.max, "red", "red", 3.1415925, ALU.min)
                        V(lambda e: e.activation(out=dst, in_=red[:], func=AF.Sin), ["red"], [dn], eng="act")

                    sinred(sinv[:], "sinv", 0.0)
                    sinred(cosv[:], "cosv", float(np.pi / 2))
                    A1 = PW[:, 1, 0, :]
                    B1 = PW[:, 1, 1, :]
                    tt(A1, mag[:], cosv[:], ALU.mult, "PW", "mag", "cosv")
                    tt(B1, mag[:], sinv[:], ALU.mult, "PW", "mag", "sinv")
                    V(lambda e: e.memset(PW[:, 0, 0, :], 1.0), [], ["PW"], eng="pool")
                    V(lambda e: e.memset(PW[:, 0, 1, :], 0.0), [], ["PW"], eng="pool")
                    V(lambda e: e.memset(IPW[:, 0, 0, :], 1.0), [], ["IPW"], eng="pool")
                    V(lambda e: e.memset(IPW[:, 0, 1, :], 0.0), [], ["IPW"], eng="pool")
                    ts(am1[:], A1, -1.0, ALU.add, "am1", "PW")
                    tt(u1[:], lr[:], lr[:], ALU.mult, "u1", "lr", "lr")
                    tt(u2[:], li[:], li[:], ALU.mult, "u2", "li", "li")
                    tt(den[:], u1[:], u2[:], ALU.add, "den", "u1", "u2")
                    V(lambda e: e.reciprocal(out=rden[:], in_=den[:]), ["den"], ["rden"])
                    tt(u1[:], am1[:], lr[:], ALU.mult, "u1", "am1", "lr")
                    tt(u2[:], B1, li[:], ALU.mult, "u2", "PW", "li")
                    tt(u3[:], u1[:], u2[:], ALU.add, "u3", "u1", "u2")
                    tt(wr[:], u3[:], rden[:], ALU.mult, "wr", "u3", "rden")
                    tt(u1[:], B1, lr[:], ALU.mult, "u1", "PW", "lr")
                    tt(u2[:], am1[:], li[:], ALU.mult, "u2", "am1", "li")
                    tt(u3[:], u1[:], u2[:], ALU.subtract, "u3", "u1", "u2")
                    tt(wi[:], u3[:], rden[:], ALU.mult, "wi", "u3", "rden")

                    def bc(t):
                        return t.unsqueeze(2).broadcast_to([128, 32, 16])

                    tt(e1[:], Y1[:], bc(wr[:, :]), ALU.mult, "e1", "Y1", "wr")
                    tt(e2[:], Y2[:], bc(wi[:, :]), ALU.mult, "e2", "Y2", "wi")
                    tt(BB1[:], e1[:], e2[:], ALU.add, "BB1", "e1", "e2")
                    tt(e1[:], Y2[:], bc(wr[:, :]), ALU.mult, "e1", "Y2", "wr")
                    tt(e2[:], Y1[:], bc(wi[:, :]), ALU.mult, "e2", "Y1", "wi")
                    tt(BB2[:], e1[:], e2[:], ALU.subtract, "BB2", "e1", "e2")

                    def cmul(oa, ob, a, b, c_, d_, on, rn):
                        tt(u1[:], a, c_, ALU.mult, "u1", rn[0], rn[1])
                        tt(u2[:], b, d_, ALU.mult, "u2", rn[0], rn[1])
                        tt(u3[:], a, d_, ALU.mult, "u3", rn[0], rn[1])
                        tt(u4[:], b, c_, ALU.mult, "u4", rn[0], rn[1])
                        tt(oa, u1[:], u2[:], ALU.subtract, on, "u1", "u2")
                        tt(ob, u3[:], u4[:], ALU.add, on, "u3", "u4")

                    for k in range(1, 8):
                        cmul(PW[:, k + 1, 0, :], PW[:, k + 1, 1, :], PW[:, k, 0, :], PW[:, k, 1, :], A1, B1, "PW", ("PW", "PW"))
                    tt(u1[:], A1, A1, ALU.mult, "u1", "PW", "PW")
                    tt(u2[:], B1, B1, ALU.mult, "u2", "PW", "PW")
                    tt(den[:], u1[:], u2[:], ALU.add, "den", "u1", "u2")
                    V(lambda e: e.reciprocal(out=rden[:], in_=den[:]), ["den"], ["rden"])
                    tt(IPW[:, 1, 0, :], A1, rden[:], ALU.mult, "IPW", "PW", "rden")
                    V(lambda e: e.scalar_tensor_tensor(out=IPW[:, 1, 1, :], in0=B1, scalar=-1.0, in1=rden[:], op0=ALU.mult, op1=ALU.mult), ["PW", "rden"], ["IPW"])
                    for k in range(1, 7):
                        cmul(IPW[:, k + 1, 0, :], IPW[:, k + 1, 1, :], IPW[:, k, 0, :], IPW[:, k, 1, :], IPW[:, 1, 0, :], IPW[:, 1, 1, :], "IPW", ("IPW", "IPW"))
                    V(lambda e: e.tensor_copy(out=MU[:, 0, :, :], in_=PW[:, 8, :, :]), ["PW"], ["MU"])
                    for j in range(10):
                        cmul(MU[:, j + 1, 0, :], MU[:, j + 1, 1, :], MU[:, j, 0, :], MU[:, j, 1, :], MU[:, j, 0, :], MU[:, j, 1, :], "MU", ("MU", "MU"))
                    if CUT < 11:
                        S.flush()
                        return
                    for (CC, XX, ccn, xn) in ((CC1, X1, "CC1", "X1"), (CC2, X2, "CC2", "X2")):
                        for ch in range(4):
                            pb, pbn = bank()
                            S.op("pe", lambda e, CC=CC, ch=ch, pb=pb: e.transpose(out=pb[:, 0:128], in_=CC[:, ch, :], identity=ident[:]), reads=[ccn], writes=[pbn])
                            V(lambda e, XX=XX, ch=ch, pb=pb: e.tensor_copy(out=XX[:, ch * 8:(ch + 1) * 8, :], in_=pb[:, 0:128].rearrange("p (a n) -> p a n", n=16)), [pbn], [xn])
                    V(lambda e: e.tensor_scalar(out=X1[64:128], in0=X1[64:128], scalar1=-1.0, scalar2=None, op0=ALU.mult), ["X1"], ["X1"])
                    V(lambda e: e.tensor_scalar(out=X2[:], in0=X2[:], scalar1=-1.0, scalar2=None, op0=ALU.mult), ["X2"], ["X2"])

                    def table(dst, dn, M1, M2, m1n, m2n, pa, pb_, pn, sig):
                        tt(e1[:], M1[:], bc(pa), ALU.mult, "e1", m1n, pn)
                        tt(e2[:], M2[:], bc(pb_), ALU.mult, "e2", m2n, pn)
                        tt(dst[:, 0:16, sig, :], e1[:, 0:16, :], e2[:, 0:16, :], ALU.add, dn, "e1", "e2")
                        tt(dst[:, 16:32, 7 - sig, :], e1[:, 16:32, :], e2[:, 16:32, :], ALU.add, dn, "e1", "e2")

                    if CUT < 12:
                        S.flush()
                        return
                    for sig in range(8):
                        table(EBt, "EBt", BB1, BB2, "BB1", "BB2", PW[:, 7 - sig, 0, :], PW[:, 7 - sig, 1, :], "PW", sig)
                        table(ENt, "ENt", BB1, BB2, "BB1", "BB2", IPW[:, sig, 0, :], IPW[:, sig, 1, :], "IPW", sig)
                        table(GPt, "GPt", X1, X2, "X1", "X2", PW[:, sig, 0, :], PW[:, sig, 1, :], "PW", sig)
                        table(CBt, "CBt", X1, X2, "X1", "X2", PW[:, sig + 1, 0, :], PW[:, sig + 1, 1, :], "PW", sig)
                    if CUT < 13:
                        S.flush()
                        return
                    if DEBUG == 3:
                        dbt = sb(ps, "dbt3", [128, 2048], F32)
                        S.op("pool", lambda e: e.memset(dbt[:], 0.0), writes=["dbt"])
                        srcs = [X1[:, 0:8, :].rearrange("p a n -> p (a n)"), CBt[:, 0, :, :].rearrange("p s n -> p (s n)"), GPt[:, 0, :, :].rearrange("p s n -> p (s n)"),
                                BB1[:, 0:8, :].rearrange("p a n -> p (a n)"), EBt[:, 0, :, :].rearrange("p s n -> p (s n)"), Y1[:, 0:8, :].rearrange("p a n -> p (a n)"),
                                X2[:, 0:8, :].rearrange("p a n -> p (a n)"), e1[:, 0:8, :].rearrange("p a n -> p (a n)")]
                        for i_, src_ in enumerate(srcs):
                            S.op("dve", lambda e, i_=i_, src_=src_: e.tensor_copy(out=dbt[:, i_ * 128:(i_ + 1) * 128], in_=src_), reads=["dbt", "X1", "X2", "CBt", "GPt", "BB1", "EBt", "Y1", "e1"], writes=["dbt"])
                        S.op("dve", lambda e: e.tensor_copy(out=dbt[:, 1024:1024 + 576], in_=PW[:].rearrange("p a b c -> p (a b c)")), reads=["dbt", "PW"], writes=["dbt"])
                        S.dma("sp", lambda e: e.dma_start(out=R["DBG"], in_=dbt[:]), reads=["dbt"], writes=["DBG"])
                        S.flush()
                        return
                    V(lambda e: e.tensor_copy(out=cblk[:].rearrange("p a b -> p (a b)"), in_=CBt[:].rearrange("p a s n -> p (a s n)")), ["CBt"], ["cblk"])
                    if CUT < 15:
                        S.flush()
                        return
                    for dg in range(32):
                        pb, pbn = bank()
                        S.op("pe", lambda e, dg=dg, pb=pb: e.transpose(out=pb[:, 0:128], in_=EBt[:, dg, :, :].rearrange("p s m -> p (s m)"), identity=ident[:]),
                             reads=["EBt"], writes=[pbn])
                        eg = evac_eng()
                        S.op(eg, copy_op(eg, bblk[:, dg, :], pb[:, 0:128]), reads=[pbn], writes=["bblk"])
                        if CUT < 16:
                            continue
                        pb2, pbn2 = bank()
                        S.op("pe", lambda e, dg=dg, pb2=pb2: e.matmul(pb2[:, 0:128], lhsT=ENt[:, dg, :, :].rearrange("p s m -> p (s m)"),
                                                                   rhs=GPt[:, dg, :, :].rearrange("p s m -> p (s m)"), start=True, stop=True),
                             reads=["ENt", "GPt"], writes=[pbn2])
                        V(lambda e, dg=dg, pb2=pb2: e.tensor_tensor(out=Dm[:, dg, :], in0=pb2[:, 0:128], in1=msk[:, dg // 16, :], op=ALU.mult), [pbn2, "msk"], ["Dm"])
                    for g in range(16 if CUT >= 17 else 0):
                        V(lambda e, g=g: e.tensor_tensor(out=Dm[:, g, :], in0=Dm[:, g, :], in1=Dm[:, 16 + g, :], op=ALU.add), ["Dm"], ["Dm"])
                        V(lambda e, g=g: e.scalar_tensor_tensor(out=dsum[:, g, :], in0=ident[:], scalar=dsk[:, g:g + 1], in1=Dm[:, g, :], op0=ALU.mult, op1=ALU.add),
                          ["Dm", "dsk"], ["dsum"])
                    S.flush()

                if CUT < 18:
                    return
                if DEBUG == 2:
                    with ExitStack() as pd:
                        dbt = sb(pd, "dbt2", [128, 2048], F32)
                        S.op("pool", lambda e: e.memset(dbt[:], 0.0), writes=["dbt"])
                        for i_, src_ in enumerate([dsum[:, 0, :], bblk[:, 0, :], cblk[:, 0, :], bblk[:, 16, :], cblk[:, 16, :], dsum[:, 5, :]]):
                            S.op("dve", lambda e, i_=i_, src_=src_: e.tensor_copy(out=dbt[:, i_ * 128:(i_ + 1) * 128], in_=src_), reads=["dbt"], writes=["dbt"])
                        S.op("dve", lambda e: e.tensor_copy(out=dbt[:, 768:768 + 704], in_=MU[:].rearrange("p a b c -> p (a b c)")), reads=["dbt"], writes=["dbt"])
                        S.dma("sp", lambda e: e.dma_start(out=R["DBG"], in_=dbt[:]), reads=["dbt"], writes=["DBG"])
                        S.flush()
                with ExitStack() as ps:
                    selin = sb(ps, "selin", [128, 8, 8, 128], BF16)
                    selout = sb(ps, "selout", [128, 8, 8, 128], BF16)
                    for g in range(8):
                        S.dma("pool", lambda e, g=g: e.dma_start(out=selin[:, g, :, :], in_=I["s5_selin"][:, g, :, :]), writes=["selin"])
                        S.dma("pool", lambda e, g=g: e.dma_start(out=selout[:, g, :, :], in_=I["s5_selout"][:, g, :, :]), writes=["selout"])
                    wglu = sb(ps, "wglu", [128, 2, 256], BF16)
                    S.dma("pool", lambda e: e.dma_start(out=wglu[:], in_=I["s5_w_glu"][l].rearrange("(k p) n -> p k n", p=128)), writes=["wglu"])
                    uT = sb(ps, "uT", [128, T], BF16)
                    Vg = sb(ps, "Vg", [128, Q], BF16)
                    VRg = sb(ps, "VRg", [128, Q], BF16)
                    Hm = [sb(ps, "Hm%d" % d, [128, Q], F32) for d in range(2)]
                    Hs = [sb(ps, "Hs%d" % d, [128, Q + 2], BF16) for d in range(2)]
                    HinN = sb(ps, "HinN", [128, Q], BF16)
                    Rt = sb(ps, "Rt", [128, 128], F32)
                    Rb = [sb(ps, "Rb%d" % i, [128, 128], BF16) for i in range(2)]
                    Yg = [sb(ps, "Yg%d" % g, [128, Q], BF16) for g in range(8)]
                    yT = sb(ps, "yT", [128, T], F32)
                    gbf = sb(ps, "gbf", [128, 2, T], BF16)
                    gl = [sb(ps, "gl%d" % i, [128, 512], F32) for i in range(3)]
                    sgt = sb(ps, "sgt", [128, 512], BF16)
                    sbo = [sb(ps, "sbo%d" % i, [128, 512], BF16) for i in range(2)]
                    for d in range(2):
                        S.op("pool", lambda e, d=d: e.memset(Hs[d][:, 0:2], 0.0), writes=["Hs%d" % d])
                    uTv = uT[:, :].rearrange("p (c s) -> p c s", s=8)
                    yTv = yT[:, :].rearrange("p (c t) -> p c t", t=8)
                    rbc = [0]
                    for hf in range(2):
                        S.dma("sp", lambda e, hf=hf: e.dma_start(out=uT[:], in_=R["UT"][hf * 128:(hf + 1) * 128, :]), writes=["uT"])
                        for g in range(8):
                            gi = hf * 8 + g
                            for (c0, c1) in PCS:
                                pb, pbn = bank()
                                for s_ in range(8):
                                    S.op("pe", lambda e, g=g, s_=s_, pb=pb, c0=c0, c1=c1: e.matmul(pb[:, 0:c1 - c0], lhsT=selin[:, g, s_, :], rhs=uTv[:, c0:c1, s_],
                                                                                                 start=(s_ == 0), stop=(s_ == 7)), reads=["selin", "uT"], writes=[pbn])
                                eg = evac_eng()
                                S.op(eg, copy_op(eg, Vg[:, c0:c1], pb[:, 0:c1 - c0]), reads=[pbn], writes=["Vg"])
                            S.op("pool", lambda e: e.tensor_copy(out=VRg[:, 0:32], in_=Vg[:, 31::-1]), reads=["Vg"], writes=["VRg"])
                            S.op("pool", lambda e: e.tensor_copy(out=VRg[:, 32:Q], in_=Vg[:, Q - 1:31:-1]), reads=["Vg"], writes=["VRg"])
                            for d in range(2 if CUT >= 21 else 0):
                                dg = d * 16 + gi
                                src = Vg if d == 0 else VRg
                                srcn = "Vg" if d == 0 else "VRg"
                                hm, hs = Hm[d], Hs[d]
                                hmn, hsn = "Hm%d" % d, "Hs%d" % d
                                for (c0, c1) in PCS:
                                    pb, pbn = bank()
                                    S.op("pe", lambda e, dg=dg, pb=pb, c0=c0, c1=c1, src=src: e.matmul(pb[:, 0:c1 - c0], lhsT=bblk[:, dg, :], rhs=src[:, c0:c1], start=True, stop=True),
                                         reads=["bblk", srcn], writes=[pbn])
                                    S.op("dve", lambda e, pb=pb, c0=c0, c1=c1, hm=hm: e.tensor_copy(out=hm[:, c0:c1], in_=pb[:, 0:c1 - c0]), reads=[pbn], writes=[hmn])
                                    S.op("act", lambda e, c0=c0, c1=c1, hs=hs, hm=hm: e.activation(out=hs[:, 2 + c0:2 + c1], in_=hm[:, c0:c1], func=AF.Copy), reads=[hmn], writes=[hsn])
                                for j in range(11 if CUT >= 22 else 0):
                                    sh = 1 << j
                                    rb = Rb[rbc[0] % 2]
                                    rbn = "Rb%d" % (rbc[0] % 2)
                                    rbc[0] += 1
                                    S.op("dve", lambda e, j=j, dg=dg: e.tensor_scalar(out=Rt[:], in0=ident[:], scalar1=MU[:, j, 0, dg:dg + 1], scalar2=None, op0=ALU.mult),
                                         reads=["MU"], writes=["Rt"])
                                    S.op("dve", lambda e, j=j, dg=dg, rb=rb: e.scalar_tensor_tensor(out=rb[:], in0=jm[:], scalar=MU[:, j, 1, dg:dg + 1], in1=Rt[:], op0=ALU.mult, op1=ALU.add),
                                         reads=["MU", "Rt", "jm"], writes=[rbn])
                                    pieces = []
                                    q0 = sh
                                    while q0 < Q:
                                        q1 = min(q0 + 512, Q)
                                        pieces.append((q0, q1))
                                        q0 = q1
                                    pbs = []
                                    for (q0, q1) in pieces:
                                        pb, pbn = bank()
                                        pbs.append((pb, pbn))
                                        S.op("pe", lambda e, pb=pb, q0=q0, q1=q1, sh=sh, rb=rb, hs=hs: e.matmul(pb[:, 0:q1 - q0], lhsT=rb[:], rhs=hs[:, 2 + q0 - sh:2 + q1 - sh], start=True, stop=True),
                                             reads=[rbn, hsn], writes=[pbn])
                                    for (q0, q1), (pb, pbn) in zip(pieces, pbs):
                                        S.op("dve", lambda e, pb=pb, q0=q0, q1=q1, hm=hm: e.tensor_tensor(out=hm[:, q0:q1], in0=pb[:, 0:q1 - q0], in1=hm[:, q0:q1], op=ALU.add),
                                             reads=[pbn, hmn], writes=[hmn])
                                    for (a0, a1) in ((0, 512), (512, 1024), (1024, Q)):
                                        if a1 <= sh:
                                            continue
                                        S.op("act", lambda e, a0=a0, a1=a1, hm=hm, hs=hs: e.activation(out=hs[:, 2 + a0:2 + a1], in_=hm[:, a0:a1], func=AF.Copy), reads=[hmn], writes=[hsn])
                            S.op("pool", lambda e: e.tensor_copy(out=HinN[:, 0:32], in_=Hs[1][:, 32:0:-1]), reads=["Hs1"], writes=["HinN"])
                            S.op("pool", lambda e: e.tensor_copy(out=HinN[:, 32:Q], in_=Hs[1][:, Q:32:-1]), reads=["Hs1"], writes=["HinN"])
                            yg = Yg[g]
                            for (c0, c1) in (PCS if CUT >= 23 else []):
                                pb, pbn = bank()
                                S.op("pe", lambda e, gi=gi, pb=pb, c0=c0, c1=c1: e.matmul(pb[:, 0:c1 - c0], lhsT=dsum[:, gi, :], rhs=Vg[:, c0:c1], start=True, stop=False),
                                     reads=["dsum", "Vg"], writes=[pbn])
                                S.op("pe", lambda e, gi=gi, pb=pb, c0=c0, c1=c1: e.matmul(pb[:, 0:c1 - c0], lhsT=cblk[:, gi, :], rhs=Hs[0][:, 1 + c0:1 + c1], start=False, stop=False),
                                     reads=["cblk", "Hs0"], writes=[pbn])
                                S.op("pe", lambda e, gi=gi, pb=pb, c0=c0, c1=c1: e.matmul(pb[:, 0:c1 - c0], lhsT=cblk[:, 16 + gi, :], rhs=HinN[:, c0:c1], start=False, stop=True),
                                     reads=["cblk", "HinN"], writes=[pbn])
                                eg = evac_eng()
                                S.op(eg, copy_op(eg, yg[:, c0:c1], pb[:, 0:c1 - c0]), reads=[pbn], writes=["Yg%d" % g])
                        for t_ in range(8 if CUT >= 24 else 0):
                            for (c0, c1) in PCS:
                                pb, pbn = bank()
                                for g in range(8):
                                    S.op("pe", lambda e, g=g, t_=t_, pb=pb, c0=c0, c1=c1: e.matmul(pb[:, 0:c1 - c0], lhsT=selout[:, t_, g, :], rhs=Yg[g][:, c0:c1],
                                                                                                 start=(g == 0), stop=(g == 7)), reads=["selout", "Yg%d" % g], writes=[pbn])
                                eg = evac_eng()
                                S.op(eg, copy_op(eg, yTv[:, c0:c1, t_], pb[:, 0:c1 - c0]), reads=[pbn], writes=["yT"])
                        if DEBUG:
                            S.dma("sp", lambda e, hf=hf: e.dma_start(out=R["YT"][hf * 128:(hf + 1) * 128, :], in_=yT[:]), reads=["yT"], writes=["YTd"])
                        for i, (t0, n, v) in enumerate(tiles_of(512) if CUT >= 25 else []):
                            a_, b_, c_ = gl
                            ys = yT[:, t0:t0 + n]
                            S.op("pool", lambda e, ys=ys, n=n: e.tensor_tensor(out=a_[:, :n], in0=ys, in1=ys, op=ALU.mult), reads=["yT"], writes=["gl0"])
                            S.op("dve", lambda e, n=n: e.tensor_scalar(out=b_[:, :n], in0=a_[:, :n], scalar1=0.044715, scalar2=1.0, op0=ALU.mult, op1=ALU.add), reads=["gl0"], writes=["gl1"])
                            S.op("pool", lambda e, ys=ys, n=n: e.tensor_tensor(out=a_[:, :n], in0=b_[:, :n], in1=ys, op=ALU.mult), reads=["gl1", "yT"], writes=["gl0"])
                            S.op("act", lambda e, n=n: e.activation(out=c_[:, :n], in_=a_[:, :n], func=AF.Sigmoid, scale=1.5957691216), reads=["gl0"], writes=["gl2"])
                            S.op("dve", lambda e, ys=ys, n=n, t0=t0, hf=hf: e.tensor_tensor(out=gbf[:, hf, t0:t0 + n], in0=c_[:, :n], in1=ys, op=ALU.mult), reads=["gl2", "yT"], writes=["gbf"])
                    SBv = R["SB"].rearrange("(c p) t -> p c t", p=128)
                    for i, (t0, n, v) in enumerate(tiles_of(512) if CUT >= 26 else []):
                        for oc in range(2):
                            pb, pbn = bank()
                            for kc in range(2):
                                S.op("pe", lambda e, oc=oc, kc=kc, pb=pb, t0=t0, n=n: e.matmul(pb[:, :n], lhsT=wglu[:, kc, oc * 128:(oc + 1) * 128], rhs=gbf[:, kc, t0:t0 + n],
                                                                                             start=(kc == 0), stop=(kc == 1)), reads=["wglu", "gbf"], writes=[pbn])
                            S.op("act", lambda e, pb=pb, n=n: e.activation(out=sgt[:, :n], in_=pb[:, :n], func=AF.Sigmoid), reads=[pbn], writes=["sgt"])
                            so = sbo[oc]
                            S.op("dve", lambda e, so=so, oc=oc, t0=t0, n=n: e.tensor_tensor(out=so[:, :n], in0=sgt[:, :n], in1=gbf[:, oc, t0:t0 + n], op=ALU.mult),
                                 reads=["sgt", "gbf"], writes=["sbo%d" % oc])
                            S.dma("sp", lambda e, so=so, oc=oc, t0=t0, n=n: e.dma_start(out=SBv[:, oc, t0:t0 + n], in_=so[:, :n]), reads=["sbo%d" % oc], writes=["SBo"])
                    S.flush()

        def phase3a(l):
            NT = 512
            tl = tiles_of(NT, with_ctx=(l < DEPTH - 1))
            with ExitStack() as ps:
                wa = sb(ps, "wa", [128, 4, D], BF16)
                wbra = sb(ps, "wbra", [128, 2, D], BF16)
                wb = sb(ps, "wb", [128, 2, D], BF16)
                wc = sb(ps, "wc", [128, 4, D], BF16)
                wo = sb(ps, "wo", [128, 8, D], BF16)
                cs64 = sb(ps, "cs64", [128, 2, 128], BF16)
                S.dma("pool", lambda e: e.dma_start(out=cs64[:], in_=I["cs64"].rearrange("a p m -> p a m")), writes=["cs64"])
                S.dma("pool", lambda e: e.dma_start(out=wbra[:], in_=I["w_br_a"][l].rearrange("(k p) n -> p k n", p=128)), writes=["wbra"])
                S.dma("pool", lambda e: e.dma_start(out=wb[:], in_=I["w_br_b"][l].rearrange("(k p) n -> p k n", p=128)), writes=["wb"])
                S.dma("pool", lambda e: e.dma_start(out=wc[:], in_=I["w_br_c"][l].rearrange("(k p) n -> p k n", p=128)), writes=["wc"])
                S.dma("pool", lambda e: e.dma_start(out=wo[:], in_=I["w_out"][l].rearrange("(k p) n -> p k n", p=128)), writes=["wo"])
                for a in range(2 if CUT > -1 else 0):
                    for q in range(2):
                        for hh in range(2):
                            pb, pbn = bank()
                            S.op("pe", lambda e, a=a, q=q, hh=hh, pb=pb: e.matmul(pb[:, :], lhsT=cs64[:, a, :], rhs=wbra[:, q, hh * 512:(hh + 1) * 512],
                                                                                 start=True, stop=True), reads=["cs64", "wbra"], writes=[pbn])
                            eg = evac_eng()
                            S.op(eg, copy_op(eg, wa[:, a * 2 + q, hh * 512:(hh + 1) * 512], pb[:, :]), reads=[pbn], writes=["wa"])
                xTs = [sb(ps, "xT%d" % i, [128, 8, NT], F32) for i in range(2)]
                fsn = [sb(ps, "fsn%d" % i, [128, 10, NT], BF16) for i in range(2)]
                gts = [sb(ps, "gts%d" % i, [128, 24, NT], BF16) for i in range(2)]
                mT = sb(ps, "mT", [128, 8, NT], BF16)
                hT = sb(ps, "hT", [128, 8, NT], BF16)
                sq = sb(ps, "sq", [128, 8, NT], BF16)
                rstd = sb(ps, "rstd", [128, NT], F32)
                tmpa = sb(ps, "tmpa", [128, NT], F32)
                tmps = [sb(ps, "tmps%d" % i, [128, NT], F32) for i in range(2)]
                t1s = [sb(ps, "t1_%d" % i, [128, NT], F32) for i in range(2)]
                t2s = [sb(ps, "t2_%d" % i, [128, NT], F32) for i in range(2)]
                t3s = [sb(ps, "t3_%d" % i, [128, NT], F32) for i in range(2)]
                XTv = R["XT"].rearrange("(c p) t -> p c t", p=128)
                XMv = R["XM"].rearrange("(c p) t -> p c t", p=128)
                H2v = R["H2"].rearrange("(c p) t -> p c t", p=128)
                FAv = R["FA"].rearrange("(c p) t -> p c t", p=128)
                SBv = R["SB"].rearrange("(c p) t -> p c t", p=128)
                NCv = R["NCT"].rearrange("(c p) t -> p c t", p=128)
                GTv = R["GT"].rearrange("(c p) t -> p c t", p=128)

                def load(i):
                    t0, n, v = tl[i]
                    b = i % 2
                    S.dma("sp", lambda e: e.dma_start(out=xTs[b][:, :, :n], in_=XTv[:, :, t0:t0 + n]), writes=["xT%d" % b])
                    S.dma("sp", lambda e: e.dma_start(out=fsn[b][:, 0:4, :n], in_=FAv[:, :, t0:t0 + n]), writes=["fsnA%d" % b])
                    S.dma("sp", lambda e: e.dma_start(out=fsn[b][:, 4:6, :n], in_=SBv[:, :, t0:t0 + n]), writes=["fsnB%d" % b])
                    S.dma("sp", lambda e: e.dma_start(out=fsn[b][:, 6:10, :n], in_=NCv[:, :, t0:t0 + n]), writes=["fsnC%d" % b])
                    for g in range(3):
                        S.dma("sp", lambda e, g=g: e.dma_start(out=gts[b][:, g * 8:(g + 1) * 8, :n], in_=GTv[:, g * 8:(g + 1) * 8, t0:t0 + n]),
                              writes=["gts%d_%d" % (b, g)])

                def compute(i):
                    t0, n, v = tl[i]
                    b = i % 2
                    xT = xTs[b]
                    xn = "xT%d" % b
                    f = fsn[b]
                    g = gts[b]
                    if CUT < 1:
                        return
                    for d in range(8):
                        dd = d % 2
                        ds = slice(d * 128, (d + 1) * 128)
                        pa, pan = bank()
                        for k in range(4):
                            S.op("pe", lambda e, k=k, pa=pa, ds=ds: e.matmul(pa[:, :n], lhsT=wa[:, k, ds], rhs=f[:, k, :n], start=(k == 0), stop=(k == 3)),
                                 reads=["wa", "fsnA%d" % b], writes=[pan])
                        pbk, pbn = bank()
                        for k in range(2):
                            S.op("pe", lambda e, k=k, pbk=pbk, ds=ds: e.matmul(pbk[:, :n], lhsT=wb[:, k, ds], rhs=f[:, 4 + k, :n], start=(k == 0), stop=(k == 1)),
                                 reads=["wb", "fsnB%d" % b], writes=[pbn])
                        pc, pcn = bank()
                        for k in range(4):
                            S.op("pe", lambda e, k=k, pc=pc, ds=ds: e.matmul(pc[:, :n], lhsT=wc[:, k, ds], rhs=f[:, 6 + k, :n], start=(k == 0), stop=(k == 3)),
                                 reads=["wc", "fsnC%d" % b], writes=[pcn])
                        t1, t2, t3 = t1s[dd], t2s[dd], t3s[dd]
                        S.op("dve", lambda e, pa=pa, t1=t1, d=d: e.tensor_tensor(out=t1[:, :n], in0=pa[:, :n], in1=g[:, d, :n], op=ALU.mult),
                             reads=[pan, "gts%d_0" % b], writes=["t1_%d" % dd])
                        S.op("dve", lambda e, pbk=pbk, t2=t2, d=d: e.tensor_tensor(out=t2[:, :n], in0=pbk[:, :n], in1=g[:, 8 + d, :n], op=ALU.mult),
                             reads=[pbn, "gts%d_1" % b], writes=["t2_%d" % dd])
                        S.op("dve", lambda e, pc=pc, t3=t3, d=d: e.tensor_tensor(out=t3[:, :n], in0=pc[:, :n], in1=g[:, 16 + d, :n], op=ALU.mult),
                             reads=[pcn, "gts%d_2" % b], writes=["t3_%d" % dd])
                        S.op("pool", lambda e, t1=t1, t2=t2: e.tensor_tensor(out=t1[:, :n], in0=t1[:, :n], in1=t2[:, :n], op=ALU.add),
                             reads=["t1_%d" % dd, "t2_%d" % dd], writes=["t1_%d" % dd])
                        S.op("pool", lambda e, t1=t1, t3=t3, d=d: e.tensor_tensor(out=mT[:, d, :n], in0=t1[:, :n], in1=t3[:, :n], op=ALU.add),
                             reads=["t1_%d" % dd, "t3_%d" % dd], writes=["mT"])
                    if CUT < 2:
                        return
                    for d in range(8):
                        ds = slice(d * 128, (d + 1) * 128)
                        po, pon = bank()
                        for k in range(8):
                            S.op("pe", lambda e, k=k, po=po, ds=ds: e.matmul(po[:, :n], lhsT=wo[:, k, ds], rhs=mT[:, k, :n], start=(k == 0), stop=(k == 7)),
                                 reads=["wo", "mT"], writes=[pon])
                        S.op("dve", lambda e, po=po, d=d: e.scalar_tensor_tensor(out=xT[:, d, :n], in0=po[:, :n], scalar=mod[:, l, 16 + d, v:v + 1],
                                                                               in1=xT[:, d, :n], op0=ALU.mult, op1=ALU.add),
                             reads=[pon, xn, "mod"], writes=[xn])
                    if CUT < 3:
                        return
                    S.dma("sp", lambda e: e.dma_start(out=XMv[:, :, t0:t0 + n], in_=xT[:, :, :n]), reads=[xn], writes=["XM%d" % i])
                    if CUT < 4:
                        return
                    norm_mod(l, 1, v, xT, n, sq, rstd, tmpa, tmps, hT, xn, "hT", i)
                    S.dma("sp", lambda e: e.dma_start(out=H2v[:, :, t0:t0 + n], in_=hT[:, :, :n]), reads=["hT"], writes=["H2%d" % i])

                load(0)
                for i in range(len(tl)):
                    if i + 1 < len(tl):
                        load(i + 1)
                    compute(i)
                S.flush()

        def phase3b(l):
            NT = 256
            last = (l == DEPTH - 1)
            tl = tiles_of(NT, with_ctx=not last)
            with ExitStack() as ps:
                w1 = sb(ps, "w1", [128, 8, DFF], BF16)
                w2 = sb(ps, "w2", [128, 32, D], BF16)
                w1v = I["w_ff1"][l].rearrange("(k p) n -> p k n", p=128)
                w2v = I["w_ff2"][l].rearrange("(k p) n -> p k n", p=128)
                for j in range(8):
                    S.dma("pool", lambda e, j=j: e.dma_start(out=w1[:, :, j * 512:(j + 1) * 512], in_=w1v[:, :, j * 512:(j + 1) * 512]), writes=["w1_%d" % j])
                for j in range(8):
                    S.dma("pool", lambda e, j=j: e.dma_start(out=w2[:, j * 4:(j + 1) * 4, :], in_=w2v[:, j * 4:(j + 1) * 4, :]), writes=["w2_%d" % j])
                xTs = [sb(ps, "xT%d" % i, [128, 8, NT], F32) for i in range(2)]
                hTs = [sb(ps, "hT%d" % i, [128, 8, NT], BF16) for i in range(2)]
                aT = sb(ps, "aT", [128, 32, NT], BF16)
                rr = [sb(ps, "rr%d" % i, [128, NT], BF16) for i in range(2)]
                XTv = R["XT"].rearrange("(c p) t -> p c t", p=128)
                XMv = R["XM"].rearrange("(c p) t -> p c t", p=128)
                H2v = R["H2"].rearrange("(c p) t -> p c t", p=128)
                if last:
                    sq = sb(ps, "sq", [128, 8, NT], BF16)
                    rstd = sb(ps, "rstd", [128, NT], F32)
                    tmpa = sb(ps, "tmpa", [128, NT], F32)
                    yT = sb(ps, "yT", [128, 8, NT], F32)
                    otok = sb(ps, "otok", [128, 2, D], F32)

                def load(i):
                    t0, n, v = tl[i]
                    b = i % 2
                    S.dma("sp", lambda e: e.dma_start(out=xTs[b][:, :, :n], in_=XMv[:, :, t0:t0 + n]), writes=["xT%d" % b])
                    S.dma("sp", lambda e: e.dma_start(out=hTs[b][:, :, :n], in_=H2v[:, :, t0:t0 + n]), writes=["hT%d" % b])

                def compute(i):
                    t0, n, v = tl[i]
                    b = i % 2
                    xT = xTs[b]
                    xn = "xT%d" % b
                    hT = hTs[b]
                    hn = "hT%d" % b
                    for f in range(32):
                        pb, pbn = bank()
                        for k in range(8):
                            S.op("pe", lambda e, k=k, f=f, pb=pb: e.matmul(pb[:, :n], lhsT=w1[:, k, f * 128:(f + 1) * 128], rhs=hT[:, k, :n],
                                                                         start=(k == 0), stop=(k == 7)), reads=[hn, "w1_%d" % (f // 4)], writes=[pbn])
                        r = rr[f % 2]
                        rn = "rr%d" % (f % 2)
                        S.op("act", lambda e, pb=pb, r=r: e.activation(out=r[:, :n], in_=pb[:, :n], func=AF.Relu), reads=[pbn], writes=[rn])
                        S.op("pool", lambda e, r=r, f=f: e.tensor_tensor(out=aT[:, f, :n], in0=r[:, :n], in1=r[:, :n], op=ALU.mult),
                             reads=[rn], writes=["aT"])
                    for d in range(8):
                        pb, pbn = bank()
                        for f in range(32):
                            S.op("pe", lambda e, f=f, d=d, pb=pb: e.matmul(pb[:, :n], lhsT=w2[:, f, d * 128:(d + 1) * 128], rhs=aT[:, f, :n],
                                                                         start=(f == 0), stop=(f == 31)), reads=["aT", "w2_%d" % (f // 4)], writes=[pbn])
                        S.op("dve", lambda e, pb=pb, d=d: e.scalar_tensor_tensor(out=xT[:, d, :n], in0=pb[:, :n], scalar=mod[:, l, 40 + d, v:v + 1],
                                                                               in1=xT[:, d, :n], op0=ALU.mult, op1=ALU.add),
                             reads=[pbn, xn, "mod"], writes=[xn])
                    if not last:
                        S.dma("sp", lambda e: e.dma_start(out=XTv[:, :, t0:t0 + n], in_=xT[:, :, :n]), reads=[xn], writes=["XT%d" % i])
                        return
                    S.op("act", lambda e: e.activation(out=sq[:, :, :n], in_=xT[:, :, :n], func=AF.Square), reads=[xn], writes=["sq"])
                    pb, pbn = bank()
                    for c in range(8):
                        S.op("pe", lambda e, c=c, pb=pb: e.matmul(pb[:, :n], lhsT=ones_bf[:], rhs=sq[:, c, :n], start=(c == 0), stop=(c == 7)),
                             reads=["sq", "ones"], writes=[pbn])
                    S.op("act", lambda e, pb=pb: e.activation(out=tmpa[:, :n], in_=pb[:, :n], func=AF.Sqrt, scale=1.0 / D, bias=epsb[:, 0:1]),
                         reads=[pbn, "epsb"], writes=["tmpa"])
                    S.op("dve", lambda e: e.reciprocal(out=rstd[:, :n], in_=tmpa[:, :n]), reads=["tmpa"], writes=["rstd"])
                    for c in range(8):
                        S.op("dve", lambda e, c=c: e.scalar_tensor_tensor(out=yT[:, c, :n], in0=xT[:, c, :n], scalar=gfin[:, c:c + 1], in1=rstd[:, :n],
                                                                          op0=ALU.mult, op1=ALU.mult), reads=[xn, "rstd", "gfin"], writes=["yT"])
                    for s in range(n // 128):
                        for half in range(2):
                            pb, pbn = bank()
                            for cc in range(4):
                                c = half * 4 + cc
                                S.op("pe", lambda e, s=s, c=c, cc=cc, pb=pb: e.transpose(out=pb[:, cc * 128:(cc + 1) * 128], in_=yT[:, c, s * 128:(s + 1) * 128],
                                                                                       identity=ident[:]), reads=["yT", "ident"], writes=[pbn])
                            eg = evac_eng()
                            S.op(eg, copy_op(eg, otok[:, s, half * 512:(half + 1) * 512], pb[:, :]), reads=[pbn], writes=["otok"])
                    r0 = t0 - LC
                    S.dma("sp", lambda e: e.dma_start(out=out[r0:r0 + n, :].rearrange("(s p) d -> p s d", p=128), in_=otok[:, :n // 128, :]),
                          reads=["otok"], writes=["out%d" % i])

                load(0)
                for i in range(len(tl)):
                    if i + 1 < len(tl):
                        load(i + 1)
                    compute(i)
                S.flush()

        if "p0" in phases:
            phase0()
        for l in layers:
            if "p1" in phases:
                phase1(l)
            if "p2f" in phases:
                phase2_fnet(l)
            if "p2n" in phases:
                phase2_na(l)
            if "p2s" in phases:
                phase2_s5(l)
            if "p3a" in phases:
                phase3a(l)
            if "p3b" in phases:
                phase3b(l)
        if S.ops["sp"] or S.ops["pe"] or S.ops["pool"]:
            S.flush()
    return nc


ALL_PHASES = ("p0", "p1", "p2f", "p2n", "p2s", "p3a", "p3b")


def kernel(**inputs):
    f32 = lambda a: np.ascontiguousarray(np.asarray(a, dtype=np.float32))
    x = f32(inputs["x"])
    ctx = f32(inputs["ctx"])
    c = f32(inputs["c"])
    c_ctx = f32(inputs["c_ctx"])
    shared = {}
    for k in W_SHAPES:
        if k == "rpbg":
            shared[k] = na_gather_rpb(f32(inputs["na_rpb"]))
        else:
            shared[k] = f32(inputs[k])
    shared.update(_consts())
    nb = x.shape[0]
    in_maps = []
    for core in range(8):
        b = core % nb
        m = dict(shared)
        m["x"] = x[b]
        m["ctx"] = ctx[b]
        m["cvec"] = np.ascontiguousarray(np.stack([c[b], c_ctx]))
        in_maps.append(m)
    nc = build(set(ALL_PHASES))
    res = run_bass_kernel_spmd(nc, in_maps, core_ids=list(range(8)))
    out = np.stack([np.asarray(res.results[b]["out"], dtype=np.float32) for b in range(nb)], axis=0)
    return out
```

```python
import numpy as np
from contextlib import ExitStack
import concourse.bass as bass
import concourse.mybir as mybir
from concourse.bass_utils import run_bass_kernel_spmd

F32 = mybir.dt.float32
BF16 = mybir.dt.bfloat16
AF = mybir.ActivationFunctionType
ALU = mybir.AluOpType

D = 1024
L = 8192
LC = 256
T = L + LC
DEPTH = 2
DIN = 5120
DFF = 4096
EPS = 1e-6
NEG = -30000.0
CUT = 99
DEBUG = 0

COMPUTE = ("pe", "dve", "act", "pool")
NDMA_SEM = 12


class Sched:
    def __init__(self, nc, stack):
        self.nc = nc
        self.ops = {e: [] for e in ("pe", "dve", "act", "pool", "sp")}
        self.sem = {}
        self.cnt = {}
        for e in COMPUTE:
            self.sem[e] = stack.enter_context(nc.semaphore("s_" + e))
            self.cnt[e] = 0
        self.dsem = {}
        self.dcnt = {}
        self.dnext = {}
        for q in ("sp", "pool", "act"):
            self.dsem[q] = [stack.enter_context(nc.semaphore("d_%s%d" % (q, j))) for j in range(NDMA_SEM)]
            self.dcnt[q] = [0] * NDMA_SEM
            self.dnext[q] = 0
        self.seen = {e: {} for e in self.ops}
        self.last_w = {}
        self.reads = {}
        self.out_events = []
        self.nblock = 0

    def _need(self, eng, reads, writes):
        need = {}

        def add(ev):
            if ev is None:
                return
            k, v = ev
            if need.get(k, 0) < v:
                need[k] = v

        for r in reads:
            add(self.last_w.get(r))
        own = eng in COMPUTE
        for w in writes:
            ev = self.last_w.get(w)
            if ev is not None and not (own and ev[0] == eng):
                add(ev)
            for ev in self.reads.get(w, ()):
                if not (own and ev[0] == eng):
                    add(ev)
        waits = []
        for k, v in need.items():
            if k == eng and eng == "pe":
                continue
            if self.seen[eng].get(k, 0) >= v:
                continue
            self.seen[eng][k] = v
            waits.append((k, v))
        return waits

    def _semof(self, k):
        if isinstance(k, tuple):
            return self.dsem[k[0]][k[1]]
        return self.sem[k]

    def _commit(self, ev, reads, writes):
        for w in writes:
            self.last_w[w] = ev
            self.reads[w] = []
        for r in reads:
            self.reads.setdefault(r, []).append(ev)

    def op(self, eng, fn, reads=(), writes=()):
        waits = self._need(eng, reads, writes)
        self.cnt[eng] += 1
        ev = (eng, self.cnt[eng])
        self.ops[eng].append((waits, fn, self.sem[eng], 1))
        self._commit(ev, reads, writes)
        return ev

    def dma(self, q, fn, reads=(), writes=(), is_output=False):
        j = self.dnext[q]
        self.dnext[q] = (j + 1) % NDMA_SEM
        key = (q, j)
        waits = self._need(q, reads, writes)
        if self.dcnt[q][j] > 0 and self.seen[q].get(key, 0) < self.dcnt[q][j]:
            self.seen[q][key] = self.dcnt[q][j]
            waits.append((key, self.dcnt[q][j]))
        self.dcnt[q][j] += 16
        ev = (key, self.dcnt[q][j])
        self.ops[q].append((waits, fn, self.dsem[q][j], 16))
        self._commit(ev, reads, writes)
        return ev

    def flush(self):
        tail = {e: [] for e in self.ops}
        for e in self.ops:
            for k in COMPUTE:
                if k != e and self.cnt[k] > self.seen[e].get(k, 0):
                    self.seen[e][k] = self.cnt[k]
                    tail[e].append((k, self.cnt[k]))
            for q in self.dsem:
                for j in range(NDMA_SEM):
                    v = self.dcnt[q][j]
                    if v > self.seen[e].get((q, j), 0):
                        self.seen[e][(q, j)] = v
                        tail[e].append(((q, j), v))
        nc = self.nc
        with nc.Block() as block:
            def replay(name):
                def body(e):
                    for waits, fn, sem, inc in self.ops[name]:
                        for k, v in waits:
                            e.wait_ge(self._semof(k), v)
                        fn(e).then_inc(sem, inc)
                    for k, v in tail[name]:
                        e.wait_ge(self._semof(k), v)
                return body
            block.sync(replay("sp"))
            block.tensor(replay("pe"))
            block.vector(replay("dve"))
            block.scalar(replay("act"))
            block.gpsimd(replay("pool"))
        for e in self.ops:
            self.ops[e] = []
        self.last_w = {}
        self.reads = {}
        self.nblock += 1


def _consts():
    c = {}
    c["ident"] = np.eye(128, dtype=np.float32)
    ij = np.outer(np.arange(64), np.arange(64)) * (2 * np.pi / 64)
    cs = np.zeros((2, 128, 128), np.float32)
    for g in range(2):
        cs[0, g * 64:(g + 1) * 64, g * 64:(g + 1) * 64] = np.cos(ij)
        cs[1, g * 64:(g + 1) * 64, g * 64:(g + 1) * 64] = np.sin(ij)
    c["cs64"] = cs
    r = np.arange(128)[:, None, None]
    cc = np.arange(64)[None, :, None]
    k1 = np.arange(128)[None, None, :]
    ang = (2 * np.pi / 8192) * ((k1 * (64 * r + cc)) % 8192)
    c["fn_tc"] = np.cos(ang).astype(np.float32)
    c["fn_tsn"] = (-np.sin(ang)).astype(np.float32)
    sc = 1.0 / np.sqrt(8192.0 * 64.0)
    a64 = (2 * np.pi / 64) * ((np.arange(64)[:, None] * np.arange(64)[None, :]) % 64)
    c["fn_w64"] = np.stack([np.cos(a64) * sc, np.sin(a64) * sc, -np.sin(a64) * sc]).astype(np.float32)
    scc = 1.0 / np.sqrt(256.0 * 64.0)
    a256 = (2 * np.pi / 256) * ((np.arange(256)[:, None] * np.arange(256)[None, :]) % 256)
    c["fn_c256"] = np.stack([np.cos(a256) * scc, -np.sin(a256) * scc]).astype(np.float32)
    w = np.arange(64)
    cs0 = np.clip(w - 8, 0, 48)
    wp = np.arange(64)[:, None]
    inside = (wp >= cs0[None, :]) & (wp < cs0[None, :] + 16)
    cm = np.where(inside, 0.0, NEG).astype(np.float32)
    jm = np.zeros((128, 128), np.float32)
    for p in range(64):
        jm[p, 64 + p] = 1.0
        jm[64 + p, p] = -1.0
    c["s5_jm"] = jm
    sidx = np.repeat(np.arange(8), 16)
    c["s5_mask"] = np.stack([(sidx[None, :] >= sidx[:, None]), (sidx[None, :] <= sidx[:, None])]).astype(np.float32)
    selin = np.zeros((128, 8, 8, 128), np.float32)
    for g in range(8):
        for s_ in range(8):
            for m_ in range(16):
                selin[g * 16 + m_, g, s_, s_ * 16 + m_] = 1.0
    c["s5_selin"] = selin
    c["s5_selout"] = np.ascontiguousarray(np.transpose(selin, (3, 2, 1, 0)))
    c["na_cm"] = np.ascontiguousarray(np.broadcast_to(np.concatenate([cm, cm], 0)[:, None, :], (128, 15, 64))).astype(np.float32)
    return c


def na_gather_rpb(rpb):
    wp = np.arange(64)[:, None]
    w = np.arange(64)[None, :]
    idx = np.clip(wp - w + 15, 0, 30)
    g = rpb[:, :, :, idx]
    g = np.transpose(g, (0, 1, 3, 2, 4))
    return np.ascontiguousarray(np.concatenate([g, g], axis=2)).astype(np.float32)


def na_blocks():
    out = []
    for j in range(64):
        rs = [min(max(2 * j + b - 4, 0), 120) for b in range(2)]
        lo, hi = min(rs), max(rs) + 7
        blocks = list(range(lo // 2, hi // 2 + 1))
        pat = []
        for i in blocks:
            for a in range(2):
                for b in range(2):
                    rho = 2 * i + a
                    r = 2 * j + b
                    ok = rs[b] <= rho <= rs[b] + 7
                    pat.append((rho - r + 7) if ok else None)
        out.append((blocks, tuple(pat)))
    return out


SCRATCH = {
    "XT": ([D, T], F32), "XM": ([D, T], F32), "H2": ([D, T], BF16),
    "ZF": ([T, 256], BF16), "VV": ([T, 512], BF16), "UT": ([256, T], BF16),
    "QT": ([512, T], BF16), "KT": ([512, T], BF16), "GT": ([3072, T], BF16),
    "FA": ([512, T], BF16), "SB": ([256, T], BF16), "NCT": ([512, T], BF16),
    "AF": ([64, 128 * 512], BF16), "DBG": ([128, 2048], F32), "YT": ([256, T], F32),
}

W_SHAPES = {
    "w_mod": [DEPTH, D, 6 * D], "b_mod": [DEPTH, 6 * D], "g_norm1": [DEPTH, D], "g_norm2": [DEPTH, D],
    "w_in": [DEPTH, D, DIN], "w_br_a": [DEPTH, 256, D], "w_br_b": [DEPTH, 256, D], "w_br_c": [DEPTH, 512, D],
    "w_out": [DEPTH, D, D], "w_ff1": [DEPTH, D, DFF], "w_ff2": [DEPTH, DFF, D], "g_final": [D],
    "rpbg": [DEPTH, 8, 128, 15, 64],
    "s5_lam_re": [DEPTH, 2, 16, 64], "s5_lam_im": [DEPTH, 2, 16, 64], "s5_log_dt": [DEPTH, 2, 16],
    "s5_b_re": [DEPTH, 2, 16, 64, 16], "s5_b_im": [DEPTH, 2, 16, 64, 16],
    "s5_c_re": [DEPTH, 2, 16, 16, 64], "s5_c_im": [DEPTH, 2, 16, 16, 64],
    "s5_d": [DEPTH, 16, 16], "s5_w_glu": [DEPTH, 256, 256],
}


def tiles_of(n, with_ctx=True):
    out = []
    if with_ctx:
        t = 0
        while t < LC:
            m = min(n, LC - t)
            out.append((t, m, 1))
            t += m
    t = LC
    while t < T:
        m = min(n, T - t)
        out.append((t, m, 0))
        t += m
    return out


def build(phases, kinds=None, layers=(0, 1)):
    kinds = kinds or {}
    nc = bass.Bass("TRN2", target_bir_lowering=False)
    I = {}
    I["x"] = nc.dram_tensor("x", [L, D], F32, kind="ExternalInput").ap()
    I["ctx"] = nc.dram_tensor("ctx", [LC, D], F32, kind="ExternalInput").ap()
    I["cvec"] = nc.dram_tensor("cvec", [2, D], F32, kind="ExternalInput").ap()
    for k, shp in W_SHAPES.items():
        I[k] = nc.dram_tensor(k, shp, F32, kind="ExternalInput").ap()
    for k, v in _consts().items():
        I[k] = nc.dram_tensor(k, list(v.shape), F32, kind="ExternalInput").ap()
    out = nc.dram_tensor("out", [L, D], F32, kind="ExternalOutput").ap()
    R = {}
    for k, (shp, dt) in SCRATCH.items():
        R[k] = nc.dram_tensor("r_" + k.lower(), shp, dt, kind=kinds.get(k, "Internal")).ap()

    with ExitStack() as st:
        S = Sched(nc, st)
        pbig = [st.enter_context(nc.psum_tensor("pbig%d" % i, [128, 1024], F32)) for i in range(4)]
        pbanks = [pbig[i // 2][:, (i % 2) * 512:(i % 2 + 1) * 512] for i in range(8)]
        pctr = [0]

        def bank():
            i = pctr[0] % 8
            pctr[0] += 1
            return pbanks[i], "pb%d" % i

        evc = [0]

        def evac_eng():
            evc[0] += 1
            return "dve" if evc[0] % 2 else "act"

        def copy_op(eng, out_ap, in_ap):
            if eng == "act":
                return lambda e: e.activation(out=out_ap, in_=in_ap, func=AF.Copy)
            return lambda e: e.tensor_copy(out=out_ap, in_=in_ap)

        uidc = [0]

        def sb(stack, name, shape, dt):
            uidc[0] += 1
            return stack.enter_context(nc.sbuf_tensor("%s_u%d" % (name, uidc[0]), shape, dt))

        ident = sb(st, "ident", [128, 128], F32)
        ones_bf = sb(st, "ones_bf", [128, 128], BF16)
        mod = sb(st, "mod", [128, DEPTH, 48, 2], F32)
        gsc = sb(st, "gsc", [128, DEPTH, 2, 8, 2], F32)
        gfin = sb(st, "gfin", [128, 8], F32)

        NCDMA = dict(allow_slow_non_contiguous=True)
        S.dma("sp", lambda e: e.dma_start(out=ident[:], in_=I["ident"]), writes=["ident"])
        S.op("pool", lambda e: e.memset(ones_bf[:], 1.0), writes=["ones"])

        def phase0():
            with ExitStack() as ps:
                cT = sb(ps, "cT", [128, 8, 2], F32)
                sT = sb(ps, "sT", [128, 8, 2], F32)
                bm = sb(ps, "bm", [128, DEPTH, 48], F32)
                gn = sb(ps, "gn", [128, DEPTH, 2, 8], F32)
                wm = [sb(ps, "wm%d" % i, [128, 8, 512], F32) for i in range(2)]
                for j in range(2):
                    S.dma("sp", lambda e, j=j: e.dma_start(out=cT[:, :, j], in_=I["cvec"][j].rearrange("(k p) -> p k", p=128), **NCDMA), writes=["cT"])
                for l in range(DEPTH):
                    S.dma("sp", lambda e, l=l: e.dma_start(out=bm[:, l, :], in_=I["b_mod"][l].rearrange("(j p) -> p j", p=128), **NCDMA), writes=["bm"])
                    S.dma("sp", lambda e, l=l: e.dma_start(out=gn[:, l, 0, :], in_=I["g_norm1"][l].rearrange("(k p) -> p k", p=128), **NCDMA), writes=["gn0"])
                    S.dma("sp", lambda e, l=l: e.dma_start(out=gn[:, l, 1, :], in_=I["g_norm2"][l].rearrange("(k p) -> p k", p=128), **NCDMA), writes=["gn1"])
                S.dma("sp", lambda e: e.dma_start(out=gfin[:], in_=I["g_final"].rearrange("(k p) -> p k", p=128), **NCDMA), writes=["gfin"])
                S.op("act", lambda e: e.activation(out=sT[:], in_=cT[:], func=AF.Silu), reads=["cT"], writes=["sT"])
                for l in range(DEPTH):
                    pb, pbn = bank()
                    wv = I["w_mod"][l].rearrange("(k p) n -> p k n", p=128)
                    for blk in range(12):
                        w = wm[blk % 2]
                        wn = "wm%d" % (blk % 2)
                        S.dma("sp", lambda e, w=w, blk=blk, wv=wv: e.dma_start(out=w[:], in_=wv[:, :, blk * 512:(blk + 1) * 512]), writes=[wn])
                        for jj in range(4):
                            j = blk * 4 + jj
                            for k in range(8):
                                S.op("pe", lambda e, w=w, jj=jj, j=j, k=k, pb=pb: e.matmul(
                                    pb[:, j * 2:j * 2 + 2], lhsT=w[:, k, jj * 128:(jj + 1) * 128], rhs=sT[:, k, :],
                                    start=(k == 0), stop=(k == 7)), reads=[wn, "sT"], writes=[pbn])
                    for v in range(2):
                        S.op("dve", lambda e, l=l, v=v, pb=pb: e.tensor_tensor(
                            out=mod[:, l, :, v], in0=pb[:, 0:96].rearrange("p (j v) -> p j v", v=2)[:, :, v], in1=bm[:, l, :], op=ALU.add),
                            reads=[pbn, "bm"], writes=["mod"])
                    for nn in range(2):
                        for v in range(2):
                            j0 = 8 + 24 * nn
                            S.op("dve", lambda e, l=l, v=v, nn=nn, j0=j0: e.scalar_tensor_tensor(
                                out=gsc[:, l, nn, :, v], in0=mod[:, l, j0:j0 + 8, v], scalar=1.0, in1=gn[:, l, nn, :],
                                op0=ALU.add, op1=ALU.mult), reads=["mod", "gn%d" % nn], writes=["gsc"])
                S.flush()

        def norm_mod(l, nn, v, xT, n, sq, rstd, tmpa, tmps, hT, xname, hname, uid):
            sh_j0 = 24 * nn
            S.op("act", lambda e: e.activation(out=sq[:, :, :n], in_=xT[:, :, :n], func=AF.Square), reads=[xname], writes=["sq"])
            pb, pbn = bank()
            for c in range(8):
                S.op("pe", lambda e, c=c, pb=pb: e.matmul(pb[:, :n], lhsT=ones_bf[:], rhs=sq[:, c, :n], start=(c == 0), stop=(c == 7)),
                     reads=["sq", "ones"], writes=[pbn])
            S.op("act", lambda e, pb=pb: e.activation(out=tmpa[:, :n], in_=pb[:, :n], func=AF.Sqrt, scale=1.0 / D, bias=epsb[:, 0:1]),
                 reads=[pbn, "epsb"], writes=["tmpa"])
            S.op("dve", lambda e: e.reciprocal(out=rstd[:, :n], in_=tmpa[:, :n]), reads=["tmpa"], writes=["rstd"])
            for c in range(8):
                tt = tmps[c % 2]
                tn = "tmps%d" % (c % 2)
                S.op("dve", lambda e, c=c, tt=tt: e.tensor_tensor(out=tt[:, :n], in0=xT[:, c, :n], in1=rstd[:, :n], op=ALU.mult),
                     reads=[xname, "rstd"], writes=[tn])
                S.op("act", lambda e, c=c, tt=tt: e.activation(out=hT[:, c, :n], in_=tt[:, :n], func=AF.Identity,
                                                              scale=gsc[:, l, nn, c, v:v + 1], bias=mod[:, l, sh_j0 + c, v:v + 1]),
                     reads=[tn, "gsc", "mod"], writes=[hname])

        epsb = sb(st, "epsb", [128, 1], F32)
        S.op("pool", lambda e: e.memset(epsb[:], EPS), writes=["epsb"])

        def phase1(l):
            NT = 512
            tl = tiles_of(NT)
            with ExitStack() as ps:
                w_sb = sb(ps, "w_in_sb", [128, 8, DIN], BF16)
                wv = I["w_in"][l].rearrange("(k p) n -> p k n", p=128)
                for j in range(10):
                    S.dma("pool", lambda e, j=j: e.dma_start(out=w_sb[:, :, j * 512:(j + 1) * 512], in_=wv[:, :, j * 512:(j + 1) * 512]),
                          writes=["w%d" % j])
                xtok = sb(ps, "xtok", [128, 4, D], F32)
                xTs = [sb(ps, "xT%d" % i, [128, 8, NT], F32) for i in range(2)]
                hTs = [sb(ps, "hT%d" % i, [128, 8, NT], BF16) for i in range(2)]
                sq = sb(ps, "sq", [128, 8, NT], BF16)
                rstd = sb(ps, "rstd", [128, NT], F32)
                tmpa = sb(ps, "tmpa", [128, NT], F32)
                tmps = [sb(ps, "tmps%d" % i, [128, NT], F32) for i in range(2)]
                stg = [sb(ps, "stg%d" % i, [128, 4, NT], BF16) for i in range(3)]
                sctr = [0]

                def stage():
                    i = sctr[0] % 3
                    sctr[0] += 1
                    return stg[i], "stg%d" % i

                XTv = R["XT"].rearrange("(c p) t -> p c t", p=128)

                def load(i):
                    t0, n, v = tl[i]
                    xT = xTs[i % 2]
                    xn = "xT%d" % (i % 2)
                    if l == 0:
                        src = I["ctx"] if v else I["x"]
                        r0 = t0 if v else t0 - LC
                        ns = n // 128
                        S.dma("sp", lambda e: e.dma_start(out=xtok[:, :ns, :], in_=src[r0:r0 + n, :].rearrange("(s p) d -> p s d", p=128)),
                              writes=["xtok"])
                        for s in range(ns):
                            for half in range(2):
                                pb, pbn = bank()
                                for cc in range(4):
                                    c = half * 4 + cc
                                    S.op("pe", lambda e, s=s, c=c, cc=cc, pb=pb: e.transpose(
                                        out=pb[:, cc * 128:(cc + 1) * 128], in_=xtok[:, s, c * 128:(c + 1) * 128], identity=ident[:]),
                                        reads=["xtok", "ident"], writes=[pbn])
                                eg = "dve"
                                S.op(eg, copy_op(eg, xT[:, half * 4:half * 4 + 4, s * 128:(s + 1) * 128],
                                                 pb[:, :].rearrange("p (c t) -> p c t", c=4)), reads=[pbn], writes=[xn])
                        S.dma("sp", lambda e: e.dma_start(out=XTv[:, :, t0:t0 + n], in_=xT[:, :, :n]), reads=[xn], writes=["XT%d" % i])
                    else:
                        S.dma("sp", lambda e: e.dma_start(out=xT[:, :, :n], in_=XTv[:, :, t0:t0 + n]), writes=[xn])

                def compute(i):
                    t0, n, v = tl[i]
                    xT = xTs[i % 2]
                    xn = "xT%d" % (i % 2)
                    hT = hTs[i % 2]
                    hn = "hT%d" % (i % 2)
                    norm_mod(l, 0, v, xT, n, sq, rstd, tmpa, tmps, hT, xn, hn, i)
                    ns = n // 128
                    for s in range(ns):
                        sg, sgn = stage()
                        pb, pbn = bank()
                        for c in range(8):
                            S.op("pe", lambda e, s=s, c=c, pb=pb: e.matmul(pb[:, 0:256], lhsT=hT[:, c, s * 128:(s + 1) * 128], rhs=w_sb[:, c, 0:256],
                                                                         start=(c == 0), stop=(c == 7)), reads=[hn, "w0"], writes=[pbn])
                        eg = evac_eng()
                        S.op(eg, copy_op(eg, sg[:, 0, 0:256], pb[:, 0:256]), reads=[pbn], writes=[sgn])
                        S.dma("sp", lambda e, s=s, sg=sg: e.dma_start(out=R["ZF"][t0 + s * 128:t0 + (s + 1) * 128, :], in_=sg[:, 0, 0:256]),
                              reads=[sgn], writes=["ZF"])
                        pb, pbn = bank()
                        for c in range(8):
                            S.op("pe", lambda e, s=s, c=c, pb=pb: e.matmul(pb[:, :], lhsT=hT[:, c, s * 128:(s + 1) * 128], rhs=w_sb[:, c, 1536:2048],
                                                                         start=(c == 0), stop=(c == 7)), reads=[hn, "w3"], writes=[pbn])
                        eg = evac_eng()
                        S.op(eg, copy_op(eg, sg[:, 1, :], pb[:, :]), reads=[pbn], writes=[sgn])
                        S.dma("sp", lambda e, s=s, sg=sg: e.dma_start(out=R["VV"][t0 + s * 128:t0 + (s + 1) * 128, :], in_=sg[:, 1, :]),
                              reads=[sgn], writes=["VV"])
                    groups = [("UT", 256, 2, False), ("QT", 512, 4, False), ("KT", 1024, 4, False)]
                    groups += [("GT%d" % g, 2048 + g * 512, 4, True) for g in range(6)]
                    for name, col0, nch, sig in groups:
                        sg, sgn = stage()
                        for q in range(nch):
                            pb, pbn = bank()
                            cs = col0 + q * 128
                            for c in range(8):
                                S.op("pe", lambda e, c=c, cs=cs, pb=pb: e.matmul(pb[:, :n], lhsT=w_sb[:, c, cs:cs + 128], rhs=hT[:, c, :n],
                                                                               start=(c == 0), stop=(c == 7)), reads=[hn, "w%d" % (cs // 512)], writes=[pbn])
                            if sig:
                                S.op("act", lambda e, q=q, pb=pb, sg=sg: e.activation(out=sg[:, q, :n], in_=pb[:, :n], func=AF.Sigmoid),
                                     reads=[pbn], writes=[sgn])
                            else:
                                eg = evac_eng()
                                S.op(eg, copy_op(eg, sg[:, q, :n], pb[:, :n]), reads=[pbn], writes=[sgn])
                        if sig:
                            g = int(name[2:])
                            dst = R["GT"].rearrange("(c p) t -> p c t", p=128)[:, g * 4:(g + 1) * 4, t0:t0 + n]
                        else:
                            dst = R[name].rearrange("(c p) t -> p c t", p=128)[:, :, t0:t0 + n]
                        S.dma("sp", lambda e, sg=sg, dst=dst, nch=nch: e.dma_start(out=dst, in_=sg[:, :nch, :n]), reads=[sgn], writes=[name])

                load(0)
                for i in range(len(tl)):
                    if i + 1 < len(tl):
                        load(i + 1)
                    compute(i)
                S.flush()


        def phase2_fnet(l):
            with ExitStack() as ps:
                tc = sb(ps, "fn_tc", [128, 64, 128], BF16)
                tsn = sb(ps, "fn_tsn", [128, 64, 128], BF16)
                w64 = sb(ps, "fn_w64", [64, 3, 64], BF16)
                zf = sb(ps, "zf", [128, 64, 256], BF16)
                xo = sb(ps, "xo", [128, 4, L], BF16)
                ablk = [sb(ps, "ablk%d" % i, [64, 16, 512], BF16) for i in range(2)]
                stg = [sb(ps, "fstg%d" % i, [128, 512], BF16) for i in range(3)]
                for q4 in range(4):
                    S.dma("pool", lambda e, q4=q4: e.dma_start(out=tc[:, q4 * 16:(q4 + 1) * 16, :], in_=I["fn_tc"][:, q4 * 16:(q4 + 1) * 16, :]), writes=["tc%d" % q4])
                    S.dma("pool", lambda e, q4=q4: e.dma_start(out=tsn[:, q4 * 16:(q4 + 1) * 16, :], in_=I["fn_tsn"][:, q4 * 16:(q4 + 1) * 16, :]), writes=["tsn%d" % q4])
                S.dma("pool", lambda e: e.dma_start(out=w64[:], in_=I["fn_w64"].rearrange("a p m -> p a m")), writes=["w64"])
                zsrc = R["ZF"][LC:T, :].rearrange("(r c) ch -> r c ch", c=64)
                for q4 in range(4):
                    S.dma("sp", lambda e, q4=q4: e.dma_start(out=zf[:, q4 * 16:(q4 + 1) * 16, :], in_=zsrc[:, q4 * 16:(q4 + 1) * 16, :]), writes=["zf%d" % q4])
                AFv = R["AF"].rearrange("c (k m) -> c k m", m=512)
                if l < DEPTH - 1:
                    c256 = sb(ps, "c256", [128, 2, 2, 256], BF16)
                    zc = sb(ps, "zc", [128, 2, 256], BF16)
                    xc = sb(ps, "xc", [128, 4, 256], BF16)
                    for a in range(2):
                        S.dma("pool", lambda e, a=a: e.dma_start(out=c256[:, a, :, :], in_=I["fn_c256"][a].rearrange("(tc p) k -> p tc k", p=128)), writes=["c256"])
                    S.dma("sp", lambda e: e.dma_start(out=zc[:], in_=R["ZF"][0:LC, :].rearrange("(a p) ch -> p a ch", p=128)), writes=["zc"])
                    for ri in range(2):
                        for q in range(2):
                            pb, pbn = bank()
                            for t2 in range(2):
                                S.op("pe", lambda e, ri=ri, q=q, t2=t2, pb=pb: e.matmul(pb[:, 0:256], lhsT=zc[:, t2, q * 128:(q + 1) * 128], rhs=c256[:, ri, t2, :],
                                                                                     start=(t2 == 0), stop=(t2 == 1)), reads=["zc", "c256"], writes=[pbn])
                            eg = evac_eng()
                            S.op(eg, copy_op(eg, xc[:, ri * 2 + q, :], pb[:, 0:256]), reads=[pbn], writes=["xc"])
                    S.dma("sp", lambda e: e.dma_start(out=R["FA"].rearrange("(a p) t -> p a t", p=128)[:, :, 0:LC], in_=xc[:]), reads=["xc"], writes=["FAc"])
                for c in range(64):
                    pb, pbn = bank()
                    S.op("pe", lambda e, c=c, pb=pb: e.matmul(pb[:, 0:256], lhsT=tc[:, c, :], rhs=zf[:, c, :], start=True, stop=True),
                         reads=["tc%d" % (c // 16), "zf%d" % (c // 16)], writes=[pbn])
                    S.op("pe", lambda e, c=c, pb=pb: e.matmul(pb[:, 256:512], lhsT=tsn[:, c, :], rhs=zf[:, c, :], start=True, stop=True),
                         reads=["tsn%d" % (c // 16), "zf%d" % (c // 16)], writes=[pbn])
                    sg = stg[c % 3]
                    sgn = "fstg%d" % (c % 3)
                    eg = evac_eng()
                    S.op(eg, copy_op(eg, sg[:, :], pb[:, :]), reads=[pbn], writes=[sgn])
                    S.dma("sp", lambda e, c=c, sg=sg: e.dma_start(out=AFv[c], in_=sg[:, :]), reads=[sgn], writes=["AF"])
                xov = xo[:, :, :].rearrange("p a (k2 k1) -> p a k2 k1", k1=128)
                for kb in range(8):
                    ab = ablk[kb % 2]
                    abn = "ablk%d" % (kb % 2)
                    S.dma("sp", lambda e, kb=kb, ab=ab: e.dma_start(out=ab[:], in_=AFv[:, kb * 16:(kb + 1) * 16, :]), reads=["AF"], writes=[abn])
                    for k1l in range(16):
                        k1 = kb * 16 + k1l
                        pb, pbn = bank()
                        for q in range(2):
                            ar = ab[:, k1l, q * 128:(q + 1) * 128]
                            ai = ab[:, k1l, 256 + q * 128:256 + (q + 1) * 128]
                            o_r = pb[:, q * 64:(q + 1) * 64]
                            o_i = pb[:, (2 + q) * 64:(3 + q) * 64]
                            S.op("pe", lambda e, ar=ar, o_r=o_r: e.matmul(o_r, lhsT=ar, rhs=w64[:, 0, :], start=True, stop=False), reads=[abn, "w64"], writes=[pbn])
                            S.op("pe", lambda e, ai=ai, o_r=o_r: e.matmul(o_r, lhsT=ai, rhs=w64[:, 1, :], start=False, stop=True), reads=[abn, "w64"], writes=[pbn])
                            S.op("pe", lambda e, ai=ai, o_i=o_i: e.matmul(o_i, lhsT=ai, rhs=w64[:, 0, :], start=True, stop=False), reads=[abn, "w64"], writes=[pbn])
                            S.op("pe", lambda e, ar=ar, o_i=o_i: e.matmul(o_i, lhsT=ar, rhs=w64[:, 2, :], start=False, stop=True), reads=[abn, "w64"], writes=[pbn])
                        eg = "dve"
                        S.op(eg, copy_op(eg, xov[:, :, :, k1], pb[:, 0:256].rearrange("p (a k) -> p a k", a=4)), reads=[pbn], writes=["xo"])
                FAl = R["FA"].rearrange("(a p) t -> p a t", p=128)
                for a in range(4):
                    S.dma("sp", lambda e, a=a: e.dma_start(out=FAl[:, a, LC:T], in_=xo[:, a, :]), reads=["xo"], writes=["FA%d" % a])
                S.flush()

        def phase2_na(l):
            blocks = na_blocks()
            pats = {}
            for blks, pat in blocks:
                if pat not in pats:
                    pats[pat] = (len(pats), len(blks))
            npat = len(pats)
            with_ctx_q = (l < DEPTH - 1)
            scale = 0.125
            with ExitStack() as ps:
                cm = sb(ps, "na_cm", [128, 15, 64], F32)
                S.dma("sp", lambda e: e.dma_start(out=cm[:], in_=I["na_cm"]), writes=["cm"])
                gt = sb(ps, "na_g", [128, 15, 64], F32)
                cb = sb(ps, "na_cb", [128, 16, 64], F32)
                bias = sb(ps, "na_bias", [128, npat, 640], F32)
                kTs = [sb(ps, "kT%d" % i, [64, T], BF16) for i in range(2)]
                qTs = [sb(ps, "qT%d" % i, [64, T], BF16) for i in range(2)]
                v1s = [sb(ps, "v1_%d" % i, [128, 66, 65], BF16) for i in range(2)]
                ncs = [sb(ps, "ncT%d" % i, [64, T], BF16) for i in range(2)]
                scb = [sb(ps, "scb%d" % i, [128, 640], F32) for i in range(2)]
                PTs = [sb(ps, "PT%d" % i, [128, 7, 128], BF16) for i in range(2)]
                otk = [sb(ps, "otk%d" % i, [128, 64], F32) for i in range(2)]
                rec = [sb(ps, "rec%d" % i, [128, 1], F32) for i in range(2)]
                for i in range(2):
                    S.op("pool", lambda e, i=i: e.memset(v1s[i][:, :, 64:65], 1.0), writes=["v1o_%d" % i])
                S.op("pool", lambda e: e.memset(cb[:, 15, :], NEG), writes=["cbneg"])
                VVv = R["VV"].rearrange("(blk p) d -> p blk d", p=128)

                def load_head(h):
                    hb = h % 2
                    S.dma("sp", lambda e: e.dma_start(out=kTs[hb][:], in_=R["KT"][h * 64:(h + 1) * 64, :]), writes=["kT%d" % hb])
                    S.dma("sp", lambda e: e.dma_start(out=qTs[hb][:], in_=R["QT"][h * 64:(h + 1) * 64, :]), writes=["qT%d" % hb])
                    for part in range(3):
                        S.dma("sp", lambda e, part=part: e.dma_start(out=v1s[hb][:, part * 22:(part + 1) * 22, 0:64],
                                                                    in_=VVv[:, part * 22:(part + 1) * 22, h * 64:(h + 1) * 64]),
                              reads=["v1o_%d" % hb], writes=["v1_%d_%d" % (hb, part)])

                def head(h):
                    hb = h % 2
                    kT, qT, v1, ncT = kTs[hb], qTs[hb], v1s[hb], ncs[hb]
                    kn, qn, ncn = "kT%d" % hb, "qT%d" % hb, "ncT%d" % hb
                    vn = ["v1_%d_%d" % (hb, p) for p in range(3)]
                    S.dma("sp", lambda e: e.dma_start(out=gt[:], in_=I["rpbg"][l, h]), writes=["na_g"])
                    S.op("dve", lambda e: e.tensor_tensor(out=cb[:, 0:15, :], in0=gt[:], in1=cm[:], op=ALU.add), reads=["na_g", "cm"], writes=["cb"])
                    for pat, (pi, nb) in pats.items():
                        k = 0
                        for s_ in range(nb):
                            for a in range(2):
                                for b in range(2):
                                    d = pat[k]
                                    k += 1
                                    row = 15 if d is None else d
                                    eng = "dve" if (k % 2) else "pool"
                                    S.op(eng, lambda e, a=a, b=b, s_=s_, row=row, pi=pi: e.tensor_copy(
                                        out=bias[a * 64:(a + 1) * 64, pi, s_ * 128 + b * 64:s_ * 128 + (b + 1) * 64], in_=cb[a * 64:(a + 1) * 64, row, :]),
                                        reads=["cb", "cbneg"], writes=["bias"])
                    qblocks = ([0, 1] if with_ctx_q else []) + list(range(2, 66))
                    for qi, tq in enumerate(qblocks):
                        jb = qi % 2
                        if tq >= 2:
                            blks, pat = blocks[tq - 2]
                            pi, nb = pats[pat]
                        else:
                            blks, nb = [], 0
                        qs = qT[:, tq * 128:(tq + 1) * 128]
                        big = pbig[jb]
                        bign = ["pb%d" % (2 * jb), "pb%d" % (2 * jb + 1)]
                        for s_, i in enumerate(blks):
                            tk = 2 + i
                            S.op("pe", lambda e, s_=s_, tk=tk, big=big, qs=qs: e.matmul(big[:, s_ * 128:(s_ + 1) * 128], lhsT=kT[:, tk * 128:(tk + 1) * 128], rhs=qs,
                                                                                      start=True, stop=True), reads=[kn, qn], writes=bign)
                        psc = pbanks[4 + jb]
                        pscn = "pb%d" % (4 + jb)
                        for s_ in range(2):
                            S.op("pe", lambda e, s_=s_, psc=psc, qs=qs: e.matmul(psc[:, s_ * 128:(s_ + 1) * 128], lhsT=kT[:, s_ * 128:(s_ + 1) * 128], rhs=qs,
                                                                               start=True, stop=True), reads=[kn, qn], writes=[pscn])
                        PT = PTs[jb]
                        ptn = "PT%d" % jb
                        scbj, recj, otkj = scb[jb], rec[jb], otk[jb]
                        if nb:
                            S.op("dve", lambda e, big=big, nb=nb, pi=pi, scbj=scbj: e.scalar_tensor_tensor(out=scbj[:, :nb * 128], in0=big[:, :nb * 128], scalar=scale,
                                                                                                          in1=bias[:, pi, :nb * 128], op0=ALU.mult, op1=ALU.add),
                                 reads=bign + ["bias"], writes=["scb%d" % jb])
                            S.op("act", lambda e, nb=nb, PT=PT, scbj=scbj: e.activation(out=PT[:, 0:nb, :], in_=scbj[:, :nb * 128].rearrange("p (s q) -> p s q", q=128), func=AF.Exp),
                                 reads=["scb%d" % jb], writes=[ptn])
                        S.op("act", lambda e, psc=psc, PT=PT: e.activation(out=PT[:, 5:7, :], in_=psc[:, 0:256].rearrange("p (s q) -> p s q", q=128), func=AF.Exp, scale=scale),
                             reads=[pscn], writes=[ptn])
                        pso = pbanks[6 + jb]
                        pson = "pb%d" % (6 + jb)
                        klist = [(s_, 2 + i) for s_, i in enumerate(blks)] + [(5, 0), (6, 1)]
                        for n_, (slot, tk) in enumerate(klist):
                            first, lastk = (n_ == 0), (n_ == len(klist) - 1)
                            S.op("pe", lambda e, slot=slot, tk=tk, first=first, lastk=lastk, pso=pso, PT=PT: e.matmul(pso[:, 0:65], lhsT=PT[:, slot, :], rhs=v1[:, tk, :],
                                                                                                                    start=first, stop=lastk),
                                 reads=[ptn, vn[tk // 22], "v1o_%d" % hb], writes=[pson])
                        if DEBUG and h == 0 and qi == DEBUG - 1:
                            dbt = sb(ps, "dbt", [128, 2048], F32)
                            S.op("dve", lambda e, psc=psc: e.tensor_copy(out=dbt[:, 0:256], in_=psc[:, 0:256]), reads=[pscn], writes=["dbt"])
                            S.op("dve", lambda e, PT=PT: e.tensor_copy(out=dbt[:, 256:256 + 896], in_=PT[:, :, :].rearrange("p s q -> p (s q)")), reads=[ptn], writes=["dbt"])
                            S.op("dve", lambda e, pso=pso: e.tensor_copy(out=dbt[:, 1152:1152 + 65], in_=pso[:, 0:65]), reads=[pson], writes=["dbt"])
                            S.op("dve", lambda e, big=big: e.tensor_copy(out=dbt[:, 1280:1280 + 640], in_=big[:, 0:640]), reads=bign, writes=["dbt"])
                            S.dma("sp", lambda e: e.dma_start(out=R["DBG"], in_=dbt[:]), reads=["dbt"], writes=["DBG"])
                        S.op("dve", lambda e, pso=pso, recj=recj: e.reciprocal(out=recj[:], in_=pso[:, 64:65]), reads=[pson], writes=["rec%d" % jb])
                        S.op("dve", lambda e, pso=pso, recj=recj, otkj=otkj: e.tensor_scalar(out=otkj[:], in0=pso[:, 0:64], scalar1=recj[:, 0:1], scalar2=None, op0=ALU.mult),
                             reads=[pson, "rec%d" % jb], writes=["otk%d" % jb])
                        S.op("pe", lambda e, pso=pso, otkj=otkj: e.transpose(out=pso[0:64, 128:256], in_=otkj[:], identity=ident[:]), reads=["otk%d" % jb], writes=[pson])
                        eg = "act"
                        S.op(eg, copy_op(eg, ncT[:, tq * 128:(tq + 1) * 128], pso[0:64, 128:256]), reads=[pson], writes=[ncn])
                    t_lo = 0 if with_ctx_q else LC
                    S.dma("sp", lambda e: e.dma_start(out=R["NCT"][h * 64:(h + 1) * 64, t_lo:T], in_=ncT[:, t_lo:T]), reads=[ncn], writes=["NCT%d" % h])

                load_head(0)
                for h in range(8):
                    if h + 1 < 8:
                        load_head(h + 1)
                    head(h)
                S.flush()


        def phase2_s5(l):
            Q = T // 8
            PCS = [(0, 352), (352, 704), (704, 1056)]
            TWO_PI = float(2 * np.pi)
            MAGIC = 12582912.0
            with ExitStack() as po:
                bblk = sb(po, "bblk", [128, 32, 128], BF16)
                cblk = sb(po, "cblk", [128, 32, 128], BF16)
                dsum = sb(po, "dsum", [128, 16, 128], BF16)
                MU = sb(po, "MU", [128, 11, 2, 32], F32)
                jm = sb(po, "jm", [128, 128], F32)
                S.dma("sp", lambda e: e.dma_start(out=jm[:], in_=I["s5_jm"]), writes=["jm"])
                with ExitStack() as ps:
                    def t32(name):
                        return sb(ps, name, [128, 32], F32)
                    lr, li, ldt, dtv, x1, mag, ang = [t32(n) for n in ("lr", "li", "ldt", "dtv", "x1", "mag", "ang")]
                    s0, tq, nn_, red, sinv, cosv = [t32(n) for n in ("s0", "tq", "nn", "red", "sinv", "cosv")]
                    am1, den, rden, wr, wi, u1, u2, u3, u4 = [t32(n) for n in ("am1", "den", "rden", "wr", "wi", "u1", "u2", "u3", "u4")]
                    PW = sb(ps, "PW", [128, 9, 2, 32], F32)
                    IPW = sb(ps, "IPW", [128, 8, 2, 32], F32)
                    Y1 = sb(ps, "Y1", [128, 32, 16], F32)
                    Y2 = sb(ps, "Y2", [128, 32, 16], F32)
                    BB1 = sb(ps, "BB1", [128, 32, 16], F32)
                    BB2 = sb(ps, "BB2", [128, 32, 16], F32)
                    CC1 = sb(ps, "CC1", [128, 4, 128], F32)
                    CC2 = sb(ps, "CC2", [128, 4, 128], F32)
                    X1 = sb(ps, "X1", [128, 32, 16], F32)
                    X2 = sb(ps, "X2", [128, 32, 16], F32)
                    e1 = sb(ps, "e1", [128, 32, 16], F32)
                    e2 = sb(ps, "e2", [128, 32, 16], F32)
                    EBt = sb(ps, "EBt", [128, 32, 8, 16], F32)
                    ENt = sb(ps, "ENt", [128, 32, 8, 16], F32)
                    GPt = sb(ps, "GPt", [128, 32, 8, 16], F32)
                    CBt = sb(ps, "CBt", [128, 32, 8, 16], F32)
                    Dm = sb(ps, "Dm", [128, 32, 128], F32)
                    msk = sb(ps, "msk", [128, 2, 128], F32)
                    dsk = sb(ps, "dsk", [128, 16], F32)
                    for hf in range(2):
                        hs = slice(hf * 64, (hf + 1) * 64)
                        S.dma("sp", lambda e, hs=hs: e.dma_start(out=lr[hs, :], in_=I["s5_lam_re"][l].rearrange("d g p -> p (d g)"), **NCDMA), writes=["lr"])
                        S.dma("sp", lambda e, hs=hs: e.dma_start(out=li[hs, :], in_=I["s5_lam_im"][l].rearrange("d g p -> p (d g)"), **NCDMA), writes=["li"])
                    S.dma("sp", lambda e: e.dma_start(out=ldt[:], in_=I["s5_log_dt"][l].rearrange("d g -> (d g)").partition_broadcast(128)), writes=["ldt"])
                    bre = I["s5_b_re"][l].rearrange("d g p m -> p (d g) m")
                    bim = I["s5_b_im"][l].rearrange("d g p m -> p (d g) m")
                    S.dma("sp", lambda e: e.dma_start(out=Y1[0:64], in_=bre), writes=["Y1"])
                    S.dma("sp", lambda e: e.dma_start(out=Y1[64:128], in_=bim), writes=["Y1"])
                    S.dma("sp", lambda e: e.dma_start(out=Y2[0:64], in_=bim), writes=["Y2"])
                    S.dma("sp", lambda e: e.dma_start(out=Y2[64:128], in_=bre), writes=["Y2"])
                    cre = I["s5_c_re"][l].rearrange("d g n p -> (d g n) p")
                    cim = I["s5_c_im"][l].rearrange("d g n p -> (d g n) p")
                    for ch in range(4):
                        rs_ = slice(ch * 128, (ch + 1) * 128)
                        S.dma("sp", lambda e, ch=ch, rs_=rs_: e.dma_start(out=CC1[:, ch, 0:64], in_=cre[rs_, :]), writes=["CC1"])
                        S.dma("sp", lambda e, ch=ch, rs_=rs_: e.dma_start(out=CC1[:, ch, 64:128], in_=cim[rs_, :]), writes=["CC1"])
                        S.dma("sp", lambda e, ch=ch, rs_=rs_: e.dma_start(out=CC2[:, ch, 0:64], in_=cim[rs_, :]), writes=["CC2"])
                        S.dma("sp", lambda e, ch=ch, rs_=rs_: e.dma_start(out=CC2[:, ch, 64:128], in_=cre[rs_, :]), writes=["CC2"])
                    S.dma("sp", lambda e: e.dma_start(out=msk[:], in_=I["s5_mask"].rearrange("d p m -> p d m")), writes=["msk"])
                    for s_ in range(8):
                        S.dma("sp", lambda e, s_=s_: e.dma_start(out=dsk[s_ * 16:(s_ + 1) * 16, :], in_=I["s5_d"][l].rearrange("g m -> m g"), **NCDMA), writes=["dsk"])

                    if CUT < 10:
                        S.flush()
                        return
                    def V(fn, reads, writes, eng="dve"):
                        S.op(eng, fn, reads=reads, writes=writes)

                    def tt(o, a, b, op, on, an, bn):
                        V(lambda e: e.tensor_tensor(out=o, in0=a, in1=b, op=op), [an, bn], [on])

                    def ts(o, a, s1, op0, on, an, s2=None, op1=None):
                        if op1 is None:
                            V(lambda e: e.tensor_scalar(out=o, in0=a, scalar1=s1, scalar2=None, op0=op0), [an], [on])
                        else:
                            V(lambda e: e.tensor_scalar(out=o, in0=a, scalar1=s1, scalar2=s2, op0=op0, op1=op1), [an], [on])

                    V(lambda e: e.tensor_scalar(out=Y2[0:64], in0=Y2[0:64], scalar1=-1.0, scalar2=None, op0=ALU.mult), ["Y2"], ["Y2"])
                    ts(lr[:], lr[:], -1e-4, ALU.min, "lr", "lr")
                    V(lambda e: e.activation(out=dtv[:], in_=ldt[:], func=AF.Exp), ["ldt"], ["dtv"], eng="act")
                    tt(x1[:], lr[:], dtv[:], ALU.mult, "x1", "lr", "dtv")
                    V(lambda e: e.activation(out=mag[:], in_=x1[:], func=AF.Exp), ["x1"], ["mag"], eng="act")
                    tt(ang[:], li[:], dtv[:], ALU.mult, "ang", "li", "dtv")

                    def sinred(dst, dn, shift):
                        ts(s0[:], ang[:], shift, ALU.add, "s0", "ang")
                        ts(tq[:], s0[:], 1.0 / TWO_PI, ALU.mult, "tq", "s0")
                        ts(nn_[:], tq[:], MAGIC, ALU.add, "nn", "tq")
                        ts(tq[:], nn_[:], -MAGIC, ALU.add, "tq", "nn")
                        V(lambda e: e.scalar_tensor_tensor(out=red[:], in0=tq[:], scalar=-TWO_PI, in1=s0[:], op0=ALU.mult, op1=ALU.add), ["tq", "s0"], ["red"])
                        ts(red[:], red[:], -3.1415925, ALU.max, "red", "red", 3.1415925, ALU.min)
                        V(lambda e: e.activation(out=dst, in_=red[:], func=AF.Sin), ["red"], [dn], eng="act")

                    sinred(sinv[:], "sinv", 0.0)
                    sinred(cosv[:], "cosv", float(np.pi / 2))
                    A1 = PW[:, 1, 0, :]
                    B1 = PW[:, 1, 1, :]
                    tt(A1, mag[:], cosv[:], ALU.mult, "PW", "mag", "cosv")
                    tt(B1, mag[:], sinv[:], ALU.mult, "PW", "mag", "sinv")
                    V(lambda e: e.memset(PW[:, 0, 0, :], 1.0), [], ["PW"], eng="pool")
                    V(lambda e: e.memset(PW[:, 0, 1, :], 0.0), [], ["PW"], eng="pool")
                    V(lambda e: e.memset(IPW[:, 0, 0, :], 1.0), [], ["IPW"], eng="pool")
                    V(lambda e: e.memset(IPW[:, 0, 1, :], 0.0), [], ["IPW"], eng="pool")
                    ts(am1[:], A1, -1.0, ALU.add, "am1", "PW")
                    tt(u1[:], lr[:], lr[:], ALU.mult, "u1", "lr", "lr")
                    tt(u2[:], li[:], li[:], ALU.mult, "u2", "li", "li")
                    tt(den[:], u1[:], u2[:], ALU.add, "den", "u1", "u2")
                    V(lambda e: e.reciprocal(out=rden[:], in_=den[:]), ["den"], ["rden"])
                    tt(u1[:], am1[:], lr[:], ALU.mult, "u1", "am1", "lr")
                    tt(u2[:], B1, li[:], ALU.mult, "u2", "PW", "li")
                    tt(u3[:], u1[:], u2[:], ALU.add, "u3", "u1", "u2")
                    tt(wr[:], u3[:], rden[:], ALU.mult, "wr", "u3", "rden")
                    tt(u1[:], B1, lr[:], ALU.mult, "u1", "PW", "lr")
                    tt(u2[:], am1[:], li[:], ALU.mult, "u2", "am1", "li")
                    tt(u3[:], u1[:], u2[:], ALU.subtract, "u3", "u1", "u2")
                    tt(wi[:], u3[:], rden[:], ALU.mult, "wi", "u3", "rden")

                    def bc(t):
                        return t.unsqueeze(2).broadcast_to([128, 32, 16])

                    tt(e1[:], Y1[:], bc(wr[:, :]), ALU.mult, "e1", "Y1", "wr")
                    tt(e2[:], Y2[:], bc(wi[:, :]), ALU.mult, "e2", "Y2", "wi")
                    tt(BB1[:], e1[:], e2[:], ALU.add, "BB1", "e1", "e2")
                    tt(e1[:], Y2[:], bc(wr[:, :]), ALU.mult, "e1", "Y2", "wr")
                    tt(e2[:], Y1[:], bc(wi[:, :]), ALU.mult, "e2", "Y1", "wi")
                    tt(BB2[:], e1[:], e2[:], ALU.subtract, "BB2", "e1", "e2")

                    def cmul(oa, ob, a, b, c_, d_, on, rn):
                        tt(u1[:], a, c_, ALU.mult, "u1", rn[0], rn[1])
                        tt(u2[:], b, d_, ALU.mult, "u2", rn[0], rn[1])
                        tt(u3[:], a, d_, ALU.mult, "u3", rn[0], rn[1])
                        tt(u4[:], b, c_, ALU.mult, "u4", rn[0], rn[1])
                        tt(oa, u1[:], u2[:], ALU.subtract, on, "u1", "u2")
                        tt(ob, u3[:], u4[:], ALU.add, on, "u3", "u4")

                    for k in range(1, 8):
                        cmul(PW[:, k + 1, 0, :], PW[:, k + 1, 1, :], PW[:, k, 0, :], PW[:, k, 1, :], A1, B1, "PW", ("PW", "PW"))
                    tt(u1[:], A1, A1, ALU.mult, "u1", "PW", "PW")
                    tt(u2[:], B1, B1, ALU.mult, "u2", "PW", "PW")
                    tt(den[:], u1[:], u2[:], ALU.add, "den", "u1", "u2")
                    V(lambda e: e.reciprocal(out=rden[:], in_=den[:]), ["den"], ["rden"])
                    tt(IPW[:, 1, 0, :], A1, rden[:], ALU.mult, "IPW", "PW", "rden")
                    V(lambda e: e.scalar_tensor_tensor(out=IPW[:, 1, 1, :], in0=B1, scalar=-1.0, in1=rden[:], op0=ALU.mult, op1=ALU.mult), ["PW", "rden"], ["IPW"])
                    for k in range(1, 7):
                        cmul(IPW[:, k + 1, 0, :], IPW[:, k + 1, 1, :], IPW[:, k, 0, :], IPW[:, k, 1, :], IPW[:, 1, 0, :], IPW[:, 1, 1, :], "IPW", ("IPW", "IPW"))
                    V(lambda e: e.tensor_copy(out=MU[:, 0, :, :], in_=PW[:, 8, :, :]), ["PW"], ["MU"])
                    for j in range(10):
                        cmul(MU[:, j + 1, 0, :], MU[:, j + 1, 1, :], MU[:, j, 0, :], MU[:, j, 1, :], MU[:, j, 0, :], MU[:, j, 1, :], "MU", ("MU", "MU"))
                    if CUT < 11:
                        S.flush()
                        return
                    for (CC, XX, ccn, xn) in ((CC1, X1, "CC1", "X1"), (CC2, X2, "CC2", "X2")):
                        for ch in range(4):
                            pb, pbn = bank()
                            S.op("pe", lambda e, CC=CC, ch=ch, pb=pb: e.transpose(out=pb[:, 0:128], in_=CC[:, ch, :], identity=ident[:]), reads=[ccn], writes=[pbn])
                            V(lambda e, XX=XX, ch=ch, pb=pb: e.tensor_copy(out=XX[:, ch * 8:(ch + 1) * 8, :], in_=pb[:, 0:128].rearrange("p (a n) -> p a n", n=16)), [pbn], [xn])
                    V(lambda e: e.tensor_scalar(out=X1[64:128], in0=X1[64:128], scalar1=-1.0, scalar2=None, op0=ALU.mult), ["X1"], ["X1"])
                    V(lambda e: e.tensor_scalar(out=X2[:], in0=X2[:], scalar1=-1.0, scalar2=None, op0=ALU.mult), ["X2"], ["X2"])

                    def table(dst, dn, M1, M2, m1n, m2n, pa, pb_, pn, sig):
                        tt(e1[:], M1[:], bc(pa), ALU.mult, "e1", m1n, pn)
                        tt(e2[:], M2[:], bc(pb_), ALU.mult, "e2", m2n, pn)
                        tt(dst[:, 0:16, sig, :], e1[:, 0:16, :], e2[:, 0:16, :], ALU.add, dn, "e1", "e2")
                        tt(dst[:, 16:32, 7 - sig, :], e1[:, 16:32, :], e2[:, 16:32, :], ALU.add, dn, "e1", "e2")

                    if CUT < 12:
                        S.flush()
                        return
                    for sig in range(8):
                        table(EBt, "EBt", BB1, BB2, "BB1", "BB2", PW[:, 7 - sig, 0, :], PW[:, 7 - sig, 1, :], "PW", sig)
                        table(ENt, "ENt", BB1, BB2, "BB1", "BB2", IPW[:, sig, 0, :], IPW[:, sig, 1, :], "IPW", sig)
                        table(GPt, "GPt", X1, X2, "X1", "X2", PW[:, sig, 0, :], PW[:, sig, 1, :], "PW", sig)
                        table(CBt, "CBt", X1, X2, "X1", "X2", PW[:, sig + 1, 0, :], PW[:, sig + 1, 1, :], "PW", sig)
                    if CUT < 13:
                        S.flush()
                        return
                    if DEBUG == 3:
                        dbt = sb(ps, "dbt3", [128, 2048], F32)
                        S.op("pool", lambda e: e.memset(dbt[:], 0.0), writes=["dbt"])
                        srcs = [X1[:, 0:8, :].rearrange("p a n -> p (a n)"), CBt[:, 0, :, :].rearrange("p s n -> p (s n)"), GPt[:, 0, :, :].rearrange("p s n -> p (s n)"),
                                BB1[:, 0:8, :].rearrange("p a n -> p (a n)"), EBt[:, 0, :, :].rearrange("p s n -> p (s n)"), Y1[:, 0:8, :].rearrange("p a n -> p (a n)"),
                                X2[:, 0:8, :].rearrange("p a n -> p (a n)"), e1[:, 0:8, :].rearrange("p a n -> p (a n)")]
                        for i_, src_ in enumerate(srcs):
                            S.op("dve", lambda e, i_=i_, src_=src_: e.tensor_copy(out=dbt[:, i_ * 128:(i_ + 1) * 128], in_=src_), reads=["dbt", "X1", "X2", "CBt", "GPt", "BB1", "EBt", "Y1", "e1"], writes=["dbt"])
                        S.op("dve", lambda e: e.tensor_copy(out=dbt[:, 1024:1024 + 576], in_=PW[:].rearrange("p a b c -> p (a b c)")), reads=["dbt", "PW"], writes=["dbt"])
                        S.dma("sp", lambda e: e.dma_start(out=R["DBG"], in_=dbt[:]), reads=["dbt"], writes=["DBG"])
                        S.flush()
                        return
                    V(lambda e: e.tensor_copy(out=cblk[:].rearrange("p a b -> p (a b)"), in_=CBt[:].rearrange("p a s n -> p (a s n)")), ["CBt"], ["cblk"])
                    if CUT < 15:
                        S.flush()
                        return
                    for dg in range(32):
                        pb, pbn = bank()
                        S.op("pe", lambda e, dg=dg, pb=pb: e.transpose(out=pb[:, 0:128], in_=EBt[:, dg, :, :].rearrange("p s m -> p (s m)"), identity=ident[:]),
                             reads=["EBt"], writes=[pbn])
                        eg = evac_eng()
                        S.op(eg, copy_op(eg, bblk[:, dg, :], pb[:, 0:128]), reads=[pbn], writes=["bblk"])
                        if CUT < 16:
                            continue
                        pb2, pbn2 = bank()
                        S.op("pe", lambda e, dg=dg, pb2=pb2: e.matmul(pb2[:, 0:128], lhsT=ENt[:, dg, :, :].rearrange("p s m -> p (s m)"),
                                                                   rhs=GPt[:, dg, :, :].rearrange("p s m -> p (s m)"), start=True, stop=True),
                             reads=["ENt", "GPt"], writes=[pbn2])
                        V(lambda e, dg=dg, pb2=pb2: e.tensor_tensor(out=Dm[:, dg, :], in0=pb2[:, 0:128], in1=msk[:, dg // 16, :], op=ALU.mult), [pbn2, "msk"], ["Dm"])
                    for g in range(16 if CUT >= 17 else 0):
                        V(lambda e, g=g: e.tensor_tensor(out=Dm[:, g, :], in0=Dm[:, g, :], in1=Dm[:, 16 + g, :], op=ALU.add), ["Dm"], ["Dm"])
                        V(lambda e, g=g: e.scalar_tensor_tensor(out=dsum[:, g, :], in0=ident[:], scalar=dsk[:, g:g + 1], in1=Dm[:, g, :], op0=ALU.mult, op1=ALU.add),
                          ["Dm", "dsk"], ["dsum"])
                    S.flush()

                if CUT < 18:
                    return
                if DEBUG == 2:
                    with ExitStack() as pd:
                        dbt = sb(pd, "dbt2", [128, 2048], F32)
                        S.op("pool", lambda e: e.memset(dbt[:], 0.0), writes=["dbt"])
                        for i_, src_ in enumerate([dsum[:, 0, :], bblk[:, 0, :], cblk[:, 0, :], bblk[:, 16, :], cblk[:, 16, :], dsum[:, 5, :]]):
                            S.op("dve", lambda e, i_=i_, src_=src_: e.tensor_copy(out=dbt[:, i_ * 128:(i_ + 1) * 128], in_=src_), reads=["dbt"], writes=["dbt"])
                        S.op("dve", lambda e: e.tensor_copy(out=dbt[:, 768:768 + 704], in_=MU[:].rearrange("p a b c -> p (a b c)")), reads=["dbt"], writes=["dbt"])
                        S.dma("sp", lambda e: e.dma_start(out=R["DBG"], in_=dbt[:]), reads=["dbt"], writes=["DBG"])
                        S.flush()
                with ExitStack() as ps:
                    selin = sb(ps, "selin", [128, 8, 8, 128], BF16)
                    selout = sb(ps, "selout", [128, 8, 8, 128], BF16)
                    for g in range(8):
                        S.dma("pool", lambda e, g=g: e.dma_start(out=selin[:, g, :, :], in_=I["s5_selin"][:, g, :, :]), writes=["selin"])
                        S.dma("pool", lambda e, g=g: e.dma_start(out=selout[:, g, :, :], in_=I["s5_selout"][:, g, :, :]), writes=["selout"])
                    wglu = sb(ps, "wglu", [128, 2, 256], BF16)
                    S.dma("pool", lambda e: e.dma_start(out=wglu[:], in_=I["s5_w_glu"][l].rearrange("(k p) n -> p k n", p=128)), writes=["wglu"])
                    uT = sb(ps, "uT", [128, T], BF16)
                    Vg = sb(ps, "Vg", [128, Q], BF16)
                    VRg = sb(ps, "VRg", [128, Q], BF16)
                    Hm = [sb(ps, "Hm%d" % d, [128, Q], F32) for d in range(2)]
                    Hs = [sb(ps, "Hs%d" % d, [128, Q + 2], BF16) for d in range(2)]
                    HinN = sb(ps, "HinN", [128, Q], BF16)
                    Rt = sb(ps, "Rt", [128, 128], F32)
                    Rb = [sb(ps, "Rb%d" % i, [128, 128], BF16) for i in range(2)]
                    Yg = [sb(ps, "Yg%d" % g, [128, Q], BF16) for g in range(8)]
                    yT = sb(ps, "yT", [128, T], F32)
                    gbf = sb(ps, "gbf", [128, 2, T], BF16)
                    gl = [sb(ps, "gl%d" % i, [128, 512], F32) for i in range(3)]
                    sgt = sb(ps, "sgt", [128, 512], BF16)
                    sbo = [sb(ps, "sbo%d" % i, [128, 512], BF16) for i in range(2)]
                    for d in range(2):
                        S.op("pool", lambda e, d=d: e.memset(Hs[d][:, 0:2], 0.0), writes=["Hs%d" % d])
                    uTv = uT[:, :].rearrange("p (c s) -> p c s", s=8)
                    yTv = yT[:, :].rearrange("p (c t) -> p c t", t=8)
                    rbc = [0]
                    for hf in range(2):
                        S.dma("sp", lambda e, hf=hf: e.dma_start(out=uT[:], in_=R["UT"][hf * 128:(hf + 1) * 128, :]), writes=["uT"])
                        for g in range(8):
                            gi = hf * 8 + g
                            for (c0, c1) in PCS:
                                pb, pbn = bank()
                                for s_ in range(8):
                                    S.op("pe", lambda e, g=g, s_=s_, pb=pb, c0=c0, c1=c1: e.matmul(pb[:, 0:c1 - c0], lhsT=selin[:, g, s_, :], rhs=uTv[:, c0:c1, s_],
                                                                                                 start=(s_ == 0), stop=(s_ == 7)), reads=["selin", "uT"], writes=[pbn])
                                eg = evac_eng()
                                S.op(eg, copy_op(eg, Vg[:, c0:c1], pb[:, 0:c1 - c0]), reads=[pbn], writes=["Vg"])
                            S.op("pool", lambda e: e.tensor_copy(out=VRg[:, 0:32], in_=Vg[:, 31::-1]), reads=["Vg"], writes=["VRg"])
                            S.op("pool", lambda e: e.tensor_copy(out=VRg[:, 32:Q], in_=Vg[:, Q - 1:31:-1]), reads=["Vg"], writes=["VRg"])
                            for d in range(2 if CUT >= 21 else 0):
                                dg = d * 16 + gi
                                src = Vg if d == 0 else VRg
                                srcn = "Vg" if d == 0 else "VRg"
                                hm, hs = Hm[d], Hs[d]
                                hmn, hsn = "Hm%d" % d, "Hs%d" % d
                                for (c0, c1) in PCS:
                                    pb, pbn = bank()
                                    S.op("pe", lambda e, dg=dg, pb=pb, c0=c0, c1=c1, src=src: e.matmul(pb[:, 0:c1 - c0], lhsT=bblk[:, dg, :], rhs=src[:, c0:c1], start=True, stop=True),
                                         reads=["bblk", srcn], writes=[pbn])
                                    S.op("dve", lambda e, pb=pb, c0=c0, c1=c1, hm=hm: e.tensor_copy(out=hm[:, c0:c1], in_=pb[:, 0:c1 - c0]), reads=[pbn], writes=[hmn])
                                    S.op("act", lambda e, c0=c0, c1=c1, hs=hs, hm=hm: e.activation(out=hs[:, 2 + c0:2 + c1], in_=hm[:, c0:c1], func=AF.Copy), reads=[hmn], writes=[hsn])
                                for j in range(11 if CUT >= 22 else 0):
                                    sh = 1 << j
                                    rb = Rb[rbc[0] % 2]
                                    rbn = "Rb%d" % (rbc[0] % 2)
                                    rbc[0] += 1
                                    S.op("dve", lambda e, j=j, dg=dg: e.tensor_scalar(out=Rt[:], in0=ident[:], scalar1=MU[:, j, 0, dg:dg + 1], scalar2=None, op0=ALU.mult),
                                         reads=["MU"], writes=["Rt"])
                                    S.op("dve", lambda e, j=j, dg=dg, rb=rb: e.scalar_tensor_tensor(out=rb[:], in0=jm[:], scalar=MU[:, j, 1, dg:dg + 1], in1=Rt[:], op0=ALU.mult, op1=ALU.add),
                                         reads=["MU", "Rt", "jm"], writes=[rbn])
                                    pieces = []
                                    q0 = sh
                                    while q0 < Q:
                                        q1 = min(q0 + 512, Q)
                                        pieces.append((q0, q1))
                                        q0 = q1
                                    pbs = []
                                    for (q0, q1) in pieces:
                                        pb, pbn = bank()
                                        pbs.append((pb, pbn))
                                        S.op("pe", lambda e, pb=pb, q0=q0, q1=q1, sh=sh, rb=rb, hs=hs: e.matmul(pb[:, 0:q1 - q0], lhsT=rb[:], rhs=hs[:, 2 + q0 - sh:2 + q1 - sh], start=True, stop=True),
                                             reads=[rbn, hsn], writes=[pbn])
                                    for (q0, q1), (pb, pbn) in zip(pieces, pbs):
                                        S.op("dve", lambda e, pb=pb, q0=q0, q1=q1, hm=hm: e.tensor_tensor(out=hm[:, q0:q1], in0=pb[:, 0:q1 - q0], in1=hm[:, q0:q1], op=ALU.add),
                                             reads=[pbn, hmn], writes=[hmn])
                                    for (a0, a1) in ((0, 512), (512, 1024), (1024, Q)):
                                        if a1 <= sh:
                                            continue
                                        S.op("act", lambda e, a0=a0, a1=a1, hm=hm, hs=hs: e.activation(out=hs[:, 2 + a0:2 + a1], in_=hm[:, a0:a1], func=AF.Copy), reads=[hmn], writes=[hsn])
                            S.op("pool", lambda e: e.tensor_copy(out=HinN[:, 0:32], in_=Hs[1][:, 32:0:-1]), reads=["Hs1"], writes=["HinN"])
                            S.op("pool", lambda e: e.tensor_copy(out=HinN[:, 32:Q], in_=Hs[1][:, Q:32:-1]), reads=["Hs1"], writes=["HinN"])
                            yg = Yg[g]
                            for (c0, c1) in (PCS if CUT >= 23 else []):
                                pb, pbn = bank()
                                S.op("pe", lambda e, gi=gi, pb=pb, c0=c0, c1=c1: e.matmul(pb[:, 0:c1 - c0], lhsT=dsum[:, gi, :], rhs=Vg[:, c0:c1], start=True, stop=False),
                                     reads=["dsum", "Vg"], writes=[pbn])
                                S.op("pe", lambda e, gi=gi, pb=pb, c0=c0, c1=c1: e.matmul(pb[:, 0:c1 - c0], lhsT=cblk[:, gi, :], rhs=Hs[0][:, 1 + c0:1 + c1], start=False, stop=False),
                                     reads=["cblk", "Hs0"], writes=[pbn])
                                S.op("pe", lambda e, gi=gi, pb=pb, c0=c0, c1=c1: e.matmul(pb[:, 0:c1 - c0], lhsT=cblk[:, 16 + gi, :], rhs=HinN[:, c0:c1], start=False, stop=True),
                                     reads=["cblk", "HinN"], writes=[pbn])
                                eg = evac_eng()
                                S.op(eg, copy_op(eg, yg[:, c0:c1], pb[:, 0:c1 - c0]), reads=[pbn], writes=["Yg%d" % g])
                        for t_ in range(8 if CUT >= 24 else 0):
                            for (c0, c1) in PCS:
                                pb, pbn = bank()
                                for g in range(8):
                                    S.op("pe", lambda e, g=g, t_=t_, pb=pb, c0=c0, c1=c1: e.matmul(pb[:, 0:c1 - c0], lhsT=selout[:, t_, g, :], rhs=Yg[g][:, c0:c1],
                                                                                                 start=(g == 0), stop=(g == 7)), reads=["selout", "Yg%d" % g], writes=[pbn])
                                eg = evac_eng()
                                S.op(eg, copy_op(eg, yTv[:, c0:c1, t_], pb[:, 0:c1 - c0]), reads=[pbn], writes=["yT"])
                        if DEBUG:
                            S.dma("sp", lambda e, hf=hf: e.dma_start(out=R["YT"][hf * 128:(hf + 1) * 128, :], in_=yT[:]), reads=["yT"], writes=["YTd"])
                        for i, (t0, n, v) in enumerate(tiles_of(512) if CUT >= 25 else []):
                            a_, b_, c_ = gl
                            ys = yT[:, t0:t0 + n]
                            S.op("pool", lambda e, ys=ys, n=n: e.tensor_tensor(out=a_[:, :n], in0=ys, in1=ys, op=ALU.mult), reads=["yT"], writes=["gl0"])
                            S.op("dve", lambda e, n=n: e.tensor_scalar(out=b_[:, :n], in0=a_[:, :n], scalar1=0.044715, scalar2=1.0, op0=ALU.mult, op1=ALU.add), reads=["gl0"], writes=["gl1"])
                            S.op("pool", lambda e, ys=ys, n=n: e.tensor_tensor(out=a_[:, :n], in0=b_[:, :n], in1=ys, op=ALU.mult), reads=["gl1", "yT"], writes=["gl0"])
                            S.op("act", lambda e, n=n: e.activation(out=c_[:, :n], in_=a_[:, :n], func=AF.Sigmoid, scale=1.5957691216), reads=["gl0"], writes=["gl2"])
                            S.op("dve", lambda e, ys=ys, n=n, t0=t0, hf=hf: e.tensor_tensor(out=gbf[:, hf, t0:t0 + n], in0=c_[:, :n], in1=ys, op=ALU.mult), reads=["gl2", "yT"], writes=["gbf"])
                    SBv = R["SB"].rearrange("(c p) t -> p c t", p=128)
                    for i, (t0, n, v) in enumerate(tiles_of(512) if CUT >= 26 else []):
                        for oc in range(2):
                            pb, pbn = bank()
                            for kc in range(2):
                                S.op("pe", lambda e, oc=oc, kc=kc, pb=pb, t0=t0, n=n: e.matmul(pb[:, :n], lhsT=wglu[:, kc, oc * 128:(oc + 1) * 128], rhs=gbf[:, kc, t0:t0 + n],
                                                                                             start=(kc == 0), stop=(kc == 1)), reads=["wglu", "gbf"], writes=[pbn])
                            S.op("act", lambda e, pb=pb, n=n: e.activation(out=sgt[:, :n], in_=pb[:, :n], func=AF.Sigmoid), reads=[pbn], writes=["sgt"])
                            so = sbo[oc]
                            S.op("dve", lambda e, so=so, oc=oc, t0=t0, n=n: e.tensor_tensor(out=so[:, :n], in0=sgt[:, :n], in1=gbf[:, oc, t0:t0 + n], op=ALU.mult),
                                 reads=["sgt", "gbf"], writes=["sbo%d" % oc])
                            S.dma("sp", lambda e, so=so, oc=oc, t0=t0, n=n: e.dma_start(out=SBv[:, oc, t0:t0 + n], in_=so[:, :n]), reads=["sbo%d" % oc], writes=["SBo"])
                    S.flush()

        def phase3a(l):
            NT = 512
            tl = tiles_of(NT, with_ctx=(l < DEPTH - 1))
            with ExitStack() as ps:
                wa = sb(ps, "wa", [128, 4, D], BF16)
                wbra = sb(ps, "wbra", [128, 2, D], BF16)
                wb = sb(ps, "wb", [128, 2, D], BF16)
                wc = sb(ps, "wc", [128, 4, D], BF16)
                wo = sb(ps, "wo", [128, 8, D], BF16)
                cs64 = sb(ps, "cs64", [128, 2, 128], BF16)
                S.dma("pool", lambda e: e.dma_start(out=cs64[:], in_=I["cs64"].rearrange("a p m -> p a m")), writes=["cs64"])
                S.dma("pool", lambda e: e.dma_start(out=wbra[:], in_=I["w_br_a"][l].rearrange("(k p) n -> p k n", p=128)), writes=["wbra"])
                S.dma("pool", lambda e: e.dma_start(out=wb[:], in_=I["w_br_b"][l].rearrange("(k p) n -> p k n", p=128)), writes=["wb"])
                S.dma("pool", lambda e: e.dma_start(out=wc[:], in_=I["w_br_c"][l].rearrange("(k p) n -> p k n", p=128)), writes=["wc"])
                S.dma("pool", lambda e: e.dma_start(out=wo[:], in_=I["w_out"][l].rearrange("(k p) n -> p k n", p=128)), writes=["wo"])
                for a in range(2 if CUT > -1 else 0):
                    for q in range(2):
                        for hh in range(2):
                            pb, pbn = bank()
                            S.op("pe", lambda e, a=a, q=q, hh=hh, pb=pb: e.matmul(pb[:, :], lhsT=cs64[:, a, :], rhs=wbra[:, q, hh * 512:(hh + 1) * 512],
                                                                                 start=True, stop=True), reads=["cs64", "wbra"], writes=[pbn])
                            eg = evac_eng()
                            S.op(eg, copy_op(eg, wa[:, a * 2 + q, hh * 512:(hh + 1) * 512], pb[:, :]), reads=[pbn], writes=["wa"])
                xTs = [sb(ps, "xT%d" % i, [128, 8, NT], F32) for i in range(2)]
                fsn = [sb(ps, "fsn%d" % i, [128, 10, NT], BF16) for i in range(2)]
                gts = [sb(ps, "gts%d" % i, [128, 24, NT], BF16) for i in range(2)]
                mT = sb(ps, "mT", [128, 8, NT], BF16)
                hT = sb(ps, "hT", [128, 8, NT], BF16)
                sq = sb(ps, "sq", [128, 8, NT], BF16)
                rstd = sb(ps, "rstd", [128, NT], F32)
                tmpa = sb(ps, "tmpa", [128, NT], F32)
                tmps = [sb(ps, "tmps%d" % i, [128, NT], F32) for i in range(2)]
                t1s = [sb(ps, "t1_%d" % i, [128, NT], F32) for i in range(2)]
                t2s = [sb(ps, "t2_%d" % i, [128, NT], F32) for i in range(2)]
                t3s = [sb(ps, "t3_%d" % i, [128, NT], F32) for i in range(2)]
                XTv = R["XT"].rearrange("(c p) t -> p c t", p=128)
                XMv = R["XM"].rearrange("(c p) t -> p c t", p=128)
                H2v = R["H2"].rearrange("(c p) t -> p c t", p=128)
                FAv = R["FA"].rearrange("(c p) t -> p c t", p=128)
                SBv = R["SB"].rearrange("(c p) t -> p c t", p=128)
                NCv = R["NCT"].rearrange("(c p) t -> p c t", p=128)
                GTv = R["GT"].rearrange("(c p) t -> p c t", p=128)

                def load(i):
                    t0, n, v = tl[i]
                    b = i % 2
                    S.dma("sp", lambda e: e.dma_start(out=xTs[b][:, :, :n], in_=XTv[:, :, t0:t0 + n]), writes=["xT%d" % b])
                    S.dma("sp", lambda e: e.dma_start(out=fsn[b][:, 0:4, :n], in_=FAv[:, :, t0:t0 + n]), writes=["fsnA%d" % b])
                    S.dma("sp", lambda e: e.dma_start(out=fsn[b][:, 4:6, :n], in_=SBv[:, :, t0:t0 + n]), writes=["fsnB%d" % b])
                    S.dma("sp", lambda e: e.dma_start(out=fsn[b][:, 6:10, :n], in_=NCv[:, :, t0:t0 + n]), writes=["fsnC%d" % b])
                    for g in range(3):
                        S.dma("sp", lambda e, g=g: e.dma_start(out=gts[b][:, g * 8:(g + 1) * 8, :n], in_=GTv[:, g * 8:(g + 1) * 8, t0:t0 + n]),
                              writes=["gts%d_%d" % (b, g)])

                def compute(i):
                    t0, n, v = tl[i]
                    b = i % 2
                    xT = xTs[b]
                    xn = "xT%d" % b
                    f = fsn[b]
                    g = gts[b]
                    if CUT < 1:
                        return
                    for d in range(8):
                        dd = d % 2
                        ds = slice(d * 128, (d + 1) * 128)
                        pa, pan = bank()
                        for k in range(4):
                            S.op("pe", lambda e, k=k, pa=pa, ds=ds: e.matmul(pa[:, :n], lhsT=wa[:, k, ds], rhs=f[:, k, :n], start=(k == 0), stop=(k == 3)),
                                 reads=["wa", "fsnA%d" % b], writes=[pan])
                        pbk, pbn = bank()
                        for k in range(2):
                            S.op("pe", lambda e, k=k, pbk=pbk, ds=ds: e.matmul(pbk[:, :n], lhsT=wb[:, k, ds], rhs=f[:, 4 + k, :n], start=(k == 0), stop=(k == 1)),
                                 reads=["wb", "fsnB%d" % b], writes=[pbn])
                        pc, pcn = bank()
                        for k in range(4):
                            S.op("pe", lambda e, k=k, pc=pc, ds=ds: e.matmul(pc[:, :n], lhsT=wc[:, k, ds], rhs=f[:, 6 + k, :n], start=(k == 0), stop=(k == 3)),
                                 reads=["wc", "fsnC%d" % b], writes=[pcn])
                        t1, t2, t3 = t1s[dd], t2s[dd], t3s[dd]
                        S.op("dve", lambda e, pa=pa, t1=t1, d=d: e.tensor_tensor(out=t1[:, :n], in0=pa[:, :n], in1=g[:, d, :n], op=ALU.mult),
                             reads=[pan, "gts%d_0" % b], writes=["t1_%d" % dd])
                        S.op("dve", lambda e, pbk=pbk, t2=t2, d=d: e.tensor_tensor(out=t2[:, :n], in0=pbk[:, :n], in1=g[:, 8 + d, :n], op=ALU.mult),
                             reads=[pbn, "gts%d_1" % b], writes=["t2_%d" % dd])
                        S.op("dve", lambda e, pc=pc, t3=t3, d=d: e.tensor_tensor(out=t3[:, :n], in0=pc[:, :n], in1=g[:, 16 + d, :n], op=ALU.mult),
                             reads=[pcn, "gts%d_2" % b], writes=["t3_%d" % dd])
                        S.op("pool", lambda e, t1=t1, t2=t2: e.tensor_tensor(out=t1[:, :n], in0=t1[:, :n], in1=t2[:, :n], op=ALU.add),
                             reads=["t1_%d" % dd, "t2_%d" % dd], writes=["t1_%d" % dd])
                        S.op("pool", lambda e, t1=t1, t3=t3, d=d: e.tensor_tensor(out=mT[:, d, :n], in0=t1[:, :n], in1=t3[:, :n], op=ALU.add),
                             reads=["t1_%d" % dd, "t3_%d" % dd], writes=["mT"])
                    if CUT < 2:
                        return
                    for d in range(8):
                        ds = slice(d * 128, (d + 1) * 128)
                        po, pon = bank()
                        for k in range(8):
                            S.op("pe", lambda e, k=k, po=po, ds=ds: e.matmul(po[:, :n], lhsT=wo[:, k, ds], rhs=mT[:, k, :n], start=(k == 0), stop=(k == 7)),
                                 reads=["wo", "mT"], writes=[pon])
                        S.op("dve", lambda e, po=po, d=d: e.scalar_tensor_tensor(out=xT[:, d, :n], in0=po[:, :n], scalar=mod[:, l, 16 + d, v:v + 1],
                                                                               in1=xT[:, d, :n], op0=ALU.mult, op1=ALU.add),
                             reads=[pon, xn, "mod"], writes=[xn])
                    if CUT < 3:
                        return
                    S.dma("sp", lambda e: e.dma_start(out=XMv[:, :, t0:t0 + n], in_=xT[:, :, :n]), reads=[xn], writes=["XM%d" % i])
                    if CUT < 4:
                        return
                    norm_mod(l, 1, v, xT, n, sq, rstd, tmpa, tmps, hT, xn, "hT", i)
                    S.dma("sp", lambda e: e.dma_start(out=H2v[:, :, t0:t0 + n], in_=hT[:, :, :n]), reads=["hT"], writes=["H2%d" % i])

                load(0)
                for i in range(len(tl)):
                    if i + 1 < len(tl):
                        load(i + 1)
                    compute(i)
                S.flush()

        def phase3b(l):
            NT = 256
            last = (l == DEPTH - 1)
            tl = tiles_of(NT, with_ctx=not last)
            with ExitStack() as ps:
                w1 = sb(ps, "w1", [128, 8, DFF], BF16)
                w2 = sb(ps, "w2", [128, 32, D], BF16)
                w1v = I["w_ff1"][l].rearrange("(k p) n -> p k n", p=128)
                w2v = I["w_ff2"][l].rearrange("(k p) n -> p k n", p=128)
                for j in range(8):
                    S.dma("pool", lambda e, j=j: e.dma_start(out=w1[:, :, j * 512:(j + 1) * 512], in_=w1v[:, :, j * 512:(j + 1) * 512]), writes=["w1_%d" % j])
                for j in range(8):
                    S.dma("pool", lambda e, j=j: e.dma_start(out=w2[:, j * 4:(j + 1) * 4, :], in_=w2v[:, j * 4:(j + 1) * 4, :]), writes=["w2_%d" % j])
                xTs = [sb(ps, "xT%d" % i, [128, 8, NT], F32) for i in range(2)]
                hTs = [sb(ps, "hT%d" % i, [128, 8, NT], BF16) for i in range(2)]
                aT = sb(ps, "aT", [128, 32, NT], BF16)
                rr = [sb(ps, "rr%d" % i, [128, NT], BF16) for i in range(2)]
                XTv = R["XT"].rearrange("(c p) t -> p c t", p=128)
                XMv = R["XM"].rearrange("(c p) t -> p c t", p=128)
                H2v = R["H2"].rearrange("(c p) t -> p c t", p=128)
                if last:
                    sq = sb(ps, "sq", [128, 8, NT], BF16)
                    rstd = sb(ps, "rstd", [128, NT], F32)
                    tmpa = sb(ps, "tmpa", [128, NT], F32)
                    yT = sb(ps, "yT", [128, 8, NT], F32)
                    otok = sb(ps, "otok", [128, 2, D], F32)

                def load(i):
                    t0, n, v = tl[i]
                    b = i % 2
                    S.dma("sp", lambda e: e.dma_start(out=xTs[b][:, :, :n], in_=XMv[:, :, t0:t0 + n]), writes=["xT%d" % b])
                    S.dma("sp", lambda e: e.dma_start(out=hTs[b][:, :, :n], in_=H2v[:, :, t0:t0 + n]), writes=["hT%d" % b])

                def compute(i):
                    t0, n, v = tl[i]
                    b = i % 2
                    xT = xTs[b]
                    xn = "xT%d" % b
                    hT = hTs[b]
                    hn = "hT%d" % b
                    for f in range(32):
                        pb, pbn = bank()
                        for k in range(8):
                            S.op("pe", lambda e, k=k, f=f, pb=pb: e.matmul(pb[:, :n], lhsT=w1[:, k, f * 128:(f + 1) * 128], rhs=hT[:, k, :n],
                                                                         start=(k == 0), stop=(k == 7)), reads=[hn, "w1_%d" % (f // 4)], writes=[pbn])
                        r = rr[f % 2]
                        rn = "rr%d" % (f % 2)
                        S.op("act", lambda e, pb=pb, r=r: e.activation(out=r[:, :n], in_=pb[:, :n], func=AF.Relu), reads=[pbn], writes=[rn])
                        S.op("pool", lambda e, r=r, f=f: e.tensor_tensor(out=aT[:, f, :n], in0=r[:, :n], in1=r[:, :n], op=ALU.mult),
                             reads=[rn], writes=["aT"])
                    for d in range(8):
                        pb, pbn = bank()
                        for f in range(32):
                            S.op("pe", lambda e, f=f, d=d, pb=pb: e.matmul(pb[:, :n], lhsT=w2[:, f, d * 128:(d + 1) * 128], rhs=aT[:, f, :n],
                                                                         start=(f == 0), stop=(f == 31)), reads=["aT", "w2_%d" % (f // 4)], writes=[pbn])
                        S.op("dve", lambda e, pb=pb, d=d: e.scalar_tensor_tensor(out=xT[:, d, :n], in0=pb[:, :n], scalar=mod[:, l, 40 + d, v:v + 1],
                                                                               in1=xT[:, d, :n], op0=ALU.mult, op1=ALU.add),
                             reads=[pbn, xn, "mod"], writes=[xn])
                    if not last:
                        S.dma("sp", lambda e: e.dma_start(out=XTv[:, :, t0:t0 + n], in_=xT[:, :, :n]), reads=[xn], writes=["XT%d" % i])
                        return
                    S.op("act", lambda e: e.activation(out=sq[:, :, :n], in_=xT[:, :, :n], func=AF.Square), reads=[xn], writes=["sq"])
                    pb, pbn = bank()
                    for c in range(8):
                        S.op("pe", lambda e, c=c, pb=pb: e.matmul(pb[:, :n], lhsT=ones_bf[:], rhs=sq[:, c, :n], start=(c == 0), stop=(c == 7)),
                             reads=["sq", "ones"], writes=[pbn])
                    S.op("act", lambda e, pb=pb: e.activation(out=tmpa[:, :n], in_=pb[:, :n], func=AF.Sqrt, scale=1.0 / D, bias=epsb[:, 0:1]),
                         reads=[pbn, "epsb"], writes=["tmpa"])
                    S.op("dve", lambda e: e.reciprocal(out=rstd[:, :n], in_=tmpa[:, :n]), reads=["tmpa"], writes=["rstd"])
                    for c in range(8):
                        S.op("dve", lambda e, c=c: e.scalar_tensor_tensor(out=yT[:, c, :n], in0=xT[:, c, :n], scalar=gfin[:, c:c + 1], in1=rstd[:, :n],
                                                                          op0=ALU.mult, op1=ALU.mult), reads=[xn, "rstd", "gfin"], writes=["yT"])
                    for s in range(n // 128):
                        for half in range(2):
                            pb, pbn = bank()
                            for cc in range(4):
                                c = half * 4 + cc
                                S.op("pe", lambda e, s=s, c=c, cc=cc, pb=pb: e.transpose(out=pb[:, cc * 128:(cc + 1) * 128], in_=yT[:, c, s * 128:(s + 1) * 128],
                                                                                       identity=ident[:]), reads=["yT", "ident"], writes=[pbn])
                            eg = evac_eng()
                            S.op(eg, copy_op(eg, otok[:, s, half * 512:(half + 1) * 512], pb[:, :]), reads=[pbn], writes=["otok"])
                    r0 = t0 - LC
                    S.dma("sp", lambda e: e.dma_start(out=out[r0:r0 + n, :].rearrange("(s p) d -> p s d", p=128), in_=otok[:, :n // 128, :]),
                          reads=["otok"], writes=["out%d" % i])

                load(0)
                for i in range(len(tl)):
                    if i + 1 < len(tl):
                        load(i + 1)
                    compute(i)
                S.flush()

        if "p0" in phases:
            phase0()
        for l in layers:
            if "p1" in phases:
                phase1(l)
            if "p2f" in phases:
                phase2_fnet(l)
            if "p2n" in phases:
                phase2_na(l)
            if "p2s" in phases:
                phase2_s5(l)
            if "p3a" in phases:
                phase3a(l)
            if "p3b" in phases:
                phase3b(l)
        if S.ops["sp"] or S.ops["pe"] or S.ops["pool"]:
            S.flush()
    return nc


ALL_PHASES = ("p0", "p1", "p2f", "p2n", "p2s", "p3a", "p3b")


def kernel(**inputs):
    f32 = lambda a: np.ascontiguousarray(np.asarray(a, dtype=np.float32))
    x = f32(inputs["x"])
    ctx = f32(inputs["ctx"])
    c = f32(inputs["c"])
    c_ctx = f32(inputs["c_ctx"])
    shared = {}
    for k in W_SHAPES:
        if k == "rpbg":
            shared[k] = na_gather_rpb(f32(inputs["na_rpb"]))
        else:
            shared[k] = f32(inputs[k])
    shared.update(_consts())
    nb = x.shape[0]
    in_maps = []
    for core in range(8):
        b = core % nb
        m = dict(shared)
        m["x"] = x[b]
        m["ctx"] = ctx[b]
        m["cvec"] = np.ascontiguousarray(np.stack([c[b], c_ctx]))
        in_maps.append(m)
    nc = build(set(ALL_PHASES))
    res = run_bass_kernel_spmd(nc, in_maps, core_ids=list(range(8)))
    out = np.stack([np.asarray(res.results[b]["out"], dtype=np.float32) for b in range(nb)], axis=0)
    return out
```

```python
import numpy as np
from contextlib import ExitStack
import concourse.bass as bass
import concourse.mybir as mybir
from concourse.bass_utils import run_bass_kernel_spmd

F32 = mybir.dt.float32
BF16 = mybir.dt.bfloat16
AF = mybir.ActivationFunctionType
ALU = mybir.AluOpType

D = 1024
L = 8192
LC = 256
T = L + LC
DEPTH = 2
DIN = 5120
DFF = 4096
EPS = 1e-6
NEG = -30000.0
CUT = 99
DEBUG = 0

COMPUTE = ("pe", "dve", "act", "pool")
NDMA_SEM = 12


class Sched:
    def __init__(self, nc, stack):
        self.nc = nc
        self.ops = {e: [] for e in ("pe", "dve", "act", "pool", "sp")}
        self.sem = {}
        self.cnt = {}
        for e in COMPUTE:
            self.sem[e] = stack.enter_context(nc.semaphore("s_" + e))
            self.cnt[e] = 0
        self.dsem = {}
        self.dcnt = {}
        self.dnext = {}
        for q in ("sp", "pool", "act"):
            self.dsem[q] = [stack.enter_context(nc.semaphore("d_%s%d" % (q, j))) for j in range(NDMA_SEM)]
            self.dcnt[q] = [0] * NDMA_SEM
            self.dnext[q] = 0
        self.seen = {e: {} for e in self.ops}
        self.last_w = {}
        self.reads = {}
        self.out_events = []
        self.nblock = 0

    def _need(self, eng, reads, writes):
        need = {}

        def add(ev):
            if ev is None:
                return
            k, v = ev
            if need.get(k, 0) < v:
                need[k] = v

        for r in reads:
            add(self.last_w.get(r))
        own = eng in COMPUTE
        for w in writes:
            ev = self.last_w.get(w)
            if ev is not None and not (own and ev[0] == eng):
                add(ev)
            for ev in self.reads.get(w, ()):
                if not (own and ev[0] == eng):
                    add(ev)
        waits = []
        for k, v in need.items():
            if k == eng and eng == "pe":
                continue
            if self.seen[eng].get(k, 0) >= v:
                continue
            self.seen[eng][k] = v
            waits.append((k, v))
        return waits

    def _semof(self, k):
        if isinstance(k, tuple):
            return self.dsem[k[0]][k[1]]
        return self.sem[k]

    def _commit(self, ev, reads, writes):
        for w in writes:
            self.last_w[w] = ev
            self.reads[w] = []
        for r in reads:
            self.reads.setdefault(r, []).append(ev)

    def op(self, eng, fn, reads=(), writes=()):
        waits = self._need(eng, reads, writes)
        self.cnt[eng] += 1
        ev = (eng, self.cnt[eng])
        self.ops[eng].append((waits, fn, self.sem[eng], 1))
        self._commit(ev, reads, writes)
        return ev

    def dma(self, q, fn, reads=(), writes=(), is_output=False):
        j = self.dnext[q]
        self.dnext[q] = (j + 1) % NDMA_SEM
        key = (q, j)
        waits = self._need(q, reads, writes)
        if self.dcnt[q][j] > 0 and self.seen[q].get(key, 0) < self.dcnt[q][j]:
            self.seen[q][key] = self.dcnt[q][j]
            waits.append((key, self.dcnt[q][j]))
        self.dcnt[q][j] += 16
        ev = (key, self.dcnt[q][j])
        self.ops[q].append((waits, fn, self.dsem[q][j], 16))
        self._commit(ev, reads, writes)
        return ev

    def flush(self):
        tail = {e: [] for e in self.ops}
        for e in self.ops:
            for k in COMPUTE:
                if k != e and self.cnt[k] > self.seen[e].get(k, 0):
                    self.seen[e][k] = self.cnt[k]
                    tail[e].append((k, self.cnt[k]))
            for q in self.dsem:
                for j in range(NDMA_SEM):
                    v = self.dcnt[q][j]
                    if v > self.seen[e].get((q, j), 0):
                        self.seen[e][(q, j)] = v
                        tail[e].append(((q, j), v))
        nc = self.nc
        with nc.Block() as block:
            def replay(name):
                def body(e):
                    for waits, fn, sem, inc in self.ops[name]:
                        for k, v in waits:
                            e.wait_ge(self._semof(k), v)
                        fn(e).then_inc(sem, inc)
                    for k, v in tail[name]:
                        e.wait_ge(self._semof(k), v)
                return body
            block.sync(replay("sp"))
            block.tensor(replay("pe"))
            block.vector(replay("dve"))
            block.scalar(replay("act"))
            block.gpsimd(replay("pool"))
        for e in self.ops:
            self.ops[e] = []
        self.last_w = {}
        self.reads = {}
        self.nblock += 1


def _consts():
    c = {}
    c["ident"] = np.eye(128, dtype=np.float32)
    ij = np.outer(np.arange(64), np.arange(64)) * (2 * np.pi / 64)
    cs = np.zeros((2, 128, 128), np.float32)
    for g in range(2):
        cs[0, g * 64:(g + 1) * 64, g * 64:(g + 1) * 64] = np.cos(ij)
        cs[1, g * 64:(g + 1) * 64, g * 64:(g + 1) * 64] = np.sin(ij)
    c["cs64"] = cs
    r = np.arange(128)[:, None, None]
    cc = np.arange(64)[None, :, None]
    k1 = np.arange(128)[None, None, :]
    ang = (2 * np.pi / 8192) * ((k1 * (64 * r + cc)) % 8192)
    c["fn_tc"] = np.cos(ang).astype(np.float32)
    c["fn_tsn"] = (-np.sin(ang)).astype(np.float32)
    sc = 1.0 / np.sqrt(8192.0 * 64.0)
    a64 = (2 * np.pi / 64) * ((np.arange(64)[:, None] * np.arange(64)[None, :]) % 64)
    c["fn_w64"] = np.stack([np.cos(a64) * sc, np.sin(a64) * sc, -np.sin(a64) * sc]).astype(np.float32)
    scc = 1.0 / np.sqrt(256.0 * 64.0)
    a256 = (2 * np.pi / 256) * ((np.arange(256)[:, None] * np.arange(256)[None, :]) % 256)
    c["fn_c256"] = np.stack([np.cos(a256) * scc, -np.sin(a256) * scc]).astype(np.float32)
    w = np.arange(64)
    cs0 = np.clip(w - 8, 0, 48)
    wp = np.arange(64)[:, None]
    inside = (wp >= cs0[None, :]) & (wp < cs0[None, :] + 16)
    cm = np.where(inside, 0.0, NEG).astype(np.float32)
    jm = np.zeros((128, 128), np.float32)
    for p in range(64):
        jm[p, 64 + p] = 1.0
        jm[64 + p, p] = -1.0
    c["s5_jm"] = jm
    sidx = np.repeat(np.arange(8), 16)
    c["s5_mask"] = np.stack([(sidx[None, :] >= sidx[:, None]), (sidx[None, :] <= sidx[:, None])]).astype(np.float32)
    selin = np.zeros((128, 8, 8, 128), np.float32)
    for g in range(8):
        for s_ in range(8):
            for m_ in range(16):
                selin[g * 16 + m_, g, s_, s_ * 16 + m_] = 1.0
    c["s5_selin"] = selin
    c["s5_selout"] = np.ascontiguousarray(np.transpose(selin, (3, 2, 1, 0)))
    c["na_cm"] = np.ascontiguousarray(np.broadcast_to(np.concatenate([cm, cm], 0)[:, None, :], (128, 15, 64))).astype(np.float32)
    return c


def na_gather_rpb(rpb):
    wp = np.arange(64)[:, None]
    w = np.arange(64)[None, :]
    idx = np.clip(wp - w + 15, 0, 30)
    g = rpb[:, :, :, idx]
    g = np.transpose(g, (0, 1, 3, 2, 4))
    return np.ascontiguousarray(np.concatenate([g, g], axis=2)).astype(np.float32)


def na_blocks():
    out = []
    for j in range(64):
        rs = [min(max(2 * j + b - 4, 0), 120) for b in range(2)]
        lo, hi = min(rs), max(rs) + 7
        blocks = list(range(lo // 2, hi // 2 + 1))
        pat = []
        for i in blocks:
            for a in range(2):
                for b in range(2):
                    rho = 2 * i + a
                    r = 2 * j + b
                    ok = rs[b] <= rho <= rs[b] + 7
                    pat.append((rho - r + 7) if ok else None)
        out.append((blocks, tuple(pat)))
    return out


SCRATCH = {
    "XT": ([D, T], F32), "XM": ([D, T], F32), "H2": ([D, T], BF16),
    "ZF": ([T, 256], BF16), "VV": ([T, 512], BF16), "UT": ([256, T], BF16),
    "QT": ([512, T], BF16), "KT": ([512, T], BF16), "GT": ([3072, T], BF16),
    "FA": ([512, T], BF16), "SB": ([256, T], BF16), "NCT": ([512, T], BF16),
    "AF": ([64, 128 * 512], BF16), "DBG": ([128, 2048], F32), "YT": ([256, T], F32),
}

W_SHAPES = {
    "w_mod": [DEPTH, D, 6 * D], "b_mod": [DEPTH, 6 * D], "g_norm1": [DEPTH, D], "g_norm2": [DEPTH, D],
    "w_in": [DEPTH, D, DIN], "w_br_a": [DEPTH, 256, D], "w_br_b": [DEPTH, 256, D], "w_br_c": [DEPTH, 512, D],
    "w_out": [DEPTH, D, D], "w_ff1": [DEPTH, D, DFF], "w_ff2": [DEPTH, DFF, D], "g_final": [D],
    "rpbg": [DEPTH, 8, 128, 15, 64],
    "s5_lam_re": [DEPTH, 2, 16, 64], "s5_lam_im": [DEPTH, 2, 16, 64], "s5_log_dt": [DEPTH, 2, 16],
    "s5_b_re": [DEPTH, 2, 16, 64, 16], "s5_b_im": [DEPTH, 2, 16, 64, 16],
    "s5_c_re": [DEPTH, 2, 16, 16, 64], "s5_c_im": [DEPTH, 2, 16, 16, 64],
    "s5_d": [DEPTH, 16, 16], "s5_w_glu": [DEPTH, 256, 256],
}


def tiles_of(n, with_ctx=True):
    out = []
    if with_ctx:
        t = 0
        while t < LC:
            m = min(n, LC - t)
            out.append((t, m, 1))
            t += m
    t = LC
    while t < T:
        m = min(n, T - t)
        out.append((t, m, 0))
        t += m
    return out


def build(phases, kinds=None, layers=(0, 1)):
    kinds = kinds or {}
    nc = bass.Bass("TRN2", target_bir_lowering=False)
    I = {}
    I["x"] = nc.dram_tensor("x", [L, D], F32, kind="ExternalInput").ap()
    I["ctx"] = nc.dram_tensor("ctx", [LC, D], F32, kind="ExternalInput").ap()
    I["cvec"] = nc.dram_tensor("cvec", [2, D], F32, kind="ExternalInput").ap()
    for k, shp in W_SHAPES.items():
        I[k] = nc.dram_tensor(k, shp, F32, kind="ExternalInput").ap()
    for k, v in _consts().items():
        I[k] = nc.dram_tensor(k, list(v.shape), F32, kind="ExternalInput").ap()
    out = nc.dram_tensor("out", [L, D], F32, kind="ExternalOutput").ap()
    R = {}
    for k, (shp, dt) in SCRATCH.items():
        R[k] = nc.dram_tensor("r_" + k.lower(), shp, dt, kind=kinds.get(k, "Internal")).ap()

    with ExitStack() as st:
        S = Sched(nc, st)
        pbig = [st.enter_context(nc.psum_tensor("pbig%d" % i, [128, 1024], F32)) for i in range(4)]
        pbanks = [pbig[i // 2][:, (i % 2) * 512:(i % 2 + 1) * 512] for i in range(8)]
        pctr = [0]

        def bank():
            i = pctr[0] % 8
            pctr[0] += 1
            return pbanks[i], "pb%d" % i

        evc = [0]

        def evac_eng():
            evc[0] += 1
            return "dve" if evc[0] % 2 else "act"

        def copy_op(eng, out_ap, in_ap):
            if eng == "act":
                return lambda e: e.activation(out=out_ap, in_=in_ap, func=AF.Copy)
            return lambda e: e.tensor_copy(out=out_ap, in_=in_ap)

        uidc = [0]

        def sb(stack, name, shape, dt):
            uidc[0] += 1
            return stack.enter_context(nc.sbuf_tensor("%s_u%d" % (name, uidc[0]), shape, dt))

        ident = sb(st, "ident", [128, 128], F32)
        ones_bf = sb(st, "ones_bf", [128, 128], BF16)
        mod = sb(st, "mod", [128, DEPTH, 48, 2], F32)
        gsc = sb(st, "gsc", [128, DEPTH, 2, 8, 2], F32)
        gfin = sb(st, "gfin", [128, 8], F32)

        NCDMA = dict(allow_slow_non_contiguous=True)
        S.dma("sp", lambda e: e.dma_start(out=ident[:], in_=I["ident"]), writes=["ident"])
        S.op("pool", lambda e: e.memset(ones_bf[:], 1.0), writes=["ones"])

        def phase0():
            with ExitStack() as ps:
                cT = sb(ps, "cT", [128, 8, 2], F32)
                sT = sb(ps, "sT", [128, 8, 2], F32)
                bm = sb(ps, "bm", [128, DEPTH, 48], F32)
                gn = sb(ps, "gn", [128, DEPTH, 2, 8], F32)
                wm = [sb(ps, "wm%d" % i, [128, 8, 512], F32) for i in range(2)]
                for j in range(2):
                    S.dma("sp", lambda e, j=j: e.dma_start(out=cT[:, :, j], in_=I["cvec"][j].rearrange("(k p) -> p k", p=128), **NCDMA), writes=["cT"])
                for l in range(DEPTH):
                    S.dma("sp", lambda e, l=l: e.dma_start(out=bm[:, l, :], in_=I["b_mod"][l].rearrange("(j p) -> p j", p=128), **NCDMA), writes=["bm"])
                    S.dma("sp", lambda e, l=l: e.dma_start(out=gn[:, l, 0, :], in_=I["g_norm1"][l].rearrange("(k p) -> p k", p=128), **NCDMA), writes=["gn0"])
                    S.dma("sp", lambda e, l=l: e.dma_start(out=gn[:, l, 1, :], in_=I["g_norm2"][l].rearrange("(k p) -> p k", p=128), **NCDMA), writes=["gn1"])
                S.dma("sp", lambda e: e.dma_start(out=gfin[:], in_=I["g_final"].rearrange("(k p) -> p k", p=128), **NCDMA), writes=["gfin"])
                S.op("act", lambda e: e.activation(out=sT[:], in_=cT[:], func=AF.Silu), reads=["cT"], writes=["sT"])
                for l in range(DEPTH):
                    pb, pbn = bank()
                    wv = I["w_mod"][l].rearrange("(k p) n -> p k n", p=128)
                    for blk in range(12):
                        w = wm[blk % 2]
                        wn = "wm%d" % (blk % 2)
                        S.dma("sp", lambda e, w=w, blk=blk, wv=wv: e.dma_start(out=w[:], in_=wv[:, :, blk * 512:(blk + 1) * 512]), writes=[wn])
                        for jj in range(4):
                            j = blk * 4 + jj
                            for k in range(8):
                                S.op("pe", lambda e, w=w, jj=jj, j=j, k=k, pb=pb: e.matmul(
                                    pb[:, j * 2:j * 2 + 2], lhsT=w[:, k, jj * 128:(jj + 1) * 128], rhs=sT[:, k, :],
                                    start=(k == 0), stop=(k == 7)), reads=[wn, "sT"], writes=[pbn])
                    for v in range(2):
                        S.op("dve", lambda e, l=l, v=v, pb=pb: e.tensor_tensor(
                            out=mod[:, l, :, v], in0=pb[:, 0:96].rearrange("p (j v) -> p j v", v=2)[:, :, v], in1=bm[:, l, :], op=ALU.add),
                            reads=[pbn, "bm"], writes=["mod"])
                    for nn in range(2):
                        for v in range(2):
                            j0 = 8 + 24 * nn
                            S.op("dve", lambda e, l=l, v=v, nn=nn, j0=j0: e.scalar_tensor_tensor(
                                out=gsc[:, l, nn, :, v], in0=mod[:, l, j0:j0 + 8, v], scalar=1.0, in1=gn[:, l, nn, :],
                                op0=ALU.add, op1=ALU.mult), reads=["mod", "gn%d" % nn], writes=["gsc"])
                S.flush()

        def norm_mod(l, nn, v, xT, n, sq, rstd, tmpa, tmps, hT, xname, hname, uid):
            sh_j0 = 24 * nn
            S.op("act", lambda e: e.activation(out=sq[:, :, :n], in_=xT[:, :, :n], func=AF.Square), reads=[xname], writes=["sq"])
            pb, pbn = bank()
            for c in range(8):
                S.op("pe", lambda e, c=c, pb=pb: e.matmul(pb[:, :n], lhsT=ones_bf[:], rhs=sq[:, c, :n], start=(c == 0), stop=(c == 7)),
                     reads=["sq", "ones"], writes=[pbn])
            S.op("act", lambda e, pb=pb: e.activation(out=tmpa[:, :n], in_=pb[:, :n], func=AF.Sqrt, scale=1.0 / D, bias=epsb[:, 0:1]),
                 reads=[pbn, "epsb"], writes=["tmpa"])
            S.op("dve", lambda e: e.reciprocal(out=rstd[:, :n], in_=tmpa[:, :n]), reads=["tmpa"], writes=["rstd"])
            for c in range(8):
                tt = tmps[c % 2]
                tn = "tmps%d" % (c % 2)
                S.op("dve", lambda e, c=c, tt=tt: e.tensor_tensor(out=tt[:, :n], in0=xT[:, c, :n], in1=rstd[:, :n], op=ALU.mult),
                     reads=[xname, "rstd"], writes=[tn])
                S.op("act", lambda e, c=c, tt=tt: e.activation(out=hT[:, c, :n], in_=tt[:, :n], func=AF.Identity,
                                                              scale=gsc[:, l, nn, c, v:v + 1], bias=mod[:, l, sh_j0 + c, v:v + 1]),
                     reads=[tn, "gsc", "mod"], writes=[hname])

        epsb = sb(st, "epsb", [128, 1], F32)
        S.op("pool", lambda e: e.memset(epsb[:], EPS), writes=["epsb"])

        def phase1(l):
            NT = 512
            tl = tiles_of(NT)
            with ExitStack() as ps:
                w_sb = sb(ps, "w_in_sb", [128, 8, DIN], BF16)
                wv = I["w_in"][l].rearrange("(k p) n -> p k n", p=128)
                for j in range(10):
                    S.dma("pool", lambda e, j=j: e.dma_start(out=w_sb[:, :, j * 512:(j + 1) * 512], in_=wv[:, :, j * 512:(j + 1) * 512]),
                          writes=["w%d" % j])
                xtok = sb(ps, "xtok", [128, 4, D], F32)
                xTs = [sb(ps, "xT%d" % i, [128, 8, NT], F32) for i in range(2)]
                hTs = [sb(ps, "hT%d" % i, [128, 8, NT], BF16) for i in range(2)]
                sq = sb(ps, "sq", [128, 8, NT], BF16)
                rstd = sb(ps, "rstd", [128, NT], F32)
                tmpa = sb(ps, "tmpa", [128, NT], F32)
                tmps = [sb(ps, "tmps%d" % i, [128, NT], F32) for i in range(2)]
                stg = [sb(ps, "stg%d" % i, [128, 4, NT], BF16) for i in range(3)]
                sctr = [0]

                def stage():
                    i = sctr[0] % 3
                    sctr[0] += 1
                    return stg[i], "stg%d" % i

                XTv = R["XT"].rearrange("(c p) t -> p c t", p=128)

                def load(i):
                    t0, n, v = tl[i]
                    xT = xTs[i % 2]
                    xn = "xT%d" % (i % 2)
                    if l == 0:
                        src = I["ctx"] if v else I["x"]
                        r0 = t0 if v else t0 - LC
                        ns = n // 128
                        S.dma("sp", lambda e: e.dma_start(out=xtok[:, :ns, :], in_=src[r0:r0 + n, :].rearrange("(s p) d -> p s d", p=128)),
                              writes=["xtok"])
                        for s in range(ns):
                            for half in range(2):
                                pb, pbn = bank()
                                for cc in range(4):
                                    c = half * 4 + cc
                                    S.op("pe", lambda e, s=s, c=c, cc=cc, pb=pb: e.transpose(
                                        out=pb[:, cc * 128:(cc + 1) * 128], in_=xtok[:, s, c * 128:(c + 1) * 128], identity=ident[:]),
                                        reads=["xtok", "ident"], writes=[pbn])
                                eg = "dve"
                                S.op(eg, copy_op(eg, xT[:, half * 4:half * 4 + 4, s * 128:(s + 1) * 128],
                                                 pb[:, :].rearrange("p (c t) -> p c t", c=4)), reads=[pbn], writes=[xn])
                        S.dma("sp", lambda e: e.dma_start(out=XTv[:, :, t0:t0 + n], in_=xT[:, :, :n]), reads=[xn], writes=["XT%d" % i])
                    else:
                        S.dma("sp", lambda e: e.dma_start(out=xT[:, :, :n], in_=XTv[:, :, t0:t0 + n]), writes=[xn])

                def compute(i):
                    t0, n, v = tl[i]
                    xT = xTs[i % 2]
                    xn = "xT%d" % (i % 2)
                    hT = hTs[i % 2]
                    hn = "hT%d" % (i % 2)
                    norm_mod(l, 0, v, xT, n, sq, rstd, tmpa, tmps, hT, xn, hn, i)
                    ns = n // 128
                    for s in range(ns):
                        sg, sgn = stage()
                        pb, pbn = bank()
                        for c in range(8):
                            S.op("pe", lambda e, s=s, c=c, pb=pb: e.matmul(pb[:, 0:256], lhsT=hT[:, c, s * 128:(s + 1) * 128], rhs=w_sb[:, c, 0:256],
                                                                         start=(c == 0), stop=(c == 7)), reads=[hn, "w0"], writes=[pbn])
                        eg = evac_eng()
                        S.op(eg, copy_op(eg, sg[:, 0, 0:256], pb[:, 0:256]), reads=[pbn], writes=[sgn])
                        S.dma("sp", lambda e, s=s, sg=sg: e.dma_start(out=R["ZF"][t0 + s * 128:t0 + (s + 1) * 128, :], in_=sg[:, 0, 0:256]),
                              reads=[sgn], writes=["ZF"])
                        pb, pbn = bank()
                        for c in range(8):
                            S.op("pe", lambda e, s=s, c=c, pb=pb: e.matmul(pb[:, :], lhsT=hT[:, c, s * 128:(s + 1) * 128], rhs=w_sb[:, c, 1536:2048],
                                                                         start=(c == 0), stop=(c == 7)), reads=[hn, "w3"], writes=[pbn])
                        eg = evac_eng()
                        S.op(eg, copy_op(eg, sg[:, 1, :], pb[:, :]), reads=[pbn], writes=[sgn])
                        S.dma("sp", lambda e, s=s, sg=sg: e.dma_start(out=R["VV"][t0 + s * 128:t0 + (s + 1) * 128, :], in_=sg[:, 1, :]),
                              reads=[sgn], writes=["VV"])
                    groups = [("UT", 256, 2, False), ("QT", 512, 4, False), ("KT", 1024, 4, False)]
                    groups += [("GT%d" % g, 2048 + g * 512, 4, True) for g in range(6)]
                    for name, col0, nch, sig in groups:
                        sg, sgn = stage()
                        for q in range(nch):
                            pb, pbn = bank()
                            cs = col0 + q * 128
                            for c in range(8):
                                S.op("pe", lambda e, c=c, cs=cs, pb=pb: e.matmul(pb[:, :n], lhsT=w_sb[:, c, cs:cs + 128], rhs=hT[:, c, :n],
                                                                               start=(c == 0), stop=(c == 7)), reads=[hn, "w%d" % (cs // 512)], writes=[pbn])
                            if sig:
                                S.op("act", lambda e, q=q, pb=pb, sg=sg: e.activation(out=sg[:, q, :n], in_=pb[:, :n], func=AF.Sigmoid),
                                     reads=[pbn], writes=[sgn])
                            else:
                                eg = evac_eng()
                                S.op(eg, copy_op(eg, sg[:, q, :n], pb[:, :n]), reads=[pbn], writes=[sgn])
                        if sig:
                            g = int(name[2:])
                            dst = R["GT"].rearrange("(c p) t -> p c t", p=128)[:, g * 4:(g + 1) * 4, t0:t0 + n]
                        else:
                            dst = R[name].rearrange("(c p) t -> p c t", p=128)[:, :, t0:t0 + n]
                        S.dma("sp", lambda e, sg=sg, dst=dst, nch=nch: e.dma_start(out=dst, in_=sg[:, :nch, :n]), reads=[sgn], writes=[name])

                load(0)
                for i in range(len(tl)):
                    if i + 1 < len(tl):
                        load(i + 1)
                    compute(i)
                S.flush()


        def phase2_fnet(l):
            with ExitStack() as ps:
                tc = sb(ps, "fn_tc", [128, 64, 128], BF16)
                tsn = sb(ps, "fn_tsn", [128, 64, 128], BF16)
                w64 = sb(ps, "fn_w64", [64, 3, 64], BF16)
                zf = sb(ps, "zf", [128, 64, 256], BF16)
                xo = sb(ps, "xo", [128, 4, L], BF16)
                ablk = [sb(ps, "ablk%d" % i, [64, 16, 512], BF16) for i in range(2)]
                stg = [sb(ps, "fstg%d" % i, [128, 512], BF16) for i in range(3)]
                for q4 in range(4):
                    S.dma("pool", lambda e, q4=q4: e.dma_start(out=tc[:, q4 * 16:(q4 + 1) * 16, :], in_=I["fn_tc"][:, q4 * 16:(q4 + 1) * 16, :]), writes=["tc%d" % q4])
                    S.dma("pool", lambda e, q4=q4: e.dma_start(out=tsn[:, q4 * 16:(q4 + 1) * 16, :], in_=I["fn_tsn"][:, q4 * 16:(q4 + 1) * 16, :]), writes=["tsn%d" % q4])
                S.dma("pool", lambda e: e.dma_start(out=w64[:], in_=I["fn_w64"].rearrange("a p m -> p a m")), writes=["w64"])
                zsrc = R["ZF"][LC:T, :].rearrange("(r c) ch -> r c ch", c=64)
                for q4 in range(4):
                    S.dma("sp", lambda e, q4=q4: e.dma_start(out=zf[:, q4 * 16:(q4 + 1) * 16, :], in_=zsrc[:, q4 * 16:(q4 + 1) * 16, :]), writes=["zf%d" % q4])
                AFv = R["AF"].rearrange("c (k m) -> c k m", m=512)
                if l < DEPTH - 1:
                    c256 = sb(ps, "c256", [128, 2, 2, 256], BF16)
                    zc = sb(ps, "zc", [128, 2, 256], BF16)
                    xc = sb(ps, "xc", [128, 4, 256], BF16)
                    for a in range(2):
                        S.dma("pool", lambda e, a=a: e.dma_start(out=c256[:, a, :, :], in_=I["fn_c256"][a].rearrange("(tc p) k -> p tc k", p=128)), writes=["c256"])
                    S.dma("sp", lambda e: e.dma_start(out=zc[:], in_=R["ZF"][0:LC, :].rearrange("(a p) ch -> p a ch", p=128)), writes=["zc"])
                    for ri in range(2):
                        for q in range(2):
                            pb, pbn = bank()
                            for t2 in range(2):
                                S.op("pe", lambda e, ri=ri, q=q, t2=t2, pb=pb: e.matmul(pb[:, 0:256], lhsT=zc[:, t2, q * 128:(q + 1) * 128], rhs=c256[:, ri, t2, :],
                                                                                     start=(t2 == 0), stop=(t2 == 1)), reads=["zc", "c256"], writes=[pbn])
                            eg = evac_eng()
                            S.op(eg, copy_op(eg, xc[:, ri * 2 + q, :], pb[:, 0:256]), reads=[pbn], writes=["xc"])
                    S.dma("sp", lambda e: e.dma_start(out=R["FA"].rearrange("(a p) t -> p a t", p=128)[:, :, 0:LC], in_=xc[:]), reads=["xc"], writes=["FAc"])
                for c in range(64):
                    pb, pbn = bank()
                    S.op("pe", lambda e, c=c, pb=pb: e.matmul(pb[:, 0:256], lhsT=tc[:, c, :], rhs=zf[:, c, :], start=True, stop=True),
                         reads=["tc%d" % (c // 16), "zf%d" % (c // 16)], writes=[pbn])
                    S.op("pe", lambda e, c=c, pb=pb: e.matmul(pb[:, 256:512], lhsT=tsn[:, c, :], rhs=zf[:, c, :], start=True, stop=True),
                         reads=["tsn%d" % (c // 16), "zf%d" % (c // 16)], writes=[pbn])
                    sg = stg[c % 3]
                    sgn = "fstg%d" % (c % 3)
                    eg = evac_eng()
                    S.op(eg, copy_op(eg, sg[:, :], pb[:, :]), reads=[pbn], writes=[sgn])
                    S.dma("sp", lambda e, c=c, sg=sg: e.dma_start(out=AFv[c], in_=sg[:, :]), reads=[sgn], writes=["AF"])
                xov = xo[:, :, :].rearrange("p a (k2 k1) -> p a k2 k1", k1=128)
                for kb in range(8):
                    ab = ablk[kb % 2]
                    abn = "ablk%d" % (kb % 2)
                    S.dma("sp", lambda e, kb=kb, ab=ab: e.dma_start(out=ab[:], in_=AFv[:, kb * 16:(kb + 1) * 16, :]), reads=["AF"], writes=[abn])
                    for k1l in range(16):
                        k1 = kb * 16 + k1l
                        pb, pbn = bank()
                        for q in range(2):
                            ar = ab[:, k1l, q * 128:(q + 1) * 128]
                            ai = ab[:, k1l, 256 + q * 128:256 + (q + 1) * 128]
                            o_r = pb[:, q * 64:(q + 1) * 64]
                            o_i = pb[:, (2 + q) * 64:(3 + q) * 64]
                            S.op("pe", lambda e, ar=ar, o_r=o_r: e.matmul(o_r, lhsT=ar, rhs=w64[:, 0, :], start=True, stop=False), reads=[abn, "w64"], writes=[pbn])
                            S.op("pe", lambda e, ai=ai, o_r=o_r: e.matmul(o_r, lhsT=ai, rhs=w64[:, 1, :], start=False, stop=True), reads=[abn, "w64"], writes=[pbn])
                            S.op("pe", lambda e, ai=ai, o_i=o_i: e.matmul(o_i, lhsT=ai, rhs=w64[:, 0, :], start=True, stop=False), reads=[abn, "w64"], writes=[pbn])
                            S.op("pe", lambda e, ar=ar, o_i=o_i: e.matmul(o_i, lhsT=ar, rhs=w64[:, 2, :], start=False, stop=True), reads=[abn, "w64"], writes=[pbn])
                        eg = "dve"
                        S.op(eg, copy_op(eg, xov[:, :, :, k1], pb[:, 0:256].rearrange("p (a k) -> p a k", a=4)), reads=[pbn], writes=["xo"])
                FAl = R["FA"].rearrange("(a p) t -> p a t", p=128)
                for a in range(4):
                    S.dma("sp", lambda e, a=a: e.dma_start(out=FAl[:, a, LC:T], in_=xo[:, a, :]), reads=["xo"], writes=["FA%d" % a])
                S.flush()

        def phase2_na(l):
            blocks = na_blocks()
            pats = {}
            for blks, pat in blocks:
                if pat not in pats:
                    pats[pat] = (len(pats), len(blks))
            npat = len(pats)
            with_ctx_q = (l < DEPTH - 1)
            scale = 0.125
            with ExitStack() as ps:
                cm = sb(ps, "na_cm", [128, 15, 64], F32)
                S.dma("sp", lambda e: e.dma_start(out=cm[:], in_=I["na_cm"]), writes=["cm"])
                gt = sb(ps, "na_g", [128, 15, 64], F32)
                cb = sb(ps, "na_cb", [128, 16, 64], F32)
                bias = sb(ps, "na_bias", [128, npat, 640], F32)
                kTs = [sb(ps, "kT%d" % i, [64, T], BF16) for i in range(2)]
                qTs = [sb(ps, "qT%d" % i, [64, T], BF16) for i in range(2)]
                v1s = [sb(ps, "v1_%d" % i, [128, 66, 65], BF16) for i in range(2)]
                ncs = [sb(ps, "ncT%d" % i, [64, T], BF16) for i in range(2)]
                scb = [sb(ps, "scb%d" % i, [128, 640], F32) for i in range(2)]
                PTs = [sb(ps, "PT%d" % i, [128, 7, 128], BF16) for i in range(2)]
                otk = [sb(ps, "otk%d" % i, [128, 64], F32) for i in range(2)]
                rec = [sb(ps, "rec%d" % i, [128, 1], F32) for i in range(2)]
                for i in range(2):
                    S.op("pool", lambda e, i=i: e.memset(v1s[i][:, :, 64:65], 1.0), writes=["v1o_%d" % i])
                S.op("pool", lambda e: e.memset(cb[:, 15, :], NEG), writes=["cbneg"])
                VVv = R["VV"].rearrange("(blk p) d -> p blk d", p=128)

                def load_head(h):
                    hb = h % 2
                    S.dma("sp", lambda e: e.dma_start(out=kTs[hb][:], in_=R["KT"][h * 64:(h + 1) * 64, :]), writes=["kT%d" % hb])
                    S.dma("sp", lambda e: e.dma_start(out=qTs[hb][:], in_=R["QT"][h * 64:(h + 1) * 64, :]), writes=["qT%d" % hb])
                    for part in range(3):
                        S.dma("sp", lambda e, part=part: e.dma_start(out=v1s[hb][:, part * 22:(part + 1) * 22, 0:64],
                                                                    in_=VVv[:, part * 22:(part + 1) * 22, h * 64:(h + 1) * 64]),
                              reads=["v1o_%d" % hb], writes=["v1_%d_%d" % (hb, part)])

                def head(h):
                    hb = h % 2
                    kT, qT, v1, ncT = kTs[hb], qTs[hb], v1s[hb], ncs[hb]
                    kn, qn, ncn = "kT%d" % hb, "qT%d" % hb, "ncT%d" % hb
                    vn = ["v1_%d_%d" % (hb, p) for p in range(3)]
                    S.dma("sp", lambda e: e.dma_start(out=gt[:], in_=I["rpbg"][l, h]), writes=["na_g"])
                    S.op("dve", lambda e: e.tensor_tensor(out=cb[:, 0:15, :], in0=gt[:], in1=cm[:], op=ALU.add), reads=["na_g", "cm"], writes=["cb"])
                    for pat, (pi, nb) in pats.items():
                        k = 0
                        for s_ in range(nb):
                            for a in range(2):
                                for b in range(2):
                                    d = pat[k]
                                    k += 1
                                    row = 15 if d is None else d
                                    eng = "dve" if (k % 2) else "pool"
                                    S.op(eng, lambda e, a=a, b=b, s_=s_, row=row, pi=pi: e.tensor_copy(
                                        out=bias[a * 64:(a + 1) * 64, pi, s_ * 128 + b * 64:s_ * 128 + (b + 1) * 64], in_=cb[a * 64:(a + 1) * 64, row, :]),
                                        reads=["cb", "cbneg"], writes=["bias"])
                    qblocks = ([0, 1] if with_ctx_q else []) + list(range(2, 66))
                    nq = len(qblocks)
                    info = {}

                    def stageA(qi):
                        tq = qblocks[qi]
                        jb = qi % 2
                        if tq >= 2:
                            blks, pat = blocks[tq - 2]
                            pi, nb = pats[pat]
                        else:
                            blks, nb, pi = [], 0, 0
                        qs = qT[:, tq * 128:(tq + 1) * 128]
                        big = pbig[jb]
                        bign = ["pb%d" % (2 * jb), "pb%d" % (2 * jb + 1)]
                        for s_, i in enumerate(blks):
                            tk = 2 + i
                            S.op("pe", lambda e, s_=s_, tk=tk, big=big, qs=qs: e.matmul(big[:, s_ * 128:(s_ + 1) * 128], lhsT=kT[:, tk * 128:(tk + 1) * 128], rhs=qs,
                                                                                      start=True, stop=True), reads=[kn, qn], writes=bign)
                        psc = pbanks[4 + jb]
                        pscn = "pb%d" % (4 + jb)
                        for s_ in range(2):
                            S.op("pe", lambda e, s_=s_, psc=psc, qs=qs: e.matmul(psc[:, s_ * 128:(s_ + 1) * 128], lhsT=kT[:, s_ * 128:(s_ + 1) * 128], rhs=qs,
                                                                               start=True, stop=True), reads=[kn, qn], writes=[pscn])
                        PT = PTs[jb]
                        ptn = "PT%d" % jb
                        scbj = scb[jb]
                        if nb:
                            S.op("dve", lambda e, big=big, nb=nb, pi=pi, scbj=scbj: e.scalar_tensor_tensor(out=scbj[:, :nb * 128], in0=big[:, :nb * 128], scalar=scale,
                                                                                                          in1=bias[:, pi, :nb * 128], op0=ALU.mult, op1=ALU.add),
                                 reads=bign + ["bias"], writes=["scb%d" % jb])
                            S.op("act", lambda e, nb=nb, PT=PT, scbj=scbj: e.activation(out=PT[:, 0:nb, :], in_=scbj[:, :nb * 128].rearrange("p (s q) -> p s q", q=128), func=AF.Exp),
                                 reads=["scb%d" % jb], writes=[ptn])
                        S.op("act", lambda e, psc=psc, PT=PT: e.activation(out=PT[:, 5:7, :], in_=psc[:, 0:256].rearrange("p (s q) -> p s q", q=128), func=AF.Exp, scale=scale),
                             reads=[pscn], writes=[ptn])
                        info[qi] = (tq, jb, blks)

                    def stageB(qi):
                        tq, jb, blks = info[qi]
                        PT = PTs[jb]
                        ptn = "PT%d" % jb
                        recj, otkj = rec[jb], otk[jb]
                        pso = pbanks[6 + jb]
                        pson = "pb%d" % (6 + jb)
                        klist = [(s_, 2 + i) for s_, i in enumerate(blks)] + [(5, 0), (6, 1)]
                        for n_, (slot, tk) in enumerate(klist):
                            first, lastk = (n_ == 0), (n_ == len(klist) - 1)
                            S.op("pe", lambda e, slot=slot, tk=tk, first=first, lastk=lastk, pso=pso, PT=PT: e.matmul(pso[:, 0:65], lhsT=PT[:, slot, :], rhs=v1[:, tk, :],
                                                                                                                    start=first, stop=lastk),
                                 reads=[ptn, vn[tk // 22], "v1o_%d" % hb], writes=[pson])
                        S.op("dve", lambda e, pso=pso, recj=recj: e.reciprocal(out=recj[:], in_=pso[:, 64:65]), reads=[pson], writes=["rec%d" % jb])
                        S.op("dve", lambda e, pso=pso, recj=recj, otkj=otkj: e.tensor_scalar(out=otkj[:], in0=pso[:, 0:64], scalar1=recj[:, 0:1], scalar2=None, op0=ALU.mult),
                             reads=[pson, "rec%d" % jb], writes=["otk%d" % jb])

                    def stageC(qi):
                        tq, jb, blks = info[qi]
                        otkj = otk[jb]
                        pso = pbanks[6 + jb]
                        pson = "pb%d" % (6 + jb)
                        S.op("pe", lambda e, pso=pso, otkj=otkj: e.transpose(out=pso[0:64, 128:256], in_=otkj[:], identity=ident[:]), reads=["otk%d" % jb], writes=[pson])
                        S.op("act", copy_op("act", ncT[:, tq * 128:(tq + 1) * 128], pso[0:64, 128:256]), reads=[pson], writes=[ncn])

                    for i in range(nq + 2):
                        if i < nq:
                            stageA(i)
                        if 0 <= i - 1 < nq:
                            stageB(i - 1)
                        if 0 <= i - 2 < nq:
                            stageC(i - 2)
                    t_lo = 0 if with_ctx_q else LC
                    S.dma("sp", lambda e: e.dma_start(out=R["NCT"][h * 64:(h + 1) * 64, t_lo:T], in_=ncT[:, t_lo:T]), reads=[ncn], writes=["NCT%d" % h])

                load_head(0)
                for h in range(8):
                    if h + 1 < 8:
                        load_head(h + 1)
                    head(h)
                S.flush()


        def phase2_s5(l):
            Q = T // 8
            PCS = [(0, 352), (352, 704), (704, 1056)]
            TWO_PI = float(2 * np.pi)
            MAGIC = 12582912.0
            with ExitStack() as po:
                bblk = sb(po, "bblk", [128, 32, 128], BF16)
                cblk = sb(po, "cblk", [128, 32, 128], BF16)
                dsum = sb(po, "dsum", [128, 16, 128], BF16)
                MU = sb(po, "MU", [128, 11, 2, 32], F32)
                jm = sb(po, "jm", [128, 128], F32)
                S.dma("sp", lambda e: e.dma_start(out=jm[:], in_=I["s5_jm"]), writes=["jm"])
                with ExitStack() as ps:
                    def t32(name):
                        return sb(ps, name, [128, 32], F32)
                    lr, li, ldt, dtv, x1, mag, ang = [t32(n) for n in ("lr", "li", "ldt", "dtv", "x1", "mag", "ang")]
                    s0, tq, nn_, red, sinv, cosv = [t32(n) for n in ("s0", "tq", "nn", "red", "sinv", "cosv")]
                    am1, den, rden, wr, wi, u1, u2, u3, u4 = [t32(n) for n in ("am1", "den", "rden", "wr", "wi", "u1", "u2", "u3", "u4")]
                    PW = sb(ps, "PW", [128, 9, 2, 32], F32)
                    IPW = sb(ps, "IPW", [128, 8, 2, 32], F32)
                    Y1 = sb(ps, "Y1", [128, 32, 16], F32)
                    Y2 = sb(ps, "Y2", [128, 32, 16], F32)
                    BB1 = sb(ps, "BB1", [128, 32, 16], F32)
                    BB2 = sb(ps, "BB2", [128, 32, 16], F32)
                    CC1 = sb(ps, "CC1", [128, 4, 128], F32)
                    CC2 = sb(ps, "CC2", [128, 4, 128], F32)
                    X1 = sb(ps, "X1", [128, 32, 16], F32)
                    X2 = sb(ps, "X2", [128, 32, 16], F32)
                    e1 = sb(ps, "e1", [128, 32, 16], F32)
                    e2 = sb(ps, "e2", [128, 32, 16], F32)
                    EBt = sb(ps, "EBt", [128, 32, 8, 16], F32)
                    ENt = sb(ps, "ENt", [128, 32, 8, 16], F32)
                    GPt = sb(ps, "GPt", [128, 32, 8, 16], F32)
                    CBt = sb(ps, "CBt", [128, 32, 8, 16], F32)
                    Dm = sb(ps, "Dm", [128, 32, 128], F32)
                    msk = sb(ps, "msk", [128, 2, 128], F32)
                    dsk = sb(ps, "dsk", [128, 16], F32)
                    for hf in range(2):
                        hs = slice(hf * 64, (hf + 1) * 64)
                        S.dma("sp", lambda e, hs=hs: e.dma_start(out=lr[hs, :], in_=I["s5_lam_re"][l].rearrange("d g p -> p (d g)"), **NCDMA), writes=["lr"])
                        S.dma("sp", lambda e, hs=hs: e.dma_start(out=li[hs, :], in_=I["s5_lam_im"][l].rearrange("d g p -> p (d g)"), **NCDMA), writes=["li"])
                    S.dma("sp", lambda e: e.dma_start(out=ldt[:], in_=I["s5_log_dt"][l].rearrange("d g -> (d g)").partition_broadcast(128)), writes=["ldt"])
                    bre = I["s5_b_re"][l].rearrange("d g p m -> p (d g) m")
                    bim = I["s5_b_im"][l].rearrange("d g p m -> p (d g) m")
                    S.dma("sp", lambda e: e.dma_start(out=Y1[0:64], in_=bre), writes=["Y1"])
                    S.dma("sp", lambda e: e.dma_start(out=Y1[64:128], in_=bim), writes=["Y1"])
                    S.dma("sp", lambda e: e.dma_start(out=Y2[0:64], in_=bim), writes=["Y2"])
                    S.dma("sp", lambda e: e.dma_start(out=Y2[64:128], in_=bre), writes=["Y2"])
                    cre = I["s5_c_re"][l].rearrange("d g n p -> (d g n) p")
                    cim = I["s5_c_im"][l].rearrange("d g n p -> (d g n) p")
                    for ch in range(4):
                        rs_ = slice(ch * 128, (ch + 1) * 128)
                        S.dma("sp", lambda e, ch=ch, rs_=rs_: e.dma_start(out=CC1[:, ch, 0:64], in_=cre[rs_, :]), writes=["CC1"])
                        S.dma("sp", lambda e, ch=ch, rs_=rs_: e.dma_start(out=CC1[:, ch, 64:128], in_=cim[rs_, :]), writes=["CC1"])
                        S.dma("sp", lambda e, ch=ch, rs_=rs_: e.dma_start(out=CC2[:, ch, 0:64], in_=cim[rs_, :]), writes=["CC2"])
                        S.dma("sp", lambda e, ch=ch, rs_=rs_: e.dma_start(out=CC2[:, ch, 64:128], in_=cre[rs_, :]), writes=["CC2"])
                    S.dma("sp", lambda e: e.dma_start(out=msk[:], in_=I["s5_mask"].rearrange("d p m -> p d m")), writes=["msk"])
                    for s_ in range(8):
                        S.dma("sp", lambda e, s_=s_: e.dma_start(out=dsk[s_ * 16:(s_ + 1) * 16, :], in_=I["s5_d"][l].rearrange("g m -> m g"), **NCDMA), writes=["dsk"])

                    if CUT < 10:
                        S.flush()
                        return
                    def V(fn, reads, writes, eng="dve"):
                        S.op(eng, fn, reads=reads, writes=writes)

                    def tt(o, a, b, op, on, an, bn):
                        V(lambda e: e.tensor_tensor(out=o, in0=a, in1=b, op=op), [an, bn], [on])

                    def ts(o, a, s1, op0, on, an, s2=None, op1=None):
                        if op1 is None:
                            V(lambda e: e.tensor_scalar(out=o, in0=a, scalar1=s1, scalar2=None, op0=op0), [an], [on])
                        else:
                            V(lambda e: e.tensor_scalar(out=o, in0=a, scalar1=s1, scalar2=s2, op0=op0, op1=op1), [an], [on])

                    V(lambda e: e.tensor_scalar(out=Y2[0:64], in0=Y2[0:64], scalar1=-1.0, scalar2=None, op0=ALU.mult), ["Y2"], ["Y2"])
                    ts(lr[:], lr[:], -1e-4, ALU.min, "lr", "lr")
                    V(lambda e: e.activation(out=dtv[:], in_=ldt[:], func=AF.Exp), ["ldt"], ["dtv"], eng="act")
                    tt(x1[:], lr[:], dtv[:], ALU.mult, "x1", "lr", "dtv")
                    V(lambda e: e.activation(out=mag[:], in_=x1[:], func=AF.Exp), ["x1"], ["mag"], eng="act")
                    tt(ang[:], li[:], dtv[:], ALU.mult, "ang", "li", "dtv")

                    def sinred(dst, dn, shift):
                        ts(s0[:], ang[:], shift, ALU.add, "s0", "ang")
                        ts(tq[:], s0[:], 1.0 / TWO_PI, ALU.mult, "tq", "s0")
                        ts(nn_[:], tq[:], MAGIC, ALU.add, "nn", "tq")
                        ts(tq[:], nn_[:], -MAGIC, ALU.add, "tq", "nn")
                        V(lambda e: e.scalar_tensor_tensor(out=red[:], in0=tq[:], scalar=-TWO_PI, in1=s0[:], op0=ALU.mult, op1=ALU.add), ["tq", "s0"], ["red"])
                        ts(red[:], red[:], -3.1415925, ALU.max, "red", "red", 3.1415925, ALU.min)
                        V(lambda e: e.activation(out=dst, in_=red[:], func=AF.Sin), ["red"], [dn], eng="act")

                    sinred(sinv[:], "sinv", 0.0)
                    sinred(cosv[:], "cosv", float(np.pi / 2))
                    A1 = PW[:, 1, 0, :]
                    B1 = PW[:, 1, 1, :]
                    tt(A1, mag[:], cosv[:], ALU.mult, "PW", "mag", "cosv")
                    tt(B1, mag[:], sinv[:], ALU.mult, "PW", "mag", "sinv")
                    V(lambda e: e.memset(PW[:, 0, 0, :], 1.0), [], ["PW"], eng="pool")
                    V(lambda e: e.memset(PW[:, 0, 1, :], 0.0), [], ["PW"], eng="pool")
                    V(lambda e: e.memset(IPW[:, 0, 0, :], 1.0), [], ["IPW"], eng="pool")
                    V(lambda e: e.memset(IPW[:, 0, 1, :], 0.0), [], ["IPW"], eng="pool")
                    ts(am1[:], A1, -1.0, ALU.add, "am1", "PW")
                    tt(u1[:], lr[:], lr[:], ALU.mult, "u1", "lr", "lr")
                    tt(u2[:], li[:], li[:], ALU.mult, "u2", "li", "li")
                    tt(den[:], u1[:], u2[:], ALU.add, "den", "u1", "u2")
                    V(lambda e: e.reciprocal(out=rden[:], in_=den[:]), ["den"], ["rden"])
                    tt(u1[:], am1[:], lr[:], ALU.mult, "u1", "am1", "lr")
                    tt(u2[:], B1, li[:], ALU.mult, "u2", "PW", "li")
                    tt(u3[:], u1[:], u2[:], ALU.add, "u3", "u1", "u2")
                    tt(wr[:], u3[:], rden[:], ALU.mult, "wr", "u3", "rden")
                    tt(u1[:], B1, lr[:], ALU.mult, "u1", "PW", "lr")
                    tt(u2[:], am1[:], li[:], ALU.mult, "u2", "am1", "li")
                    tt(u3[:], u1[:], u2[:], ALU.subtract, "u3", "u1", "u2")
                    tt(wi[:], u3[:], rden[:], ALU.mult, "wi", "u3", "rden")

                    def bc(t):
                        return t.unsqueeze(2).broadcast_to([128, 32, 16])

                    tt(e1[:], Y1[:], bc(wr[:, :]), ALU.mult, "e1", "Y1", "wr")
                    tt(e2[:], Y2[:], bc(wi[:, :]), ALU.mult, "e2", "Y2", "wi")
                    tt(BB1[:], e1[:], e2[:], ALU.add, "BB1", "e1", "e2")
                    tt(e1[:], Y2[:], bc(wr[:, :]), ALU.mult, "e1", "Y2", "wr")
                    tt(e2[:], Y1[:], bc(wi[:, :]), ALU.mult, "e2", "Y1", "wi")
                    tt(BB2[:], e1[:], e2[:], ALU.subtract, "BB2", "e1", "e2")

                    def cmul(oa, ob, a, b, c_, d_, on, rn):
                        tt(u1[:], a, c_, ALU.mult, "u1", rn[0], rn[1])
                        tt(u2[:], b, d_, ALU.mult, "u2", rn[0], rn[1])
                        tt(u3[:], a, d_, ALU.mult, "u3", rn[0], rn[1])
                        tt(u4[:], b, c_, ALU.mult, "u4", rn[0], rn[1])
                        tt(oa, u1[:], u2[:], ALU.subtract, on, "u1", "u2")
                        tt(ob, u3[:], u4[:], ALU.add, on, "u3", "u4")

                    for k in range(1, 8):
                        cmul(PW[:, k + 1, 0, :], PW[:, k + 1, 1, :], PW[:, k, 0, :], PW[:, k, 1, :], A1, B1, "PW", ("PW", "PW"))
                    tt(u1[:], A1, A1, ALU.mult, "u1", "PW", "PW")
                    tt(u2[:], B1, B1, ALU.mult, "u2", "PW", "PW")
                    tt(den[:], u1[:], u2[:], ALU.add, "den", "u1", "u2")
                    V(lambda e: e.reciprocal(out=rden[:], in_=den[:]), ["den"], ["rden"])
                    tt(IPW[:, 1, 0, :], A1, rden[:], ALU.mult, "IPW", "PW", "rden")
                    V(lambda e: e.scalar_tensor_tensor(out=IPW[:, 1, 1, :], in0=B1, scalar=-1.0, in1=rden[:], op0=ALU.mult, op1=ALU.mult), ["PW", "rden"], ["IPW"])
                    for k in range(1, 7):
                        cmul(IPW[:, k + 1, 0, :], IPW[:, k + 1, 1, :], IPW[:, k, 0, :], IPW[:, k, 1, :], IPW[:, 1, 0, :], IPW[:, 1, 1, :], "IPW", ("IPW", "IPW"))
                    V(lambda e: e.tensor_copy(out=MU[:, 0, :, :], in_=PW[:, 8, :, :]), ["PW"], ["MU"])
                    for j in range(10):
                        cmul(MU[:, j + 1, 0, :], MU[:, j + 1, 1, :], MU[:, j, 0, :], MU[:, j, 1, :], MU[:, j, 0, :], MU[:, j, 1, :], "MU", ("MU", "MU"))
                    if CUT < 11:
                        S.flush()
                        return
                    for (CC, XX, ccn, xn) in ((CC1, X1, "CC1", "X1"), (CC2, X2, "CC2", "X2")):
                        for ch in range(4):
                            pb, pbn = bank()
                            S.op("pe", lambda e, CC=CC, ch=ch, pb=pb: e.transpose(out=pb[:, 0:128], in_=CC[:, ch, :], identity=ident[:]), reads=[ccn], writes=[pbn])
                            V(lambda e, XX=XX, ch=ch, pb=pb: e.tensor_copy(out=XX[:, ch * 8:(ch + 1) * 8, :], in_=pb[:, 0:128].rearrange("p (a n) -> p a n", n=16)), [pbn], [xn])
                    V(lambda e: e.tensor_scalar(out=X1[64:128], in0=X1[64:128], scalar1=-1.0, scalar2=None, op0=ALU.mult), ["X1"], ["X1"])
                    V(lambda e: e.tensor_scalar(out=X2[:], in0=X2[:], scalar1=-1.0, scalar2=None, op0=ALU.mult), ["X2"], ["X2"])

                    def table(dst, dn, M1, M2, m1n, m2n, pa, pb_, pn, sig):
                        tt(e1[:], M1[:], bc(pa), ALU.mult, "e1", m1n, pn)
                        tt(e2[:], M2[:], bc(pb_), ALU.mult, "e2", m2n, pn)
                        tt(dst[:, 0:16, sig, :], e1[:, 0:16, :], e2[:, 0:16, :], ALU.add, dn, "e1", "e2")
                        tt(dst[:, 16:32, 7 - sig, :], e1[:, 16:32, :], e2[:, 16:32, :], ALU.add, dn, "e1", "e2")

                    if CUT < 12:
                        S.flush()
                        return
                    for sig in range(8):
                        table(EBt, "EBt", BB1, BB2, "BB1", "BB2", PW[:, 7 - sig, 0, :], PW[:, 7 - sig, 1, :], "PW", sig)
                        table(ENt, "ENt", BB1, BB2, "BB1", "BB2", IPW[:, sig, 0, :], IPW[:, sig, 1, :], "IPW", sig)
                        table(GPt, "GPt", X1, X2, "X1", "X2", PW[:, sig, 0, :], PW[:, sig, 1, :], "PW", sig)
                        table(CBt, "CBt", X1, X2, "X1", "X2", PW[:, sig + 1, 0, :], PW[:, sig + 1, 1, :], "PW", sig)
                    if CUT < 13:
                        S.flush()
                        return
                    if DEBUG == 3:
                        dbt = sb(ps, "dbt3", [128, 2048], F32)
                        S.op("pool", lambda e: e.memset(dbt[:], 0.0), writes=["dbt"])
                        srcs = [X1[:, 0:8, :].rearrange("p a n -> p (a n)"), CBt[:, 0, :, :].rearrange("p s n -> p (s n)"), GPt[:, 0, :, :].rearrange("p s n -> p (s n)"),
                                BB1[:, 0:8, :].rearrange("p a n -> p (a n)"), EBt[:, 0, :, :].rearrange("p s n -> p (s n)"), Y1[:, 0:8, :].rearrange("p a n -> p (a n)"),
                                X2[:, 0:8, :].rearrange("p a n -> p (a n)"), e1[:, 0:8, :].rearrange("p a n -> p (a n)")]
                        for i_, src_ in enumerate(srcs):
                            S.op("dve", lambda e, i_=i_, src_=src_: e.tensor_copy(out=dbt[:, i_ * 128:(i_ + 1) * 128], in_=src_), reads=["dbt", "X1", "X2", "CBt", "GPt", "BB1", "EBt", "Y1", "e1"], writes=["dbt"])
                        S.op("dve", lambda e: e.tensor_copy(out=dbt[:, 1024:1024 + 576], in_=PW[:].rearrange("p a b c -> p (a b c)")), reads=["dbt", "PW"], writes=["dbt"])
                        S.dma("sp", lambda e: e.dma_start(out=R["DBG"], in_=dbt[:]), reads=["dbt"], writes=["DBG"])
                        S.flush()
                        return
                    V(lambda e: e.tensor_copy(out=cblk[:].rearrange("p a b -> p (a b)"), in_=CBt[:].rearrange("p a s n -> p (a s n)")), ["CBt"], ["cblk"])
                    if CUT < 15:
                        S.flush()
                        return
                    for dg in range(32):
                        pb, pbn = bank()
                        S.op("pe", lambda e, dg=dg, pb=pb: e.transpose(out=pb[:, 0:128], in_=EBt[:, dg, :, :].rearrange("p s m -> p (s m)"), identity=ident[:]),
                             reads=["EBt"], writes=[pbn])
                        eg = evac_eng()
                        S.op(eg, copy_op(eg, bblk[:, dg, :], pb[:, 0:128]), reads=[pbn], writes=["bblk"])
                        if CUT < 16:
                            continue
                        pb2, pbn2 = bank()
                        S.op("pe", lambda e, dg=dg, pb2=pb2: e.matmul(pb2[:, 0:128], lhsT=ENt[:, dg, :, :].rearrange("p s m -> p (s m)"),
                                                                   rhs=GPt[:, dg, :, :].rearrange("p s m -> p (s m)"), start=True, stop=True),
                             reads=["ENt", "GPt"], writes=[pbn2])
                        V(lambda e, dg=dg, pb2=pb2: e.tensor_tensor(out=Dm[:, dg, :], in0=pb2[:, 0:128], in1=msk[:, dg // 16, :], op=ALU.mult), [pbn2, "msk"], ["Dm"])
                    for g in range(16 if CUT >= 17 else 0):
                        V(lambda e, g=g: e.tensor_tensor(out=Dm[:, g, :], in0=Dm[:, g, :], in1=Dm[:, 16 + g, :], op=ALU.add), ["Dm"], ["Dm"])
                        V(lambda e, g=g: e.scalar_tensor_tensor(out=dsum[:, g, :], in0=ident[:], scalar=dsk[:, g:g + 1], in1=Dm[:, g, :], op0=ALU.mult, op1=ALU.add),
                          ["Dm", "dsk"], ["dsum"])
                    S.flush()

                if CUT < 18:
                    return
                if DEBUG == 2:
                    with ExitStack() as pd:
                        dbt = sb(pd, "dbt2", [128, 2048], F32)
                        S.op("pool", lambda e: e.memset(dbt[:], 0.0), writes=["dbt"])
                        for i_, src_ in enumerate([dsum[:, 0, :], bblk[:, 0, :], cblk[:, 0, :], bblk[:, 16, :], cblk[:, 16, :], dsum[:, 5, :]]):
                            S.op("dve", lambda e, i_=i_, src_=src_: e.tensor_copy(out=dbt[:, i_ * 128:(i_ + 1) * 128], in_=src_), reads=["dbt"], writes=["dbt"])
                        S.op("dve", lambda e: e.tensor_copy(out=dbt[:, 768:768 + 704], in_=MU[:].rearrange("p a b c -> p (a b c)")), reads=["dbt"], writes=["dbt"])
                        S.dma("sp", lambda e: e.dma_start(out=R["DBG"], in_=dbt[:]), reads=["dbt"], writes=["DBG"])
                        S.flush()
                with ExitStack() as ps:
                    selin = sb(ps, "selin", [128, 8, 8, 128], BF16)
                    selout = sb(ps, "selout", [128, 8, 8, 128], BF16)
                    for g in range(8):
                        S.dma("pool", lambda e, g=g: e.dma_start(out=selin[:, g, :, :], in_=I["s5_selin"][:, g, :, :]), writes=["selin"])
                        S.dma("pool", lambda e, g=g: e.dma_start(out=selout[:, g, :, :], in_=I["s5_selout"][:, g, :, :]), writes=["selout"])
                    wglu = sb(ps, "wglu", [128, 2, 256], BF16)
                    S.dma("pool", lambda e: e.dma_start(out=wglu[:], in_=I["s5_w_glu"][l].rearrange("(k p) n -> p k n", p=128)), writes=["wglu"])
                    uT = sb(ps, "uT", [128, T], BF16)
                    Vg = sb(ps, "Vg", [128, Q], BF16)
                    VRg = sb(ps, "VRg", [128, Q], BF16)
                    Hm = [sb(ps, "Hm%d" % d, [128, Q], F32) for d in range(2)]
                    Hs = [sb(ps, "Hs%d" % d, [128, Q + 2], BF16) for d in range(2)]
                    HinN = sb(ps, "HinN", [128, Q], BF16)
                    Rt = [sb(ps, "Rt%d" % i, [128, 128], F32) for i in range(2)]
                    Rb = [[sb(ps, "Rb%d_%d" % (d, i), [128, 128], BF16) for i in range(2)] for d in range(2)]
                    Yg = [sb(ps, "Yg%d" % g, [128, Q], BF16) for g in range(8)]
                    yT = sb(ps, "yT", [128, T], F32)
                    gbf = sb(ps, "gbf", [128, 2, T], BF16)
                    gl = [sb(ps, "gl%d" % i, [128, 512], F32) for i in range(3)]
                    sgt = sb(ps, "sgt", [128, 512], BF16)
                    sbo = [sb(ps, "sbo%d" % i, [128, 512], BF16) for i in range(2)]
                    for d in range(2):
                        S.op("pool", lambda e, d=d: e.memset(Hs[d][:, 0:2], 0.0), writes=["Hs%d" % d])
                    uTv = uT[:, :].rearrange("p (c s) -> p c s", s=8)
                    yTv = yT[:, :].rearrange("p (c t) -> p c t", t=8)
                    rbc = [0]
                    for hf in range(2):
                        S.dma("sp", lambda e, hf=hf: e.dma_start(out=uT[:], in_=R["UT"][hf * 128:(hf + 1) * 128, :]), writes=["uT"])
                        for g in range(8):
                            gi = hf * 8 + g
                            for (c0, c1) in PCS:
                                pb, pbn = bank()
                                for s_ in range(8):
                                    S.op("pe", lambda e, g=g, s_=s_, pb=pb, c0=c0, c1=c1: e.matmul(pb[:, 0:c1 - c0], lhsT=selin[:, g, s_, :], rhs=uTv[:, c0:c1, s_],
                                                                                                 start=(s_ == 0), stop=(s_ == 7)), reads=["selin", "uT"], writes=[pbn])
                                eg = evac_eng()
                                S.op(eg, copy_op(eg, Vg[:, c0:c1], pb[:, 0:c1 - c0]), reads=[pbn], writes=["Vg"])
                            S.op("pool", lambda e: e.tensor_copy(out=VRg[:, 0:32], in_=Vg[:, 31::-1]), reads=["Vg"], writes=["VRg"])
                            S.op("pool", lambda e: e.tensor_copy(out=VRg[:, 32:Q], in_=Vg[:, Q - 1:31:-1]), reads=["Vg"], writes=["VRg"])
                            for d in range(2):
                                dg = d * 16 + gi
                                src = Vg if d == 0 else VRg
                                srcn = "Vg" if d == 0 else "VRg"
                                hm, hs = Hm[d], Hs[d]
                                hmn, hsn = "Hm%d" % d, "Hs%d" % d
                                for (c0, c1) in PCS:
                                    pb, pbn = bank()
                                    S.op("pe", lambda e, dg=dg, pb=pb, c0=c0, c1=c1, src=src: e.matmul(pb[:, 0:c1 - c0], lhsT=bblk[:, dg, :], rhs=src[:, c0:c1], start=True, stop=True),
                                         reads=["bblk", srcn], writes=[pbn])
                                    S.op("dve", lambda e, pb=pb, c0=c0, c1=c1, hm=hm: e.tensor_copy(out=hm[:, c0:c1], in_=pb[:, 0:c1 - c0]), reads=[pbn], writes=[hmn])
                                    S.op("act", lambda e, c0=c0, c1=c1, hs=hs, hm=hm: e.activation(out=hs[:, 2 + c0:2 + c1], in_=hm[:, c0:c1], func=AF.Copy), reads=[hmn], writes=[hsn])
                            for j in range(11):
                                sh = 1 << j
                                pieces = []
                                q0 = sh
                                while q0 < Q:
                                    q1 = min(q0 + 512, Q)
                                    pieces.append((q0, q1))
                                    q0 = q1
                                pend = []
                                for d in range(2):
                                    dg = d * 16 + gi
                                    hm, hs = Hm[d], Hs[d]
                                    hmn, hsn = "Hm%d" % d, "Hs%d" % d
                                    rb = Rb[d][j % 2]
                                    rbn = "Rb%d_%d" % (d, j % 2)
                                    rt = Rt[d]
                                    rtn = "Rt%d" % d
                                    S.op("dve", lambda e, j=j, dg=dg, rt=rt: e.tensor_scalar(out=rt[:], in0=ident[:], scalar1=MU[:, j, 0, dg:dg + 1], scalar2=None, op0=ALU.mult),
                                         reads=["MU"], writes=[rtn])
                                    S.op("dve", lambda e, j=j, dg=dg, rb=rb, rt=rt: e.scalar_tensor_tensor(out=rb[:], in0=jm[:], scalar=MU[:, j, 1, dg:dg + 1], in1=rt[:], op0=ALU.mult, op1=ALU.add),
                                         reads=["MU", rtn, "jm"], writes=[rbn])
                                    pbs = []
                                    for (q0, q1) in pieces:
                                        pb, pbn = bank()
                                        pbs.append((pb, pbn))
                                        S.op("pe", lambda e, pb=pb, q0=q0, q1=q1, sh=sh, rb=rb, hs=hs: e.matmul(pb[:, 0:q1 - q0], lhsT=rb[:], rhs=hs[:, 2 + q0 - sh:2 + q1 - sh], start=True, stop=True),
                                             reads=[rbn, hsn], writes=[pbn])
                                    pend.append((hm, hs, hmn, hsn, pbs))
                                for (hm, hs, hmn, hsn, pbs) in pend:
                                    for (q0, q1), (pb, pbn) in zip(pieces, pbs):
                                        S.op("dve", lambda e, pb=pb, q0=q0, q1=q1, hm=hm: e.tensor_tensor(out=hm[:, q0:q1], in0=pb[:, 0:q1 - q0], in1=hm[:, q0:q1], op=ALU.add),
                                             reads=[pbn, hmn], writes=[hmn])
                                    for (a0, a1) in ((0, 512), (512, 1024), (1024, Q)):
                                        if a1 <= sh:
                                            continue
                                        S.op("act", lambda e, a0=a0, a1=a1, hm=hm, hs=hs: e.activation(out=hs[:, 2 + a0:2 + a1], in_=hm[:, a0:a1], func=AF.Copy), reads=[hmn], writes=[hsn])
                            S.op("pool", lambda e: e.tensor_copy(out=HinN[:, 0:32], in_=Hs[1][:, 32:0:-1]), reads=["Hs1"], writes=["HinN"])
                            S.op("pool", lambda e: e.tensor_copy(out=HinN[:, 32:Q], in_=Hs[1][:, Q:32:-1]), reads=["Hs1"], writes=["HinN"])
                            yg = Yg[g]
                            for (c0, c1) in (PCS if CUT >= 23 else []):
                                pb, pbn = bank()
                                S.op("pe", lambda e, gi=gi, pb=pb, c0=c0, c1=c1: e.matmul(pb[:, 0:c1 - c0], lhsT=dsum[:, gi, :], rhs=Vg[:, c0:c1], start=True, stop=False),
                                     reads=["dsum", "Vg"], writes=[pbn])
                                S.op("pe", lambda e, gi=gi, pb=pb, c0=c0, c1=c1: e.matmul(pb[:, 0:c1 - c0], lhsT=cblk[:, gi, :], rhs=Hs[0][:, 1 + c0:1 + c1], start=False, stop=False),
                                     reads=["cblk", "Hs0"], writes=[pbn])
                                S.op("pe", lambda e, gi=gi, pb=pb, c0=c0, c1=c1: e.matmul(pb[:, 0:c1 - c0], lhsT=cblk[:, 16 + gi, :], rhs=HinN[:, c0:c1], start=False, stop=True),
                                     reads=["cblk", "HinN"], writes=[pbn])
                                eg = evac_eng()
                                S.op(eg, copy_op(eg, yg[:, c0:c1], pb[:, 0:c1 - c0]), reads=[pbn], writes=["Yg%d" % g])
                        for t_ in range(8 if CUT >= 24 else 0):
                            for (c0, c1) in PCS:
                                pb, pbn = bank()
                                for g in range(8):
                                    S.op("pe", lambda e, g=g, t_=t_, pb=pb, c0=c0, c1=c1: e.matmul(pb[:, 0:c1 - c0], lhsT=selout[:, t_, g, :], rhs=Yg[g][:, c0:c1],
                                                                                                 start=(g == 0), stop=(g == 7)), reads=["selout", "Yg%d" % g], writes=[pbn])
                                eg = evac_eng()
                                S.op(eg, copy_op(eg, yTv[:, c0:c1, t_], pb[:, 0:c1 - c0]), reads=[pbn], writes=["yT"])
                        if DEBUG:
                            S.dma("sp", lambda e, hf=hf: e.dma_start(out=R["YT"][hf * 128:(hf + 1) * 128, :], in_=yT[:]), reads=["yT"], writes=["YTd"])
                        for i, (t0, n, v) in enumerate(tiles_of(512) if CUT >= 25 else []):
                            a_, b_, c_ = gl
                            ys = yT[:, t0:t0 + n]
                            S.op("pool", lambda e, ys=ys, n=n: e.tensor_tensor(out=a_[:, :n], in0=ys, in1=ys, op=ALU.mult), reads=["yT"], writes=["gl0"])
                            S.op("dve", lambda e, n=n: e.tensor_scalar(out=b_[:, :n], in0=a_[:, :n], scalar1=0.044715, scalar2=1.0, op0=ALU.mult, op1=ALU.add), reads=["gl0"], writes=["gl1"])
                            S.op("pool", lambda e, ys=ys, n=n: e.tensor_tensor(out=a_[:, :n], in0=b_[:, :n], in1=ys, op=ALU.mult), reads=["gl1", "yT"], writes=["gl0"])
                            S.op("act", lambda e, n=n: e.activation(out=c_[:, :n], in_=a_[:, :n], func=AF.Sigmoid, scale=1.5957691216), reads=["gl0"], writes=["gl2"])
                            S.op("dve", lambda e, ys=ys, n=n, t0=t0, hf=hf: e.tensor_tensor(out=gbf[:, hf, t0:t0 + n], in0=c_[:, :n], in1=ys, op=ALU.mult), reads=["gl2", "yT"], writes=["gbf"])
                    SBv = R["SB"].rearrange("(c p) t -> p c t", p=128)
                    for i, (t0, n, v) in enumerate(tiles_of(512) if CUT >= 26 else []):
                        for oc in range(2):
                            pb, pbn = bank()
                            for kc in range(2):
                                S.op("pe", lambda e, oc=oc, kc=kc, pb=pb, t0=t0, n=n: e.matmul(pb[:, :n], lhsT=wglu[:, kc, oc * 128:(oc + 1) * 128], rhs=gbf[:, kc, t0:t0 + n],
                                                                                             start=(kc == 0), stop=(kc == 1)), reads=["wglu", "gbf"], writes=[pbn])
                            S.op("act", lambda e, pb=pb, n=n: e.activation(out=sgt[:, :n], in_=pb[:, :n], func=AF.Sigmoid), reads=[pbn], writes=["sgt"])
                            so = sbo[oc]
                            S.op("dve", lambda e, so=so, oc=oc, t0=t0, n=n: e.tensor_tensor(out=so[:, :n], in0=sgt[:, :n], in1=gbf[:, oc, t0:t0 + n], op=ALU.mult),
                                 reads=["sgt", "gbf"], writes=["sbo%d" % oc])
                            S.dma("sp", lambda e, so=so, oc=oc, t0=t0, n=n: e.dma_start(out=SBv[:, oc, t0:t0 + n], in_=so[:, :n]), reads=["sbo%d" % oc], writes=["SBo"])
                    S.flush()

        def phase3a(l):
            NT = 512
            tl = tiles_of(NT, with_ctx=(l < DEPTH - 1))
            with ExitStack() as ps:
                wa = sb(ps, "wa", [128, 4, D], BF16)
                wbra = sb(ps, "wbra", [128, 2, D], BF16)
                wb = sb(ps, "wb", [128, 2, D], BF16)
                wc = sb(ps, "wc", [128, 4, D], BF16)
                wo = sb(ps, "wo", [128, 8, D], BF16)
                cs64 = sb(ps, "cs64", [128, 2, 128], BF16)
                S.dma("pool", lambda e: e.dma_start(out=cs64[:], in_=I["cs64"].rearrange("a p m -> p a m")), writes=["cs64"])
                S.dma("pool", lambda e: e.dma_start(out=wbra[:], in_=I["w_br_a"][l].rearrange("(k p) n -> p k n", p=128)), writes=["wbra"])
                S.dma("pool", lambda e: e.dma_start(out=wb[:], in_=I["w_br_b"][l].rearrange("(k p) n -> p k n", p=128)), writes=["wb"])
                S.dma("pool", lambda e: e.dma_start(out=wc[:], in_=I["w_br_c"][l].rearrange("(k p) n -> p k n", p=128)), writes=["wc"])
                S.dma("pool", lambda e: e.dma_start(out=wo[:], in_=I["w_out"][l].rearrange("(k p) n -> p k n", p=128)), writes=["wo"])
                for a in range(2 if CUT > -1 else 0):
                    for q in range(2):
                        for hh in range(2):
                            pb, pbn = bank()
                            S.op("pe", lambda e, a=a, q=q, hh=hh, pb=pb: e.matmul(pb[:, :], lhsT=cs64[:, a, :], rhs=wbra[:, q, hh * 512:(hh + 1) * 512],
                                                                                 start=True, stop=True), reads=["cs64", "wbra"], writes=[pbn])
                            eg = evac_eng()
                            S.op(eg, copy_op(eg, wa[:, a * 2 + q, hh * 512:(hh + 1) * 512], pb[:, :]), reads=[pbn], writes=["wa"])
                xTs = [sb(ps, "xT%d" % i, [128, 8, NT], F32) for i in range(2)]
                fsn = [sb(ps, "fsn%d" % i, [128, 10, NT], BF16) for i in range(2)]
                gts = [sb(ps, "gts%d" % i, [128, 24, NT], BF16) for i in range(2)]
                mT = sb(ps, "mT", [128, 8, NT], BF16)
                hT = sb(ps, "hT", [128, 8, NT], BF16)
                sq = sb(ps, "sq", [128, 8, NT], BF16)
                rstd = sb(ps, "rstd", [128, NT], F32)
                tmpa = sb(ps, "tmpa", [128, NT], F32)
                tmps = [sb(ps, "tmps%d" % i, [128, NT], F32) for i in range(2)]
                t1s = [sb(ps, "t1_%d" % i, [128, NT], F32) for i in range(2)]
                t2s = [sb(ps, "t2_%d" % i, [128, NT], F32) for i in range(2)]
                t3s = [sb(ps, "t3_%d" % i, [128, NT], F32) for i in range(2)]
                XTv = R["XT"].rearrange("(c p) t -> p c t", p=128)
                XMv = R["XM"].rearrange("(c p) t -> p c t", p=128)
                H2v = R["H2"].rearrange("(c p) t -> p c t", p=128)
                FAv = R["FA"].rearrange("(c p) t -> p c t", p=128)
                SBv = R["SB"].rearrange("(c p) t -> p c t", p=128)
                NCv = R["NCT"].rearrange("(c p) t -> p c t", p=128)
                GTv = R["GT"].rearrange("(c p) t -> p c t", p=128)

                def load(i):
                    t0, n, v = tl[i]
                    b = i % 2
                    S.dma("sp", lambda e: e.dma_start(out=xTs[b][:, :, :n], in_=XTv[:, :, t0:t0 + n]), writes=["xT%d" % b])
                    S.dma("sp", lambda e: e.dma_start(out=fsn[b][:, 0:4, :n], in_=FAv[:, :, t0:t0 + n]), writes=["fsnA%d" % b])
                    S.dma("sp", lambda e: e.dma_start(out=fsn[b][:, 4:6, :n], in_=SBv[:, :, t0:t0 + n]), writes=["fsnB%d" % b])
                    S.dma("sp", lambda e: e.dma_start(out=fsn[b][:, 6:10, :n], in_=NCv[:, :, t0:t0 + n]), writes=["fsnC%d" % b])
                    for g in range(3):
                        S.dma("sp", lambda e, g=g: e.dma_start(out=gts[b][:, g * 8:(g + 1) * 8, :n], in_=GTv[:, g * 8:(g + 1) * 8, t0:t0 + n]),
                              writes=["gts%d_%d" % (b, g)])

                def compute(i):
                    t0, n, v = tl[i]
                    b = i % 2
                    xT = xTs[b]
                    xn = "xT%d" % b
                    f = fsn[b]
                    g = gts[b]
                    if CUT < 1:
                        return
                    for d in range(8):
                        dd = d % 2
                        ds = slice(d * 128, (d + 1) * 128)
                        pa, pan = bank()
                        for k in range(4):
                            S.op("pe", lambda e, k=k, pa=pa, ds=ds: e.matmul(pa[:, :n], lhsT=wa[:, k, ds], rhs=f[:, k, :n], start=(k == 0), stop=(k == 3)),
                                 reads=["wa", "fsnA%d" % b], writes=[pan])
                        pbk, pbn = bank()
                        for k in range(2):
                            S.op("pe", lambda e, k=k, pbk=pbk, ds=ds: e.matmul(pbk[:, :n], lhsT=wb[:, k, ds], rhs=f[:, 4 + k, :n], start=(k == 0), stop=(k == 1)),
                                 reads=["wb", "fsnB%d" % b], writes=[pbn])
                        pc, pcn = bank()
                        for k in range(4):
                            S.op("pe", lambda e, k=k, pc=pc, ds=ds: e.matmul(pc[:, :n], lhsT=wc[:, k, ds], rhs=f[:, 6 + k, :n], start=(k == 0), stop=(k == 3)),
                                 reads=["wc", "fsnC%d" % b], writes=[pcn])
                        t1, t2, t3 = t1s[dd], t2s[dd], t3s[dd]
                        S.op("dve", lambda e, pa=pa, t1=t1, d=d: e.tensor_tensor(out=t1[:, :n], in0=pa[:, :n], in1=g[:, d, :n], op=ALU.mult),
                             reads=[pan, "gts%d_0" % b], writes=["t1_%d" % dd])
                        S.op("dve", lambda e, pbk=pbk, t2=t2, d=d: e.tensor_tensor(out=t2[:, :n], in0=pbk[:, :n], in1=g[:, 8 + d, :n], op=ALU.mult),
                             reads=[pbn, "gts%d_1" % b], writes=["t2_%d" % dd])
                        S.op("dve", lambda e, pc=pc, t3=t3, d=d: e.tensor_tensor(out=t3[:, :n], in0=pc[:, :n], in1=g[:, 16 + d, :n], op=ALU.mult),
                             reads=[pcn, "gts%d_2" % b], writes=["t3_%d" % dd])
                        S.op("pool", lambda e, t1=t1, t2=t2: e.tensor_tensor(out=t1[:, :n], in0=t1[:, :n], in1=t2[:, :n], op=ALU.add),
                             reads=["t1_%d" % dd, "t2_%d" % dd], writes=["t1_%d" % dd])
                        S.op("pool", lambda e, t1=t1, t3=t3, d=d: e.tensor_tensor(out=mT[:, d, :n], in0=t1[:, :n], in1=t3[:, :n], op=ALU.add),
                             reads=["t1_%d" % dd, "t3_%d" % dd], writes=["mT"])
                    if CUT < 2:
                        return
                    for d in range(8):
                        ds = slice(d * 128, (d + 1) * 128)
                        po, pon = bank()
                        for k in range(8):
                            S.op("pe", lambda e, k=k, po=po, ds=ds: e.matmul(po[:, :n], lhsT=wo[:, k, ds], rhs=mT[:, k, :n], start=(k == 0), stop=(k == 7)),
                                 reads=["wo", "mT"], writes=[pon])
                        S.op("dve", lambda e, po=po, d=d: e.scalar_tensor_tensor(out=xT[:, d, :n], in0=po[:, :n], scalar=mod[:, l, 16 + d, v:v + 1],
                                                                               in1=xT[:, d, :n], op0=ALU.mult, op1=ALU.add),
                             reads=[pon, xn, "mod"], writes=[xn])
                    if CUT < 3:
                        return
                    S.dma("sp", lambda e: e.dma_start(out=XMv[:, :, t0:t0 + n], in_=xT[:, :, :n]), reads=[xn], writes=["XM%d" % i])
                    if CUT < 4:
                        return
                    norm_mod(l, 1, v, xT, n, sq, rstd, tmpa, tmps, hT, xn, "hT", i)
                    S.dma("sp", lambda e: e.dma_start(out=H2v[:, :, t0:t0 + n], in_=hT[:, :, :n]), reads=["hT"], writes=["H2%d" % i])

                load(0)
                for i in range(len(tl)):
                    if i + 1 < len(tl):
                        load(i + 1)
                    compute(i)
                S.flush()

        def phase3b(l):
            NT = 256
            last = (l == DEPTH - 1)
            tl = tiles_of(NT, with_ctx=not last)
            with ExitStack() as ps:
                w1 = sb(ps, "w1", [128, 8, DFF], BF16)
                w2 = sb(ps, "w2", [128, 32, D], BF16)
                w1v = I["w_ff1"][l].rearrange("(k p) n -> p k n", p=128)
                w2v = I["w_ff2"][l].rearrange("(k p) n -> p k n", p=128)
                for j in range(8):
                    S.dma("pool", lambda e, j=j: e.dma_start(out=w1[:, :, j * 512:(j + 1) * 512], in_=w1v[:, :, j * 512:(j + 1) * 512]), writes=["w1_%d" % j])
                for j in range(8):
                    S.dma("pool", lambda e, j=j: e.dma_start(out=w2[:, j * 4:(j + 1) * 4, :], in_=w2v[:, j * 4:(j + 1) * 4, :]), writes=["w2_%d" % j])
                xTs = [sb(ps, "xT%d" % i, [128, 8, NT], F32) for i in range(2)]
                hTs = [sb(ps, "hT%d" % i, [128, 8, NT], BF16) for i in range(2)]
                aT = sb(ps, "aT", [128, 32, NT], BF16)
                rr = [sb(ps, "rr%d" % i, [128, NT], BF16) for i in range(2)]
                XTv = R["XT"].rearrange("(c p) t -> p c t", p=128)
                XMv = R["XM"].rearrange("(c p) t -> p c t", p=128)
                H2v = R["H2"].rearrange("(c p) t -> p c t", p=128)
                if last:
                    sq = sb(ps, "sq", [128, 8, NT], BF16)
                    rstd = sb(ps, "rstd", [128, NT], F32)
                    tmpa = sb(ps, "tmpa", [128, NT], F32)
                    yT = sb(ps, "yT", [128, 8, NT], F32)
                    otok = sb(ps, "otok", [128, 2, D], F32)

                def load(i):
                    t0, n, v = tl[i]
                    b = i % 2
                    S.dma("sp", lambda e: e.dma_start(out=xTs[b][:, :, :n], in_=XMv[:, :, t0:t0 + n]), writes=["xT%d" % b])
                    S.dma("sp", lambda e: e.dma_start(out=hTs[b][:, :, :n], in_=H2v[:, :, t0:t0 + n]), writes=["hT%d" % b])

                def compute(i):
                    t0, n, v = tl[i]
                    b = i % 2
                    xT = xTs[b]
                    xn = "xT%d" % b
                    hT = hTs[b]
                    hn = "hT%d" % b
                    for f in range(32):
                        pb, pbn = bank()
                        for k in range(8):
                            S.op("pe", lambda e, k=k, f=f, pb=pb: e.matmul(pb[:, :n], lhsT=w1[:, k, f * 128:(f + 1) * 128], rhs=hT[:, k, :n],
                                                                         start=(k == 0), stop=(k == 7)), reads=[hn, "w1_%d" % (f // 4)], writes=[pbn])
                        r = rr[f % 2]
                        rn = "rr%d" % (f % 2)
                        S.op("act", lambda e, pb=pb, r=r: e.activation(out=r[:, :n], in_=pb[:, :n], func=AF.Relu), reads=[pbn], writes=[rn])
                        S.op("pool", lambda e, r=r, f=f: e.tensor_tensor(out=aT[:, f, :n], in0=r[:, :n], in1=r[:, :n], op=ALU.mult),
                             reads=[rn], writes=["aT"])
                    for d in range(8):
                        pb, pbn = bank()
                        for f in range(32):
                            S.op("pe", lambda e, f=f, d=d, pb=pb: e.matmul(pb[:, :n], lhsT=w2[:, f, d * 128:(d + 1) * 128], rhs=aT[:, f, :n],
                                                                         start=(f == 0), stop=(f == 31)), reads=["aT", "w2_%d" % (f // 4)], writes=[pbn])
                        S.op("dve", lambda e, pb=pb, d=d: e.scalar_tensor_tensor(out=xT[:, d, :n], in0=pb[:, :n], scalar=mod[:, l, 40 + d, v:v + 1],
                                                                               in1=xT[:, d, :n], op0=ALU.mult, op1=ALU.add),
                             reads=[pbn, xn, "mod"], writes=[xn])
                    if not last:
                        S.dma("sp", lambda e: e.dma_start(out=XTv[:, :, t0:t0 + n], in_=xT[:, :, :n]), reads=[xn], writes=["XT%d" % i])
                        return
                    S.op("act", lambda e: e.activation(out=sq[:, :, :n], in_=xT[:, :, :n], func=AF.Square), reads=[xn], writes=["sq"])
                    pb, pbn = bank()
                    for c in range(8):
                        S.op("pe", lambda e, c=c, pb=pb: e.matmul(pb[:, :n], lhsT=ones_bf[:], rhs=sq[:, c, :n], start=(c == 0), stop=(c == 7)),
                             reads=["sq", "ones"], writes=[pbn])
                    S.op("act", lambda e, pb=pb: e.activation(out=tmpa[:, :n], in_=pb[:, :n], func=AF.Sqrt, scale=1.0 / D, bias=epsb[:, 0:1]),
                         reads=[pbn, "epsb"], writes=["tmpa"])
                    S.op("dve", lambda e: e.reciprocal(out=rstd[:, :n], in_=tmpa[:, :n]), reads=["tmpa"], writes=["rstd"])
                    for c in range(8):
                        S.op("dve", lambda e, c=c: e.scalar_tensor_tensor(out=yT[:, c, :n], in0=xT[:, c, :n], scalar=gfin[:, c:c + 1], in1=rstd[:, :n],
                                                                          op0=ALU.mult, op1=ALU.mult), reads=[xn, "rstd", "gfin"], writes=["yT"])
                    for s in range(n // 128):
                        for half in range(2):
                            pb, pbn = bank()
                            for cc in range(4):
                                c = half * 4 + cc
                                S.op("pe", lambda e, s=s, c=c, cc=cc, pb=pb: e.transpose(out=pb[:, cc * 128:(cc + 1) * 128], in_=yT[:, c, s * 128:(s + 1) * 128],
                                                                                       identity=ident[:]), reads=["yT", "ident"], writes=[pbn])
                            eg = evac_eng()
                            S.op(eg, copy_op(eg, otok[:, s, half * 512:(half + 1) * 512], pb[:, :]), reads=[pbn], writes=["otok"])
                    r0 = t0 - LC
                    S.dma("sp", lambda e: e.dma_start(out=out[r0:r0 + n, :].rearrange("(s p) d -> p s d", p=128), in_=otok[:, :n // 128, :]),
                          reads=["otok"], writes=["out%d" % i])

                load(0)
                for i in range(len(tl)):
                    if i + 1 < len(tl):
                        load(i + 1)
                    compute(i)
                S.flush()

        if "p0" in phases:
            phase0()
        for l in layers:
            if "p1" in phases:
                phase1(l)
            if "p2f" in phases:
                phase2_fnet(l)
            if "p2n" in phases:
                phase2_na(l)
            if "p2s" in phases:
                phase2_s5(l)
            if "p3a" in phases:
                phase3a(l)
            if "p3b" in phases:
                phase3b(l)
        if S.ops["sp"] or S.ops["pe"] or S.ops["pool"]:
            S.flush()
    return nc


ALL_PHASES = ("p0", "p1", "p2f", "p2n", "p2s", "p3a", "p3b")


def kernel(**inputs):
    f32 = lambda a: np.ascontiguousarray(np.asarray(a, dtype=np.float32))
    x = f32(inputs["x"])
    ctx = f32(inputs["ctx"])
    c = f32(inputs["c"])
    c_ctx = f32(inputs["c_ctx"])
    shared = {}
    for k in W_SHAPES:
        if k == "rpbg":
            shared[k] = na_gather_rpb(f32(inputs["na_rpb"]))
        else:
            shared[k] = f32(inputs[k])
    shared.update(_consts())
    nb = x.shape[0]
    in_maps = []
    for core in range(8):
        b = core % nb
        m = dict(shared)
        m["x"] = x[b]
        m["ctx"] = ctx[b]
        m["cvec"] = np.ascontiguousarray(np.stack([c[b], c_ctx]))
        in_maps.append(m)
    nc = build(set(ALL_PHASES))
    res = run_bass_kernel_spmd(nc, in_maps, core_ids=list(range(8)))
    out = np.stack([np.asarray(res.results[b]["out"], dtype=np.float32) for b in range(nb)], axis=0)
    return out
```

```python
import numpy as np
from contextlib import ExitStack
import concourse.bass as bass
import concourse.mybir as mybir
from concourse.bass_utils import run_bass_kernel_spmd

F32 = mybir.dt.float32
BF16 = mybir.dt.bfloat16
AF = mybir.ActivationFunctionType
ALU = mybir.AluOpType

D = 1024
L = 8192
LC = 256
T = L + LC
DEPTH = 2
DIN = 5120
DFF = 4096
EPS = 1e-6
NEG = -30000.0
CUT = 99
DEBUG = 0

COMPUTE = ("pe", "dve", "act", "pool")
NDMA_SEM = 12


class Sched:
    def __init__(self, nc, stack):
        self.nc = nc
        self.ops = {e: [] for e in ("pe", "dve", "act", "pool", "sp")}
        self.sem = {}
        self.cnt = {}
        for e in COMPUTE:
            self.sem[e] = stack.enter_context(nc.semaphore("s_" + e))
            self.cnt[e] = 0
        self.dsem = {}
        self.dcnt = {}
        self.dnext = {}
        for q in ("sp", "pool", "act"):
            self.dsem[q] = [stack.enter_context(nc.semaphore("d_%s%d" % (q, j))) for j in range(NDMA_SEM)]
            self.dcnt[q] = [0] * NDMA_SEM
            self.dnext[q] = 0
        self.seen = {e: {} for e in self.ops}
        self.last_w = {}
        self.reads = {}
        self.out_events = []
        self.nblock = 0

    def _need(self, eng, reads, writes):
        need = {}

        def add(ev):
            if ev is None:
                return
            k, v = ev
            if need.get(k, 0) < v:
                need[k] = v

        for r in reads:
            add(self.last_w.get(r))
        own = eng in COMPUTE
        for w in writes:
            ev = self.last_w.get(w)
            if ev is not None and not (own and ev[0] == eng):
                add(ev)
            for ev in self.reads.get(w, ()):
                if not (own and ev[0] == eng):
                    add(ev)
        waits = []
        for k, v in need.items():
            if k == eng and eng == "pe":
                continue
            if self.seen[eng].get(k, 0) >= v:
                continue
            self.seen[eng][k] = v
            waits.append((k, v))
        return waits

    def _semof(self, k):
        if isinstance(k, tuple):
            return self.dsem[k[0]][k[1]]
        return self.sem[k]

    def _commit(self, ev, reads, writes):
        for w in writes:
            self.last_w[w] = ev
            self.reads[w] = []
        for r in reads:
            self.reads.setdefault(r, []).append(ev)

    def op(self, eng, fn, reads=(), writes=()):
        waits = self._need(eng, reads, writes)
        self.cnt[eng] += 1
        ev = (eng, self.cnt[eng])
        self.ops[eng].append((waits, fn, self.sem[eng], 1))
        self._commit(ev, reads, writes)
        return ev

    def dma(self, q, fn, reads=(), writes=(), is_output=False):
        j = self.dnext[q]
        self.dnext[q] = (j + 1) % NDMA_SEM
        key = (q, j)
        waits = self._need(q, reads, writes)
        if self.dcnt[q][j] > 0 and self.seen[q].get(key, 0) < self.dcnt[q][j]:
            self.seen[q][key] = self.dcnt[q][j]
            waits.append((key, self.dcnt[q][j]))
        self.dcnt[q][j] += 16
        ev = (key, self.dcnt[q][j])
        self.ops[q].append((waits, fn, self.dsem[q][j], 16))
        self._commit(ev, reads, writes)
        return ev

    def flush(self):
        tail = {e: [] for e in self.ops}
        for e in self.ops:
            for k in COMPUTE:
                if k != e and self.cnt[k] > self.seen[e].get(k, 0):
                    self.seen[e][k] = self.cnt[k]
                    tail[e].append((k, self.cnt[k]))
            for q in self.dsem:
                for j in range(NDMA_SEM):
                    v = self.dcnt[q][j]
                    if v > self.seen[e].get((q, j), 0):
                        self.seen[e][(q, j)] = v
                        tail[e].append(((q, j), v))
        nc = self.nc
        with nc.Block() as block:
            def replay(name):
                def body(e):
                    for waits, fn, sem, inc in self.ops[name]:
                        for k, v in waits:
                            e.wait_ge(self._semof(k), v)
                        fn(e).then_inc(sem, inc)
                    for k, v in tail[name]:
                        e.wait_ge(self._semof(k), v)
                return body
            block.sync(replay("sp"))
            block.tensor(replay("pe"))
            block.vector(replay("dve"))
            block.scalar(replay("act"))
            block.gpsimd(replay("pool"))
        for e in self.ops:
            self.ops[e] = []
        self.last_w = {}
        self.reads = {}
        self.nblock += 1


def _consts():
    c = {}
    c["ident"] = np.eye(128, dtype=np.float32)
    ij = np.outer(np.arange(64), np.arange(64)) * (2 * np.pi / 64)
    cs = np.zeros((2, 128, 128), np.float32)
    for g in range(2):
        cs[0, g * 64:(g + 1) * 64, g * 64:(g + 1) * 64] = np.cos(ij)
        cs[1, g * 64:(g + 1) * 64, g * 64:(g + 1) * 64] = np.sin(ij)
    c["cs64"] = cs
    r = np.arange(128)[:, None, None]
    cc = np.arange(64)[None, :, None]
    k1 = np.arange(128)[None, None, :]
    ang = (2 * np.pi / 8192) * ((k1 * (64 * r + cc)) % 8192)
    c["fn_tc"] = np.cos(ang).astype(np.float32)
    c["fn_tsn"] = (-np.sin(ang)).astype(np.float32)
    sc = 1.0 / np.sqrt(8192.0 * 64.0)
    a64 = (2 * np.pi / 64) * ((np.arange(64)[:, None] * np.arange(64)[None, :]) % 64)
    c["fn_w64"] = np.stack([np.cos(a64) * sc, np.sin(a64) * sc, -np.sin(a64) * sc]).astype(np.float32)
    scc = 1.0 / np.sqrt(256.0 * 64.0)
    a256 = (2 * np.pi / 256) * ((np.arange(256)[:, None] * np.arange(256)[None, :]) % 256)
    c["fn_c256"] = np.stack([np.cos(a256) * scc, -np.sin(a256) * scc]).astype(np.float32)
    w = np.arange(64)
    cs0 = np.clip(w - 8, 0, 48)
    wp = np.arange(64)[:, None]
    inside = (wp >= cs0[None, :]) & (wp < cs0[None, :] + 16)
    cm = np.where(inside, 0.0, NEG).astype(np.float32)
    jm = np.zeros((128, 128), np.float32)
    for p in range(64):
        jm[p, 64 + p] = 1.0
        jm[64 + p, p] = -1.0
    c["s5_jm"] = jm
    sidx = np.repeat(np.arange(8), 16)
    c["s5_mask"] = np.stack([(sidx[None, :] >= sidx[:, None]), (sidx[None, :] <= sidx[:, None])]).astype(np.float32)
    selin = np.zeros((128, 8, 8, 128), np.float32)
    for g in range(8):
        for s_ in range(8):
            for m_ in range(16):
                selin[g * 16 + m_, g, s_, s_ * 16 + m_] = 1.0
    c["s5_selin"] = selin
    c["s5_selout"] = np.ascontiguousarray(np.transpose(selin, (3, 2, 1, 0)))
    c["na_cm"] = np.ascontiguousarray(np.broadcast_to(np.concatenate([cm, cm], 0)[:, None, :], (128, 15, 64))).astype(np.float32)
    return c


def na_gather_rpb(rpb):
    wp = np.arange(64)[:, None]
    w = np.arange(64)[None, :]
    idx = np.clip(wp - w + 15, 0, 30)
    g = rpb[:, :, :, idx]
    g = np.transpose(g, (0, 1, 3, 2, 4))
    return np.ascontiguousarray(np.concatenate([g, g], axis=2)).astype(np.float32)


def na_blocks():
    out = []
    for j in range(64):
        rs = [min(max(2 * j + b - 4, 0), 120) for b in range(2)]
        lo, hi = min(rs), max(rs) + 7
        blocks = list(range(lo // 2, hi // 2 + 1))
        pat = []
        for i in blocks:
            for a in range(2):
                for b in range(2):
                    rho = 2 * i + a
                    r = 2 * j + b
                    ok = rs[b] <= rho <= rs[b] + 7
                    pat.append((rho - r + 7) if ok else None)
        out.append((blocks, tuple(pat)))
    return out


SCRATCH = {
    "XT": ([D, T], F32), "XM": ([D, T], F32), "H2": ([D, T], BF16),
    "ZF": ([T, 256], BF16), "VV": ([T, 512], BF16), "UT": ([256, T], BF16),
    "QT": ([512, T], BF16), "KT": ([512, T], BF16), "GT": ([3072, T], BF16),
    "FA": ([512, T], BF16), "SB": ([256, T], BF16), "NCT": ([512, T], BF16),
    "AF": ([64, 128 * 512], BF16), "DBG": ([128, 2048], F32), "YT": ([256, T], F32),
}

W_SHAPES = {
    "w_mod": [DEPTH, D, 6 * D], "b_mod": [DEPTH, 6 * D], "g_norm1": [DEPTH, D], "g_norm2": [DEPTH, D],
    "w_in": [DEPTH, D, DIN], "w_br_a": [DEPTH, 256, D], "w_br_b": [DEPTH, 256, D], "w_br_c": [DEPTH, 512, D],
    "w_out": [DEPTH, D, D], "w_ff1": [DEPTH, D, DFF], "w_ff2": [DEPTH, DFF, D], "g_final": [D],
    "rpbg": [DEPTH, 8, 128, 15, 64],
    "s5_lam_re": [DEPTH, 2, 16, 64], "s5_lam_im": [DEPTH, 2, 16, 64], "s5_log_dt": [DEPTH, 2, 16],
    "s5_b_re": [DEPTH, 2, 16, 64, 16], "s5_b_im": [DEPTH, 2, 16, 64, 16],
    "s5_c_re": [DEPTH, 2, 16, 16, 64], "s5_c_im": [DEPTH, 2, 16, 16, 64],
    "s5_d": [DEPTH, 16, 16], "s5_w_glu": [DEPTH, 256, 256],
}


def tiles_of(n, with_ctx=True):
    out = []
    if with_ctx:
        t = 0
        while t < LC:
            m = min(n, LC - t)
            out.append((t, m, 1))
            t += m
    t = LC
    while t < T:
        m = min(n, T - t)
        out.append((t, m, 0))
        t += m
    return out


def build(phases, kinds=None, layers=(0, 1)):
    kinds = kinds or {}
    nc = bass.Bass("TRN2", target_bir_lowering=False)
    I = {}
    I["x"] = nc.dram_tensor("x", [L, D], F32, kind="ExternalInput").ap()
    I["ctx"] = nc.dram_tensor("ctx", [LC, D], F32, kind="ExternalInput").ap()
    I["cvec"] = nc.dram_tensor("cvec", [2, D], F32, kind="ExternalInput").ap()
    for k, shp in W_SHAPES.items():
        I[k] = nc.dram_tensor(k, shp, F32, kind="ExternalInput").ap()
    for k, v in _consts().items():
        I[k] = nc.dram_tensor(k, list(v.shape), F32, kind="ExternalInput").ap()
    out = nc.dram_tensor("out", [L, D], F32, kind="ExternalOutput").ap()
    R = {}
    for k, (shp, dt) in SCRATCH.items():
        R[k] = nc.dram_tensor("r_" + k.lower(), shp, dt, kind=kinds.get(k, "Internal")).ap()

    with ExitStack() as st:
        S = Sched(nc, st)
        pbig = [st.enter_context(nc.psum_tensor("pbig%d" % i, [128, 1024], F32)) for i in range(4)]
        pbanks = [pbig[i // 2][:, (i % 2) * 512:(i % 2 + 1) * 512] for i in range(8)]
        pctr = [0]

        def bank():
            i = pctr[0] % 8
            pctr[0] += 1
            return pbanks[i], "pb%d" % i

        evc = [0]

        def evac_eng():
            evc[0] += 1
            return "dve" if evc[0] % 2 else "act"

        def copy_op(eng, out_ap, in_ap):
            if eng == "act":
                return lambda e: e.activation(out=out_ap, in_=in_ap, func=AF.Copy)
            return lambda e: e.tensor_copy(out=out_ap, in_=in_ap)

        uidc = [0]

        def sb(stack, name, shape, dt):
            uidc[0] += 1
            return stack.enter_context(nc.sbuf_tensor("%s_u%d" % (name, uidc[0]), shape, dt))

        ident = sb(st, "ident", [128, 128], F32)
        ones_bf = sb(st, "ones_bf", [128, 128], BF16)
        mod = sb(st, "mod", [128, DEPTH, 48, 2], F32)
        gsc = sb(st, "gsc", [128, DEPTH, 2, 8, 2], F32)
        gfin = sb(st, "gfin", [128, 8], F32)

        NCDMA = dict(allow_slow_non_contiguous=True)
        S.dma("sp", lambda e: e.dma_start(out=ident[:], in_=I["ident"]), writes=["ident"])
        S.op("pool", lambda e: e.memset(ones_bf[:], 1.0), writes=["ones"])

        def phase0():
            with ExitStack() as ps:
                cT = sb(ps, "cT", [128, 8, 2], F32)
                sT = sb(ps, "sT", [128, 8, 2], F32)
                bm = sb(ps, "bm", [128, DEPTH, 48], F32)
                gn = sb(ps, "gn", [128, DEPTH, 2, 8], F32)
                wm = [sb(ps, "wm%d" % i, [128, 8, 512], F32) for i in range(2)]
                for j in range(2):
                    S.dma("sp", lambda e, j=j: e.dma_start(out=cT[:, :, j], in_=I["cvec"][j].rearrange("(k p) -> p k", p=128), **NCDMA), writes=["cT"])
                for l in range(DEPTH):
                    S.dma("sp", lambda e, l=l: e.dma_start(out=bm[:, l, :], in_=I["b_mod"][l].rearrange("(j p) -> p j", p=128), **NCDMA), writes=["bm"])
                    S.dma("sp", lambda e, l=l: e.dma_start(out=gn[:, l, 0, :], in_=I["g_norm1"][l].rearrange("(k p) -> p k", p=128), **NCDMA), writes=["gn0"])
                    S.dma("sp", lambda e, l=l: e.dma_start(out=gn[:, l, 1, :], in_=I["g_norm2"][l].rearrange("(k p) -> p k", p=128), **NCDMA), writes=["gn1"])
                S.dma("sp", lambda e: e.dma_start(out=gfin[:], in_=I["g_final"].rearrange("(k p) -> p k", p=128), **NCDMA), writes=["gfin"])
                S.op("act", lambda e: e.activation(out=sT[:], in_=cT[:], func=AF.Silu), reads=["cT"], writes=["sT"])
                for l in range(DEPTH):
                    pb, pbn = bank()
                    wv = I["w_mod"][l].rearrange("(k p) n -> p k n", p=128)
                    for blk in range(12):
                        w = wm[blk % 2]
                        wn = "wm%d" % (blk % 2)
                        S.dma("sp", lambda e, w=w, blk=blk, wv=wv: e.dma_start(out=w[:], in_=wv[:, :, blk * 512:(blk + 1) * 512]), writes=[wn])
                        for jj in range(4):
                            j = blk * 4 + jj
                            for k in range(8):
                                S.op("pe", lambda e, w=w, jj=jj, j=j, k=k, pb=pb: e.matmul(
                                    pb[:, j * 2:j * 2 + 2], lhsT=w[:, k, jj * 128:(jj + 1) * 128], rhs=sT[:, k, :],
                                    start=(k == 0), stop=(k == 7)), reads=[wn, "sT"], writes=[pbn])
                    for v in range(2):
                        S.op("dve", lambda e, l=l, v=v, pb=pb: e.tensor_tensor(
                            out=mod[:, l, :, v], in0=pb[:, 0:96].rearrange("p (j v) -> p j v", v=2)[:, :, v], in1=bm[:, l, :], op=ALU.add),
                            reads=[pbn, "bm"], writes=["mod"])
                    for nn in range(2):
                        for v in range(2):
                            j0 = 8 + 24 * nn
                            S.op("dve", lambda e, l=l, v=v, nn=nn, j0=j0: e.scalar_tensor_tensor(
                                out=gsc[:, l, nn, :, v], in0=mod[:, l, j0:j0 + 8, v], scalar=1.0, in1=gn[:, l, nn, :],
                                op0=ALU.add, op1=ALU.mult), reads=["mod", "gn%d" % nn], writes=["gsc"])
                S.flush()

        def norm_mod(l, nn, v, xT, n, sq, rstd, tmpa, tmps, hT, xname, hname, uid):
            sh_j0 = 24 * nn
            S.op("act", lambda e: e.activation(out=sq[:, :, :n], in_=xT[:, :, :n], func=AF.Square), reads=[xname], writes=["sq"])
            pb, pbn = bank()
            for c in range(8):
                S.op("pe", lambda e, c=c, pb=pb: e.matmul(pb[:, :n], lhsT=ones_bf[:], rhs=sq[:, c, :n], start=(c == 0), stop=(c == 7)),
                     reads=["sq", "ones"], writes=[pbn])
            S.op("act", lambda e, pb=pb: e.activation(out=tmpa[:, :n], in_=pb[:, :n], func=AF.Sqrt, scale=1.0 / D, bias=epsb[:, 0:1]),
                 reads=[pbn, "epsb"], writes=["tmpa"])
            S.op("dve", lambda e: e.reciprocal(out=rstd[:, :n], in_=tmpa[:, :n]), reads=["tmpa"], writes=["rstd"])
            for c in range(8):
                tt = tmps[c % 2]
                tn = "tmps%d" % (c % 2)
                S.op("dve", lambda e, c=c, tt=tt: e.tensor_tensor(out=tt[:, :n], in0=xT[:, c, :n], in1=rstd[:, :n], op=ALU.mult),
                     reads=[xname, "rstd"], writes=[tn])
                S.op("act", lambda e, c=c, tt=tt: e.activation(out=hT[:, c, :n], in_=tt[:, :n], func=AF.Identity,
                                                              scale=gsc[:, l, nn, c, v:v + 1], bias=mod[:, l, sh_j0 + c, v:v + 1]),
                     reads=[tn, "gsc", "mod"], writes=[hname])

        epsb = sb(st, "epsb", [128, 1], F32)
        S.op("pool", lambda e: e.memset(epsb[:], EPS), writes=["epsb"])

        def phase1(l):
            NT = 512
            tl = tiles_of(NT)
            with ExitStack() as ps:
                w_sb = sb(ps, "w_in_sb", [128, 8, DIN], BF16)
                wv = I["w_in"][l].rearrange("(k p) n -> p k n", p=128)
                for j in range(10):
                    S.dma("pool", lambda e, j=j: e.dma_start(out=w_sb[:, :, j * 512:(j + 1) * 512], in_=wv[:, :, j * 512:(j + 1) * 512]),
                          writes=["w%d" % j])
                xtok = sb(ps, "xtok", [128, 4, D], F32)
                xTs = [sb(ps, "xT%d" % i, [128, 8, NT], F32) for i in range(2)]
                hTs = [sb(ps, "hT%d" % i, [128, 8, NT], BF16) for i in range(2)]
                sq = sb(ps, "sq", [128, 8, NT], BF16)
                rstd = sb(ps, "rstd", [128, NT], F32)
                tmpa = sb(ps, "tmpa", [128, NT], F32)
                tmps = [sb(ps, "tmps%d" % i, [128, NT], F32) for i in range(2)]
                stg = [sb(ps, "stg%d" % i, [128, 4, NT], BF16) for i in range(3)]
                sctr = [0]

                def stage():
                    i = sctr[0] % 3
                    sctr[0] += 1
                    return stg[i], "stg%d" % i

                XTv = R["XT"].rearrange("(c p) t -> p c t", p=128)

                def load(i):
                    t0, n, v = tl[i]
                    xT = xTs[i % 2]
                    xn = "xT%d" % (i % 2)
                    if l == 0:
                        src = I["ctx"] if v else I["x"]
                        r0 = t0 if v else t0 - LC
                        ns = n // 128
                        S.dma("sp", lambda e: e.dma_start(out=xtok[:, :ns, :], in_=src[r0:r0 + n, :].rearrange("(s p) d -> p s d", p=128)),
                              writes=["xtok"])
                        for s in range(ns):
                            for half in range(2):
                                pb, pbn = bank()
                                for cc in range(4):
                                    c = half * 4 + cc
                                    S.op("pe", lambda e, s=s, c=c, cc=cc, pb=pb: e.transpose(
                                        out=pb[:, cc * 128:(cc + 1) * 128], in_=xtok[:, s, c * 128:(c + 1) * 128], identity=ident[:]),
                                        reads=["xtok", "ident"], writes=[pbn])
                                eg = "dve"
                                S.op(eg, copy_op(eg, xT[:, half * 4:half * 4 + 4, s * 128:(s + 1) * 128],
                                                 pb[:, :].rearrange("p (c t) -> p c t", c=4)), reads=[pbn], writes=[xn])
                        S.dma("sp", lambda e: e.dma_start(out=XTv[:, :, t0:t0 + n], in_=xT[:, :, :n]), reads=[xn], writes=["XT%d" % i])
                    else:
                        S.dma("sp", lambda e: e.dma_start(out=xT[:, :, :n], in_=XTv[:, :, t0:t0 + n]), writes=[xn])

                def compute(i):
                    t0, n, v = tl[i]
                    xT = xTs[i % 2]
                    xn = "xT%d" % (i % 2)
                    hT = hTs[i % 2]
                    hn = "hT%d" % (i % 2)
                    norm_mod(l, 0, v, xT, n, sq, rstd, tmpa, tmps, hT, xn, hn, i)
                    ns = n // 128
                    for s in range(ns):
                        sg, sgn = stage()
                        pb, pbn = bank()
                        for c in range(8):
                            S.op("pe", lambda e, s=s, c=c, pb=pb: e.matmul(pb[:, 0:256], lhsT=hT[:, c, s * 128:(s + 1) * 128], rhs=w_sb[:, c, 0:256],
                                                                         start=(c == 0), stop=(c == 7)), reads=[hn, "w0"], writes=[pbn])
                        eg = evac_eng()
                        S.op(eg, copy_op(eg, sg[:, 0, 0:256], pb[:, 0:256]), reads=[pbn], writes=[sgn])
                        S.dma("sp", lambda e, s=s, sg=sg: e.dma_start(out=R["ZF"][t0 + s * 128:t0 + (s + 1) * 128, :], in_=sg[:, 0, 0:256]),
                              reads=[sgn], writes=["ZF"])
                        pb, pbn = bank()
                        for c in range(8):
                            S.op("pe", lambda e, s=s, c=c, pb=pb: e.matmul(pb[:, :], lhsT=hT[:, c, s * 128:(s + 1) * 128], rhs=w_sb[:, c, 1536:2048],
                                                                         start=(c == 0), stop=(c == 7)), reads=[hn, "w3"], writes=[pbn])
                        eg = evac_eng()
                        S.op(eg, copy_op(eg, sg[:, 1, :], pb[:, :]), reads=[pbn], writes=[sgn])
                        S.dma("sp", lambda e, s=s, sg=sg: e.dma_start(out=R["VV"][t0 + s * 128:t0 + (s + 1) * 128, :], in_=sg[:, 1, :]),
                              reads=[sgn], writes=["VV"])
                    groups = [("UT", 256, 2, False), ("QT", 512, 4, False), ("KT", 1024, 4, False)]
                    groups += [("GT%d" % g, 2048 + g * 512, 4, True) for g in range(6)]
                    for name, col0, nch, sig in groups:
                        sg, sgn = stage()
                        for q in range(nch):
                            pb, pbn = bank()
                            cs = col0 + q * 128
                            for c in range(8):
                                S.op("pe", lambda e, c=c, cs=cs, pb=pb: e.matmul(pb[:, :n], lhsT=w_sb[:, c, cs:cs + 128], rhs=hT[:, c, :n],
                                                                               start=(c == 0), stop=(c == 7)), reads=[hn, "w%d" % (cs // 512)], writes=[pbn])
                            if sig:
                                S.op("act", lambda e, q=q, pb=pb, sg=sg: e.activation(out=sg[:, q, :n], in_=pb[:, :n], func=AF.Sigmoid),
                                     reads=[pbn], writes=[sgn])
                            else:
                                eg = evac_eng()
                                S.op(eg, copy_op(eg, sg[:, q, :n], pb[:, :n]), reads=[pbn], writes=[sgn])
                        if sig:
                            g = int(name[2:])
                            dst = R["GT"].rearrange("(c p) t -> p c t", p=128)[:, g * 4:(g + 1) * 4, t0:t0 + n]
                        else:
                            dst = R[name].rearrange("(c p) t -> p c t", p=128)[:, :, t0:t0 + n]
                        S.dma("sp", lambda e, sg=sg, dst=dst, nch=nch: e.dma_start(out=dst, in_=sg[:, :nch, :n]), reads=[sgn], writes=[name])

                load(0)
                for i in range(len(tl)):
                    if i + 1 < len(tl):
                        load(i + 1)
                    compute(i)
                S.flush()


        def phase2_fnet(l):
            with ExitStack() as ps:
                tc = sb(ps, "fn_tc", [128, 64, 128], BF16)
                tsn = sb(ps, "fn_tsn", [128, 64, 128], BF16)
                w64 = sb(ps, "fn_w64", [64, 3, 64], BF16)
                zf = sb(ps, "zf", [128, 64, 256], BF16)
                xo = sb(ps, "xo", [128, 4, L], BF16)
                ablk = [sb(ps, "ablk%d" % i, [64, 16, 512], BF16) for i in range(2)]
                stg = [sb(ps, "fstg%d" % i, [128, 512], BF16) for i in range(3)]
                for q4 in range(4):
                    S.dma("pool", lambda e, q4=q4: e.dma_start(out=tc[:, q4 * 16:(q4 + 1) * 16, :], in_=I["fn_tc"][:, q4 * 16:(q4 + 1) * 16, :]), writes=["tc%d" % q4])
                    S.dma("pool", lambda e, q4=q4: e.dma_start(out=tsn[:, q4 * 16:(q4 + 1) * 16, :], in_=I["fn_tsn"][:, q4 * 16:(q4 + 1) * 16, :]), writes=["tsn%d" % q4])
                S.dma("pool", lambda e: e.dma_start(out=w64[:], in_=I["fn_w64"].rearrange("a p m -> p a m")), writes=["w64"])
                zsrc = R["ZF"][LC:T, :].rearrange("(r c) ch -> r c ch", c=64)
                for q4 in range(4):
                    S.dma("sp", lambda e, q4=q4: e.dma_start(out=zf[:, q4 * 16:(q4 + 1) * 16, :], in_=zsrc[:, q4 * 16:(q4 + 1) * 16, :]), writes=["zf%d" % q4])
                AFv = R["AF"].rearrange("c (k m) -> c k m", m=512)
                if l < DEPTH - 1:
                    c256 = sb(ps, "c256", [128, 2, 2, 256], BF16)
                    zc = sb(ps, "zc", [128, 2, 256], BF16)
                    xc = sb(ps, "xc", [128, 4, 256], BF16)
                    for a in range(2):
                        S.dma("pool", lambda e, a=a: e.dma_start(out=c256[:, a, :, :], in_=I["fn_c256"][a].rearrange("(tc p) k -> p tc k", p=128)), writes=["c256"])
                    S.dma("sp", lambda e: e.dma_start(out=zc[:], in_=R["ZF"][0:LC, :].rearrange("(a p) ch -> p a ch", p=128)), writes=["zc"])
                    for ri in range(2):
                        for q in range(2):
                            pb, pbn = bank()
                            for t2 in range(2):
                                S.op("pe", lambda e, ri=ri, q=q, t2=t2, pb=pb: e.matmul(pb[:, 0:256], lhsT=zc[:, t2, q * 128:(q + 1) * 128], rhs=c256[:, ri, t2, :],
                                                                                     start=(t2 == 0), stop=(t2 == 1)), reads=["zc", "c256"], writes=[pbn])
                            eg = evac_eng()
                            S.op(eg, copy_op(eg, xc[:, ri * 2 + q, :], pb[:, 0:256]), reads=[pbn], writes=["xc"])
                    S.dma("sp", lambda e: e.dma_start(out=R["FA"].rearrange("(a p) t -> p a t", p=128)[:, :, 0:LC], in_=xc[:]), reads=["xc"], writes=["FAc"])
                for c in range(64):
                    pb, pbn = bank()
                    S.op("pe", lambda e, c=c, pb=pb: e.matmul(pb[:, 0:256], lhsT=tc[:, c, :], rhs=zf[:, c, :], start=True, stop=True),
                         reads=["tc%d" % (c // 16), "zf%d" % (c // 16)], writes=[pbn])
                    S.op("pe", lambda e, c=c, pb=pb: e.matmul(pb[:, 256:512], lhsT=tsn[:, c, :], rhs=zf[:, c, :], start=True, stop=True),
                         reads=["tsn%d" % (c // 16), "zf%d" % (c // 16)], writes=[pbn])
                    sg = stg[c % 3]
                    sgn = "fstg%d" % (c % 3)
                    eg = evac_eng()
                    S.op(eg, copy_op(eg, sg[:, :], pb[:, :]), reads=[pbn], writes=[sgn])
                    S.dma("sp", lambda e, c=c, sg=sg: e.dma_start(out=AFv[c], in_=sg[:, :]), reads=[sgn], writes=["AF"])
                xov = xo[:, :, :].rearrange("p a (k2 k1) -> p a k2 k1", k1=128)
                for kb in range(8):
                    ab = ablk[kb % 2]
                    abn = "ablk%d" % (kb % 2)
                    S.dma("sp", lambda e, kb=kb, ab=ab: e.dma_start(out=ab[:], in_=AFv[:, kb * 16:(kb + 1) * 16, :]), reads=["AF"], writes=[abn])
                    for k1l in range(16):
                        k1 = kb * 16 + k1l
                        pb, pbn = bank()
                        for q in range(2):
                            ar = ab[:, k1l, q * 128:(q + 1) * 128]
                            ai = ab[:, k1l, 256 + q * 128:256 + (q + 1) * 128]
                            o_r = pb[:, q * 64:(q + 1) * 64]
                            o_i = pb[:, (2 + q) * 64:(3 + q) * 64]
                            S.op("pe", lambda e, ar=ar, o_r=o_r: e.matmul(o_r, lhsT=ar, rhs=w64[:, 0, :], start=True, stop=False), reads=[abn, "w64"], writes=[pbn])
                            S.op("pe", lambda e, ai=ai, o_r=o_r: e.matmul(o_r, lhsT=ai, rhs=w64[:, 1, :], start=False, stop=True), reads=[abn, "w64"], writes=[pbn])
                            S.op("pe", lambda e, ai=ai, o_i=o_i: e.matmul(o_i, lhsT=ai, rhs=w64[:, 0, :], start=True, stop=False), reads=[abn, "w64"], writes=[pbn])
                            S.op("pe", lambda e, ar=ar, o_i=o_i: e.matmul(o_i, lhsT=ar, rhs=w64[:, 2, :], start=False, stop=True), reads=[abn, "w64"], writes=[pbn])
                        eg = "dve"
                        S.op(eg, copy_op(eg, xov[:, :, :, k1], pb[:, 0:256].rearrange("p (a k) -> p a k", a=4)), reads=[pbn], writes=["xo"])
                FAl = R["FA"].rearrange("(a p) t -> p a t", p=128)
                for a in range(4):
                    S.dma("sp", lambda e, a=a: e.dma_start(out=FAl[:, a, LC:T], in_=xo[:, a, :]), reads=["xo"], writes=["FA%d" % a])
                S.flush()

        def phase2_na(l):
            blocks = na_blocks()
            pats = {}
            for blks, pat in blocks:
                if pat not in pats:
                    pats[pat] = (len(pats), len(blks))
            npat = len(pats)
            with_ctx_q = (l < DEPTH - 1)
            scale = 0.125
            with ExitStack() as ps:
                cm = sb(ps, "na_cm", [128, 15, 64], F32)
                S.dma("sp", lambda e: e.dma_start(out=cm[:], in_=I["na_cm"]), writes=["cm"])
                gt = sb(ps, "na_g", [128, 15, 64], F32)
                cb = sb(ps, "na_cb", [128, 16, 64], F32)
                bias = sb(ps, "na_bias", [128, npat, 640], F32)
                kTs = [sb(ps, "kT%d" % i, [64, T], BF16) for i in range(2)]
                qTs = [sb(ps, "qT%d" % i, [64, T], BF16) for i in range(2)]
                v1s = [sb(ps, "v1_%d" % i, [128, 66, 65], BF16) for i in range(2)]
                ncs = [sb(ps, "ncT%d" % i, [64, T], BF16) for i in range(2)]
                scb = [sb(ps, "scb%d" % i, [128, 640], F32) for i in range(2)]
                PTs = [sb(ps, "PT%d" % i, [128, 7, 128], BF16) for i in range(2)]
                otk = [sb(ps, "otk%d" % i, [128, 64], F32) for i in range(2)]
                rec = [sb(ps, "rec%d" % i, [128, 1], F32) for i in range(2)]
                for i in range(2):
                    S.op("pool", lambda e, i=i: e.memset(v1s[i][:, :, 64:65], 1.0), writes=["v1o_%d" % i])
                S.op("pool", lambda e: e.memset(cb[:, 15, :], NEG), writes=["cbneg"])
                VVv = R["VV"].rearrange("(blk p) d -> p blk d", p=128)

                def load_head(h):
                    hb = h % 2
                    S.dma("sp", lambda e: e.dma_start(out=kTs[hb][:], in_=R["KT"][h * 64:(h + 1) * 64, :]), writes=["kT%d" % hb])
                    S.dma("sp", lambda e: e.dma_start(out=qTs[hb][:], in_=R["QT"][h * 64:(h + 1) * 64, :]), writes=["qT%d" % hb])
                    for part in range(3):
                        S.dma("sp", lambda e, part=part: e.dma_start(out=v1s[hb][:, part * 22:(part + 1) * 22, 0:64],
                                                                    in_=VVv[:, part * 22:(part + 1) * 22, h * 64:(h + 1) * 64]),
                              reads=["v1o_%d" % hb], writes=["v1_%d_%d" % (hb, part)])

                def head(h):
                    hb = h % 2
                    kT, qT, v1, ncT = kTs[hb], qTs[hb], v1s[hb], ncs[hb]
                    kn, qn, ncn = "kT%d" % hb, "qT%d" % hb, "ncT%d" % hb
                    vn = ["v1_%d_%d" % (hb, p) for p in range(3)]
                    S.dma("sp", lambda e: e.dma_start(out=gt[:], in_=I["rpbg"][l, h]), writes=["na_g"])
                    S.op("dve", lambda e: e.tensor_tensor(out=cb[:, 0:15, :], in0=gt[:], in1=cm[:], op=ALU.add), reads=["na_g", "cm"], writes=["cb"])
                    for pat, (pi, nb) in pats.items():
                        k = 0
                        for s_ in range(nb):
                            for a in range(2):
                                for b in range(2):
                                    d = pat[k]
                                    k += 1
                                    row = 15 if d is None else d
                                    eng = "dve" if (k % 2) else "pool"
                                    S.op(eng, lambda e, a=a, b=b, s_=s_, row=row, pi=pi: e.tensor_copy(
                                        out=bias[a * 64:(a + 1) * 64, pi, s_ * 128 + b * 64:s_ * 128 + (b + 1) * 64], in_=cb[a * 64:(a + 1) * 64, row, :]),
                                        reads=["cb", "cbneg"], writes=["bias"])
                    qblocks = ([0, 1] if with_ctx_q else []) + list(range(2, 66))
                    nq = len(qblocks)
                    info = {}

                    def stageA(qi):
                        tq = qblocks[qi]
                        jb = qi % 2
                        if tq >= 2:
                            blks, pat = blocks[tq - 2]
                            pi, nb = pats[pat]
                        else:
                            blks, nb, pi = [], 0, 0
                        qs = qT[:, tq * 128:(tq + 1) * 128]
                        big = pbig[jb]
                        bign = ["pb%d" % (2 * jb), "pb%d" % (2 * jb + 1)]
                        for s_, i in enumerate(blks):
                            tk = 2 + i
                            S.op("pe", lambda e, s_=s_, tk=tk, big=big, qs=qs: e.matmul(big[:, s_ * 128:(s_ + 1) * 128], lhsT=kT[:, tk * 128:(tk + 1) * 128], rhs=qs,
                                                                                      start=True, stop=True), reads=[kn, qn], writes=bign)
                        psc = pbanks[4 + jb]
                        pscn = "pb%d" % (4 + jb)
                        for s_ in range(2):
                            S.op("pe", lambda e, s_=s_, psc=psc, qs=qs: e.matmul(psc[:, s_ * 128:(s_ + 1) * 128], lhsT=kT[:, s_ * 128:(s_ + 1) * 128], rhs=qs,
                                                                               start=True, stop=True), reads=[kn, qn], writes=[pscn])
                        PT = PTs[jb]
                        ptn = "PT%d" % jb
                        scbj = scb[jb]
                        if nb:
                            S.op("dve", lambda e, big=big, nb=nb, pi=pi, scbj=scbj: e.scalar_tensor_tensor(out=scbj[:, :nb * 128], in0=big[:, :nb * 128], scalar=scale,
                                                                                                          in1=bias[:, pi, :nb * 128], op0=ALU.mult, op1=ALU.add),
                                 reads=bign + ["bias"], writes=["scb%d" % jb])
                            S.op("act", lambda e, nb=nb, PT=PT, scbj=scbj: e.activation(out=PT[:, 0:nb, :], in_=scbj[:, :nb * 128].rearrange("p (s q) -> p s q", q=128), func=AF.Exp),
                                 reads=["scb%d" % jb], writes=[ptn])
                        S.op("act", lambda e, psc=psc, PT=PT: e.activation(out=PT[:, 5:7, :], in_=psc[:, 0:256].rearrange("p (s q) -> p s q", q=128), func=AF.Exp, scale=scale),
                             reads=[pscn], writes=[ptn])
                        info[qi] = (tq, jb, blks)

                    def stageB(qi):
                        tq, jb, blks = info[qi]
                        PT = PTs[jb]
                        ptn = "PT%d" % jb
                        recj, otkj = rec[jb], otk[jb]
                        pso = pbanks[6 + jb]
                        pson = "pb%d" % (6 + jb)
                        klist = [(s_, 2 + i) for s_, i in enumerate(blks)] + [(5, 0), (6, 1)]
                        for n_, (slot, tk) in enumerate(klist):
                            first, lastk = (n_ == 0), (n_ == len(klist) - 1)
                            S.op("pe", lambda e, slot=slot, tk=tk, first=first, lastk=lastk, pso=pso, PT=PT: e.matmul(pso[:, 0:65], lhsT=PT[:, slot, :], rhs=v1[:, tk, :],
                                                                                                                    start=first, stop=lastk),
                                 reads=[ptn, vn[tk // 22], "v1o_%d" % hb], writes=[pson])
                        S.op("dve", lambda e, pso=pso, recj=recj: e.reciprocal(out=recj[:], in_=pso[:, 64:65]), reads=[pson], writes=["rec%d" % jb])
                        S.op("dve", lambda e, pso=pso, recj=recj, otkj=otkj: e.tensor_scalar(out=otkj[:], in0=pso[:, 0:64], scalar1=recj[:, 0:1], scalar2=None, op0=ALU.mult),
                             reads=[pson, "rec%d" % jb], writes=["otk%d" % jb])

                    def stageC(qi):
                        tq, jb, blks = info[qi]
                        otkj = otk[jb]
                        pso = pbanks[6 + jb]
                        pson = "pb%d" % (6 + jb)
                        S.op("pe", lambda e, pso=pso, otkj=otkj: e.transpose(out=pso[0:64, 128:256], in_=otkj[:], identity=ident[:]), reads=["otk%d" % jb], writes=[pson])
                        S.op("act", copy_op("act", ncT[:, tq * 128:(tq + 1) * 128], pso[0:64, 128:256]), reads=[pson], writes=[ncn])

                    for i in range(nq + 2):
                        if i < nq:
                            stageA(i)
                        if 0 <= i - 1 < nq:
                            stageB(i - 1)
                        if 0 <= i - 2 < nq:
                            stageC(i - 2)
                    t_lo = 0 if with_ctx_q else LC
                    S.dma("sp", lambda e: e.dma_start(out=R["NCT"][h * 64:(h + 1) * 64, t_lo:T], in_=ncT[:, t_lo:T]), reads=[ncn], writes=["NCT%d" % h])

                load_head(0)
                for h in range(8):
                    if h + 1 < 8:
                        load_head(h + 1)
                    head(h)
                S.flush()


        def phase2_s5(l):
            Q = T // 8
            PCS = [(0, 352), (352, 704), (704, 1056)]
            TWO_PI = float(2 * np.pi)
            MAGIC = 12582912.0
            with ExitStack() as po:
                bblk = sb(po, "bblk", [128, 32, 128], BF16)
                cblk = sb(po, "cblk", [128, 32, 128], BF16)
                dsum = sb(po, "dsum", [128, 16, 128], BF16)
                MU = sb(po, "MU", [128, 11, 2, 32], F32)
                jm = sb(po, "jm", [128, 128], F32)
                S.dma("sp", lambda e: e.dma_start(out=jm[:], in_=I["s5_jm"]), writes=["jm"])
                with ExitStack() as ps:
                    def t32(name):
                        return sb(ps, name, [128, 32], F32)
                    lr, li, ldt, dtv, x1, mag, ang = [t32(n) for n in ("lr", "li", "ldt", "dtv", "x1", "mag", "ang")]
                    s0, tq, nn_, red, sinv, cosv = [t32(n) for n in ("s0", "tq", "nn", "red", "sinv", "cosv")]
                    am1, den, rden, wr, wi, u1, u2, u3, u4 = [t32(n) for n in ("am1", "den", "rden", "wr", "wi", "u1", "u2", "u3", "u4")]
                    PW = sb(ps, "PW", [128, 9, 2, 32], F32)
                    IPW = sb(ps, "IPW", [128, 8, 2, 32], F32)
                    Y1 = sb(ps, "Y1", [128, 32, 16], F32)
                    Y2 = sb(ps, "Y2", [128, 32, 16], F32)
                    BB1 = sb(ps, "BB1", [128, 32, 16], F32)
                    BB2 = sb(ps, "BB2", [128, 32, 16], F32)
                    CC1 = sb(ps, "CC1", [128, 4, 128], F32)
                    CC2 = sb(ps, "CC2", [128, 4, 128], F32)
                    X1 = sb(ps, "X1", [128, 32, 16], F32)
                    X2 = sb(ps, "X2", [128, 32, 16], F32)
                    e1 = sb(ps, "e1", [128, 32, 16], F32)
                    e2 = sb(ps, "e2", [128, 32, 16], F32)
                    EBt = sb(ps, "EBt", [128, 32, 8, 16], F32)
                    ENt = sb(ps, "ENt", [128, 32, 8, 16], F32)
                    GPt = sb(ps, "GPt", [128, 32, 8, 16], F32)
                    CBt = sb(ps, "CBt", [128, 32, 8, 16], F32)
                    Dm = sb(ps, "Dm", [128, 32, 128], F32)
                    msk = sb(ps, "msk", [128, 2, 128], F32)
                    dsk = sb(ps, "dsk", [128, 16], F32)
                    for hf in range(2):
                        hs = slice(hf * 64, (hf + 1) * 64)
                        S.dma("sp", lambda e, hs=hs: e.dma_start(out=lr[hs, :], in_=I["s5_lam_re"][l].rearrange("d g p -> p (d g)"), **NCDMA), writes=["lr"])
                        S.dma("sp", lambda e, hs=hs: e.dma_start(out=li[hs, :], in_=I["s5_lam_im"][l].rearrange("d g p -> p (d g)"), **NCDMA), writes=["li"])
                    S.dma("sp", lambda e: e.dma_start(out=ldt[:], in_=I["s5_log_dt"][l].rearrange("d g -> (d g)").partition_broadcast(128)), writes=["ldt"])
                    bre = I["s5_b_re"][l].rearrange("d g p m -> p (d g) m")
                    bim = I["s5_b_im"][l].rearrange("d g p m -> p (d g) m")
                    S.dma("sp", lambda e: e.dma_start(out=Y1[0:64], in_=bre), writes=["Y1"])
                    S.dma("sp", lambda e: e.dma_start(out=Y1[64:128], in_=bim), writes=["Y1"])
                    S.dma("sp", lambda e: e.dma_start(out=Y2[0:64], in_=bim), writes=["Y2"])
                    S.dma("sp", lambda e: e.dma_start(out=Y2[64:128], in_=bre), writes=["Y2"])
                    cre = I["s5_c_re"][l].rearrange("d g n p -> (d g n) p")
                    cim = I["s5_c_im"][l].rearrange("d g n p -> (d g n) p")
                    for ch in range(4):
                        rs_ = slice(ch * 128, (ch + 1) * 128)
                        S.dma("sp", lambda e, ch=ch, rs_=rs_: e.dma_start(out=CC1[:, ch, 0:64], in_=cre[rs_, :]), writes=["CC1"])
                        S.dma("sp", lambda e, ch=ch, rs_=rs_: e.dma_start(out=CC1[:, ch, 64:128], in_=cim[rs_, :]), writes=["CC1"])
                        S.dma("sp", lambda e, ch=ch, rs_=rs_: e.dma_start(out=CC2[:, ch, 0:64], in_=cim[rs_, :]), writes=["CC2"])
                        S.dma("sp", lambda e, ch=ch, rs_=rs_: e.dma_start(out=CC2[:, ch, 64:128], in_=cre[rs_, :]), writes=["CC2"])
                    S.dma("sp", lambda e: e.dma_start(out=msk[:], in_=I["s5_mask"].rearrange("d p m -> p d m")), writes=["msk"])
                    for s_ in range(8):
                        S.dma("sp", lambda e, s_=s_: e.dma_start(out=dsk[s_ * 16:(s_ + 1) * 16, :], in_=I["s5_d"][l].rearrange("g m -> m g"), **NCDMA), writes=["dsk"])

                    if CUT < 10:
                        S.flush()
                        return
                    def V(fn, reads, writes, eng="dve"):
                        S.op(eng, fn, reads=reads, writes=writes)

                    def tt(o, a, b, op, on, an, bn):
                        V(lambda e: e.tensor_tensor(out=o, in0=a, in1=b, op=op), [an, bn], [on])

                    def ts(o, a, s1, op0, on, an, s2=None, op1=None):
                        if op1 is None:
                            V(lambda e: e.tensor_scalar(out=o, in0=a, scalar1=s1, scalar2=None, op0=op0), [an], [on])
                        else:
                            V(lambda e: e.tensor_scalar(out=o, in0=a, scalar1=s1, scalar2=s2, op0=op0, op1=op1), [an], [on])

                    V(lambda e: e.tensor_scalar(out=Y2[0:64], in0=Y2[0:64], scalar1=-1.0, scalar2=None, op0=ALU.mult), ["Y2"], ["Y2"])
                    ts(lr[:], lr[:], -1e-4, ALU.min, "lr", "lr")
                    V(lambda e: e.activation(out=dtv[:], in_=ldt[:], func=AF.Exp), ["ldt"], ["dtv"], eng="act")
                    tt(x1[:], lr[:], dtv[:], ALU.mult, "x1", "lr", "dtv")
                    V(lambda e: e.activation(out=mag[:], in_=x1[:], func=AF.Exp), ["x1"], ["mag"], eng="act")
                    tt(ang[:], li[:], dtv[:], ALU.mult, "ang", "li", "dtv")

                    def sinred(dst, dn, shift):
                        ts(s0[:], ang[:], shift, ALU.add, "s0", "ang")
                        ts(tq[:], s0[:], 1.0 / TWO_PI, ALU.mult, "tq", "s0")
                        ts(nn_[:], tq[:], MAGIC, ALU.add, "nn", "tq")
                        ts(tq[:], nn_[:], -MAGIC, ALU.add, "tq", "nn")
                        V(lambda e: e.scalar_tensor_tensor(out=red[:], in0=tq[:], scalar=-TWO_PI, in1=s0[:], op0=ALU.mult, op1=ALU.add), ["tq", "s0"], ["red"])
                        ts(red[:], red[:], -3.1415925, ALU.max, "red", "red", 3.1415925, ALU.min)
                        V(lambda e: e.activation(out=dst, in_=red[:], func=AF.Sin), ["red"], [dn], eng="act")

                    sinred(sinv[:], "sinv", 0.0)
                    sinred(cosv[:], "cosv", float(np.pi / 2))
                    A1 = PW[:, 1, 0, :]
                    B1 = PW[:, 1, 1, :]
                    tt(A1, mag[:], cosv[:], ALU.mult, "PW", "mag", "cosv")
                    tt(B1, mag[:], sinv[:], ALU.mult, "PW", "mag", "sinv")
                    V(lambda e: e.memset(PW[:, 0, 0, :], 1.0), [], ["PW"], eng="pool")
                    V(lambda e: e.memset(PW[:, 0, 1, :], 0.0), [], ["PW"], eng="pool")
                    V(lambda e: e.memset(IPW[:, 0, 0, :], 1.0), [], ["IPW"], eng="pool")
                    V(lambda e: e.memset(IPW[:, 0, 1, :], 0.0), [], ["IPW"], eng="pool")
                    ts(am1[:], A1, -1.0, ALU.add, "am1", "PW")
                    tt(u1[:], lr[:], lr[:], ALU.mult, "u1", "lr", "lr")
                    tt(u2[:], li[:], li[:], ALU.mult, "u2", "li", "li")
                    tt(den[:], u1[:], u2[:], ALU.add, "den", "u1", "u2")
                    V(lambda e: e.reciprocal(out=rden[:], in_=den[:]), ["den"], ["rden"])
                    tt(u1[:], am1[:], lr[:], ALU.mult, "u1", "am1", "lr")
                    tt(u2[:], B1, li[:], ALU.mult, "u2", "PW", "li")
                    tt(u3[:], u1[:], u2[:], ALU.add, "u3", "u1", "u2")
                    tt(wr[:], u3[:], rden[:], ALU.mult, "wr", "u3", "rden")
                    tt(u1[:], B1, lr[:], ALU.mult, "u1", "PW", "lr")
                    tt(u2[:], am1[:], li[:], ALU.mult, "u2", "am1", "li")
                    tt(u3[:], u1[:], u2[:], ALU.subtract, "u3", "u1", "u2")
                    tt(wi[:], u3[:], rden[:], ALU.mult, "wi", "u3", "rden")

                    def bc(t):
                        return t.unsqueeze(2).broadcast_to([128, 32, 16])

                    tt(e1[:], Y1[:], bc(wr[:, :]), ALU.mult, "e1", "Y1", "wr")
                    tt(e2[:], Y2[:], bc(wi[:, :]), ALU.mult, "e2", "Y2", "wi")
                    tt(BB1[:], e1[:], e2[:], ALU.add, "BB1", "e1", "e2")
                    tt(e1[:], Y2[:], bc(wr[:, :]), ALU.mult, "e1", "Y2", "wr")
                    tt(e2[:], Y1[:], bc(wi[:, :]), ALU.mult, "e2", "Y1", "wi")
                    tt(BB2[:], e1[:], e2[:], ALU.subtract, "BB2", "e1", "e2")

                    def cmul(oa, ob, a, b, c_, d_, on, rn):
                        tt(u1[:], a, c_, ALU.mult, "u1", rn[0], rn[1])
                        tt(u2[:], b, d_, ALU.mult, "u2", rn[0], rn[1])
                        tt(u3[:], a, d_, ALU.mult, "u3", rn[0], rn[1])
                        tt(u4[:], b, c_, ALU.mult, "u4", rn[0], rn[1])
                        tt(oa, u1[:], u2[:], ALU.subtract, on, "u1", "u2")
                        tt(ob, u3[:], u4[:], ALU.add, on, "u3", "u4")

                    for k in range(1, 8):
                        cmul(PW[:, k + 1, 0, :], PW[:, k + 1, 1, :], PW[:, k, 0, :], PW[:, k, 1, :], A1, B1, "PW", ("PW", "PW"))
                    tt(u1[:], A1, A1, ALU.mult, "u1", "PW", "PW")
                    tt(u2[:], B1, B1, ALU.mult, "u2", "PW", "PW")
                    tt(den[:], u1[:], u2[:], ALU.add, "den", "u1", "u2")
                    V(lambda e: e.reciprocal(out=rden[:], in_=den[:]), ["den"], ["rden"])
                    tt(IPW[:, 1, 0, :], A1, rden[:], ALU.mult, "IPW", "PW", "rden")
                    V(lambda e: e.scalar_tensor_tensor(out=IPW[:, 1, 1, :], in0=B1, scalar=-1.0, in1=rden[:], op0=ALU.mult, op1=ALU.mult), ["PW", "rden"], ["IPW"])
                    for k in range(1, 7):
                        cmul(IPW[:, k + 1, 0, :], IPW[:, k + 1, 1, :], IPW[:, k, 0, :], IPW[:, k, 1, :], IPW[:, 1, 0, :], IPW[:, 1, 1, :], "IPW", ("IPW", "IPW"))
                    V(lambda e: e.tensor_copy(out=MU[:, 0, :, :], in_=PW[:, 8, :, :]), ["PW"], ["MU"])
                    for j in range(10):
                        cmul(MU[:, j + 1, 0, :], MU[:, j + 1, 1, :], MU[:, j, 0, :], MU[:, j, 1, :], MU[:, j, 0, :], MU[:, j, 1, :], "MU", ("MU", "MU"))
                    if CUT < 11:
                        S.flush()
                        return
                    for (CC, XX, ccn, xn) in ((CC1, X1, "CC1", "X1"), (CC2, X2, "CC2", "X2")):
                        for ch in range(4):
                            pb, pbn = bank()
                            S.op("pe", lambda e, CC=CC, ch=ch, pb=pb: e.transpose(out=pb[:, 0:128], in_=CC[:, ch, :], identity=ident[:]), reads=[ccn], writes=[pbn])
                            V(lambda e, XX=XX, ch=ch, pb=pb: e.tensor_copy(out=XX[:, ch * 8:(ch + 1) * 8, :], in_=pb[:, 0:128].rearrange("p (a n) -> p a n", n=16)), [pbn], [xn])
                    V(lambda e: e.tensor_scalar(out=X1[64:128], in0=X1[64:128], scalar1=-1.0, scalar2=None, op0=ALU.mult), ["X1"], ["X1"])
                    V(lambda e: e.tensor_scalar(out=X2[:], in0=X2[:], scalar1=-1.0, scalar2=None, op0=ALU.mult), ["X2"], ["X2"])

                    def table(dst, dn, M1, M2, m1n, m2n, pa, pb_, pn, sig):
                        tt(e1[:], M1[:], bc(pa), ALU.mult, "e1", m1n, pn)
                        tt(e2[:], M2[:], bc(pb_), ALU.mult, "e2", m2n, pn)
                        tt(dst[:, 0:16, sig, :], e1[:, 0:16, :], e2[:, 0:16, :], ALU.add, dn, "e1", "e2")
                        tt(dst[:, 16:32, 7 - sig, :], e1[:, 16:32, :], e2[:, 16:32, :], ALU.add, dn, "e1", "e2")

                    if CUT < 12:
                        S.flush()
                        return
                    for sig in range(8):
                        table(EBt, "EBt", BB1, BB2, "BB1", "BB2", PW[:, 7 - sig, 0, :], PW[:, 7 - sig, 1, :], "PW", sig)
                        table(ENt, "ENt", BB1, BB2, "BB1", "BB2", IPW[:, sig, 0, :], IPW[:, sig, 1, :], "IPW", sig)
                        table(GPt, "GPt", X1, X2, "X1", "X2", PW[:, sig, 0, :], PW[:, sig, 1, :], "PW", sig)
                        table(CBt, "CBt", X1, X2, "X1", "X2", PW[:, sig + 1, 0, :], PW[:, sig + 1, 1, :], "PW", sig)
                    if CUT < 13:
                        S.flush()
                        return
                    if DEBUG == 3:
                        dbt = sb(ps, "dbt3", [128, 2048], F32)
                        S.op("pool", lambda e: e.memset(dbt[:], 0.0), writes=["dbt"])
                        srcs = [X1[:, 0:8, :].rearrange("p a n -> p (a n)"), CBt[:, 0, :, :].rearrange("p s n -> p (s n)"), GPt[:, 0, :, :].rearrange("p s n -> p (s n)"),
                                BB1[:, 0:8, :].rearrange("p a n -> p (a n)"), EBt[:, 0, :, :].rearrange("p s n -> p (s n)"), Y1[:, 0:8, :].rearrange("p a n -> p (a n)"),
                                X2[:, 0:8, :].rearrange("p a n -> p (a n)"), e1[:, 0:8, :].rearrange("p a n -> p (a n)")]
                        for i_, src_ in enumerate(srcs):
                            S.op("dve", lambda e, i_=i_, src_=src_: e.tensor_copy(out=dbt[:, i_ * 128:(i_ + 1) * 128], in_=src_), reads=["dbt", "X1", "X2", "CBt", "GPt", "BB1", "EBt", "Y1", "e1"], writes=["dbt"])
                        S.op("dve", lambda e: e.tensor_copy(out=dbt[:, 1024:1024 + 576], in_=PW[:].rearrange("p a b c -> p (a b c)")), reads=["dbt", "PW"], writes=["dbt"])
                        S.dma("sp", lambda e: e.dma_start(out=R["DBG"], in_=dbt[:]), reads=["dbt"], writes=["DBG"])
                        S.flush()
                        return
                    V(lambda e: e.tensor_copy(out=cblk[:].rearrange("p a b -> p (a b)"), in_=CBt[:].rearrange("p a s n -> p (a s n)")), ["CBt"], ["cblk"])
                    if CUT < 15:
                        S.flush()
                        return
                    for dg in range(32):
                        pb, pbn = bank()
                        S.op("pe", lambda e, dg=dg, pb=pb: e.transpose(out=pb[:, 0:128], in_=EBt[:, dg, :, :].rearrange("p s m -> p (s m)"), identity=ident[:]),
                             reads=["EBt"], writes=[pbn])
                        eg = evac_eng()
                        S.op(eg, copy_op(eg, bblk[:, dg, :], pb[:, 0:128]), reads=[pbn], writes=["bblk"])
                        if CUT < 16:
                            continue
                        pb2, pbn2 = bank()
                        S.op("pe", lambda e, dg=dg, pb2=pb2: e.matmul(pb2[:, 0:128], lhsT=ENt[:, dg, :, :].rearrange("p s m -> p (s m)"),
                                                                   rhs=GPt[:, dg, :, :].rearrange("p s m -> p (s m)"), start=True, stop=True),
                             reads=["ENt", "GPt"], writes=[pbn2])
                        V(lambda e, dg=dg, pb2=pb2: e.tensor_tensor(out=Dm[:, dg, :], in0=pb2[:, 0:128], in1=msk[:, dg // 16, :], op=ALU.mult), [pbn2, "msk"], ["Dm"])
                    for g in range(16 if CUT >= 17 else 0):
                        V(lambda e, g=g: e.tensor_tensor(out=Dm[:, g, :], in0=Dm[:, g, :], in1=Dm[:, 16 + g, :], op=ALU.add), ["Dm"], ["Dm"])
                        V(lambda e, g=g: e.scalar_tensor_tensor(out=dsum[:, g, :], in0=ident[:], scalar=dsk[:, g:g + 1], in1=Dm[:, g, :], op0=ALU.mult, op1=ALU.add),
                          ["Dm", "dsk"], ["dsum"])
                    S.flush()

                if CUT < 18:
                    return
                if DEBUG == 2:
                    with ExitStack() as pd:
                        dbt = sb(pd, "dbt2", [128, 2048], F32)
                        S.op("pool", lambda e: e.memset(dbt[:], 0.0), writes=["dbt"])
                        for i_, src_ in enumerate([dsum[:, 0, :], bblk[:, 0, :], cblk[:, 0, :], bblk[:, 16, :], cblk[:, 16, :], dsum[:, 5, :]]):
                            S.op("dve", lambda e, i_=i_, src_=src_: e.tensor_copy(out=dbt[:, i_ * 128:(i_ + 1) * 128], in_=src_), reads=["dbt"], writes=["dbt"])
                        S.op("dve", lambda e: e.tensor_copy(out=dbt[:, 768:768 + 704], in_=MU[:].rearrange("p a b c -> p (a b c)")), reads=["dbt"], writes=["dbt"])
                        S.dma("sp", lambda e: e.dma_start(out=R["DBG"], in_=dbt[:]), reads=["dbt"], writes=["DBG"])
                        S.flush()
                with ExitStack() as ps:
                    selin = sb(ps, "selin", [128, 8, 8, 128], BF16)
                    selout = sb(ps, "selout", [128, 8, 8, 128], BF16)
                    for g in range(8):
                        S.dma("pool", lambda e, g=g: e.dma_start(out=selin[:, g, :, :], in_=I["s5_selin"][:, g, :, :]), writes=["selin"])
                        S.dma("pool", lambda e, g=g: e.dma_start(out=selout[:, g, :, :], in_=I["s5_selout"][:, g, :, :]), writes=["selout"])
                    wglu = sb(ps, "wglu", [128, 2, 256], BF16)
                    S.dma("pool", lambda e: e.dma_start(out=wglu[:], in_=I["s5_w_glu"][l].rearrange("(k p) n -> p k n", p=128)), writes=["wglu"])
                    uT = sb(ps, "uT", [128, T], BF16)
                    Vg = sb(ps, "Vg", [128, Q], BF16)
                    VRg = sb(ps, "VRg", [128, Q], BF16)
                    Hm = [sb(ps, "Hm%d" % d, [128, Q], F32) for d in range(2)]
                    Hs = [sb(ps, "Hs%d" % d, [128, Q + 2], BF16) for d in range(2)]
                    HinN = sb(ps, "HinN", [128, Q], BF16)
                    Rt = [sb(ps, "Rt%d" % i, [128, 128], F32) for i in range(2)]
                    Rall = [[sb(ps, "Rall%d_%d" % (par, d), [128, 11, 128], BF16) for d in range(2)] for par in range(2)]
                    Yg = [sb(ps, "Yg%d" % g, [128, Q], BF16) for g in range(8)]
                    yT = sb(ps, "yT", [128, T], F32)
                    gbf = sb(ps, "gbf", [128, 2, T], BF16)
                    gl = [sb(ps, "gl%d" % i, [128, 512], F32) for i in range(3)]
                    sgt = sb(ps, "sgt", [128, 512], BF16)
                    sbo = [sb(ps, "sbo%d" % i, [128, 512], BF16) for i in range(2)]
                    for d in range(2):
                        S.op("pool", lambda e, d=d: e.memset(Hs[d][:, 0:2], 0.0), writes=["Hs%d" % d])
                    uTv = uT[:, :].rearrange("p (c s) -> p c s", s=8)
                    yTv = yT[:, :].rearrange("p (c t) -> p c t", t=8)
                    rbc = [0]
                    for hf in range(2):
                        S.dma("sp", lambda e, hf=hf: e.dma_start(out=uT[:], in_=R["UT"][hf * 128:(hf + 1) * 128, :]), writes=["uT"])
                        for g in range(8):
                            gi = hf * 8 + g
                            for d in range(2):
                                dg = d * 16 + gi
                                rt = Rt[d]
                                rtn = "Rt%d" % d
                                for j in range(11):
                                    S.op("dve", lambda e, j=j, dg=dg, rt=rt: e.tensor_scalar(out=rt[:], in0=ident[:], scalar1=MU[:, j, 0, dg:dg + 1], scalar2=None, op0=ALU.mult),
                                         reads=["MU"], writes=[rtn])
                                    ro = Rall[g % 2][d][:, j, :]
                                    S.op("dve", lambda e, j=j, dg=dg, ro=ro, rt=rt: e.scalar_tensor_tensor(out=ro, in0=jm[:], scalar=MU[:, j, 1, dg:dg + 1], in1=rt[:], op0=ALU.mult, op1=ALU.add),
                                         reads=["MU", rtn, "jm"], writes=["Rall%d_%d" % (g % 2, d)])
                            for (c0, c1) in PCS:
                                pb, pbn = bank()
                                for s_ in range(8):
                                    S.op("pe", lambda e, g=g, s_=s_, pb=pb, c0=c0, c1=c1: e.matmul(pb[:, 0:c1 - c0], lhsT=selin[:, g, s_, :], rhs=uTv[:, c0:c1, s_],
                                                                                                 start=(s_ == 0), stop=(s_ == 7)), reads=["selin", "uT"], writes=[pbn])
                                eg = evac_eng()
                                S.op(eg, copy_op(eg, Vg[:, c0:c1], pb[:, 0:c1 - c0]), reads=[pbn], writes=["Vg"])
                            S.op("pool", lambda e: e.tensor_copy(out=VRg[:, 0:32], in_=Vg[:, 31::-1]), reads=["Vg"], writes=["VRg"])
                            S.op("pool", lambda e: e.tensor_copy(out=VRg[:, 32:Q], in_=Vg[:, Q - 1:31:-1]), reads=["Vg"], writes=["VRg"])
                            for d in range(2):
                                dg = d * 16 + gi
                                src = Vg if d == 0 else VRg
                                srcn = "Vg" if d == 0 else "VRg"
                                hm, hs = Hm[d], Hs[d]
                                hmn, hsn = "Hm%d" % d, "Hs%d" % d
                                for (c0, c1) in PCS:
                                    pb, pbn = bank()
                                    S.op("pe", lambda e, dg=dg, pb=pb, c0=c0, c1=c1, src=src: e.matmul(pb[:, 0:c1 - c0], lhsT=bblk[:, dg, :], rhs=src[:, c0:c1], start=True, stop=True),
                                         reads=["bblk", srcn], writes=[pbn])
                                    S.op("dve", lambda e, pb=pb, c0=c0, c1=c1, hm=hm: e.tensor_copy(out=hm[:, c0:c1], in_=pb[:, 0:c1 - c0]), reads=[pbn], writes=[hmn])
                                    S.op("act", lambda e, c0=c0, c1=c1, hs=hs, hm=hm: e.activation(out=hs[:, 2 + c0:2 + c1], in_=hm[:, c0:c1], func=AF.Copy), reads=[hmn], writes=[hsn])
                            for j in range(11):
                                sh = 1 << j
                                pieces = []
                                q0 = sh
                                while q0 < Q:
                                    q1 = min(q0 + 512, Q)
                                    pieces.append((q0, q1))
                                    q0 = q1
                                pend = []
                                for d in range(2):
                                    dg = d * 16 + gi
                                    hm, hs = Hm[d], Hs[d]
                                    hmn, hsn = "Hm%d" % d, "Hs%d" % d
                                    rb = Rall[g % 2][d][:, j, :]
                                    rbn = "Rall%d_%d" % (g % 2, d)
                                    pbs = []
                                    for (q0, q1) in pieces:
                                        pb, pbn = bank()
                                        pbs.append((pb, pbn))
                                        S.op("pe", lambda e, pb=pb, q0=q0, q1=q1, sh=sh, rb=rb, hs=hs: e.matmul(pb[:, 0:q1 - q0], lhsT=rb, rhs=hs[:, 2 + q0 - sh:2 + q1 - sh], start=True, stop=True),
                                             reads=[rbn, hsn], writes=[pbn])
                                    pend.append((hm, hs, hmn, hsn, pbs))
                                for (hm, hs, hmn, hsn, pbs) in pend:
                                    for (q0, q1), (pb, pbn) in zip(pieces, pbs):
                                        S.op("dve", lambda e, pb=pb, q0=q0, q1=q1, hm=hm: e.tensor_tensor(out=hm[:, q0:q1], in0=pb[:, 0:q1 - q0], in1=hm[:, q0:q1], op=ALU.add),
                                             reads=[pbn, hmn], writes=[hmn])
                                    for (a0, a1) in ((0, 512), (512, 1024), (1024, Q)):
                                        if a1 <= sh:
                                            continue
                                        S.op("act", lambda e, a0=a0, a1=a1, hm=hm, hs=hs: e.activation(out=hs[:, 2 + a0:2 + a1], in_=hm[:, a0:a1], func=AF.Copy), reads=[hmn], writes=[hsn])
                            S.op("pool", lambda e: e.tensor_copy(out=HinN[:, 0:32], in_=Hs[1][:, 32:0:-1]), reads=["Hs1"], writes=["HinN"])
                            S.op("pool", lambda e: e.tensor_copy(out=HinN[:, 32:Q], in_=Hs[1][:, Q:32:-1]), reads=["Hs1"], writes=["HinN"])
                            yg = Yg[g]
                            for (c0, c1) in (PCS if CUT >= 23 else []):
                                pb, pbn = bank()
                                S.op("pe", lambda e, gi=gi, pb=pb, c0=c0, c1=c1: e.matmul(pb[:, 0:c1 - c0], lhsT=dsum[:, gi, :], rhs=Vg[:, c0:c1], start=True, stop=False),
                                     reads=["dsum", "Vg"], writes=[pbn])
                                S.op("pe", lambda e, gi=gi, pb=pb, c0=c0, c1=c1: e.matmul(pb[:, 0:c1 - c0], lhsT=cblk[:, gi, :], rhs=Hs[0][:, 1 + c0:1 + c1], start=False, stop=False),
                                     reads=["cblk", "Hs0"], writes=[pbn])
                                S.op("pe", lambda e, gi=gi, pb=pb, c0=c0, c1=c1: e.matmul(pb[:, 0:c1 - c0], lhsT=cblk[:, 16 + gi, :], rhs=HinN[:, c0:c1], start=False, stop=True),
                                     reads=["cblk", "HinN"], writes=[pbn])
                                eg = evac_eng()
                                S.op(eg, copy_op(eg, yg[:, c0:c1], pb[:, 0:c1 - c0]), reads=[pbn], writes=["Yg%d" % g])
                        for t_ in range(8 if CUT >= 24 else 0):
                            for (c0, c1) in PCS:
                                pb, pbn = bank()
                                for g in range(8):
                                    S.op("pe", lambda e, g=g, t_=t_, pb=pb, c0=c0, c1=c1: e.matmul(pb[:, 0:c1 - c0], lhsT=selout[:, t_, g, :], rhs=Yg[g][:, c0:c1],
                                                                                                 start=(g == 0), stop=(g == 7)), reads=["selout", "Yg%d" % g], writes=[pbn])
                                eg = evac_eng()
                                S.op(eg, copy_op(eg, yTv[:, c0:c1, t_], pb[:, 0:c1 - c0]), reads=[pbn], writes=["yT"])
                        if DEBUG:
                            S.dma("sp", lambda e, hf=hf: e.dma_start(out=R["YT"][hf * 128:(hf + 1) * 128, :], in_=yT[:]), reads=["yT"], writes=["YTd"])
                        for i, (t0, n, v) in enumerate(tiles_of(512) if CUT >= 25 else []):
                            a_, b_, c_ = gl
                            ys = yT[:, t0:t0 + n]
                            S.op("pool", lambda e, ys=ys, n=n: e.tensor_tensor(out=a_[:, :n], in0=ys, in1=ys, op=ALU.mult), reads=["yT"], writes=["gl0"])
                            S.op("dve", lambda e, n=n: e.tensor_scalar(out=b_[:, :n], in0=a_[:, :n], scalar1=0.044715, scalar2=1.0, op0=ALU.mult, op1=ALU.add), reads=["gl0"], writes=["gl1"])
                            S.op("pool", lambda e, ys=ys, n=n: e.tensor_tensor(out=a_[:, :n], in0=b_[:, :n], in1=ys, op=ALU.mult), reads=["gl1", "yT"], writes=["gl0"])
                            S.op("act", lambda e, n=n: e.activation(out=c_[:, :n], in_=a_[:, :n], func=AF.Sigmoid, scale=1.5957691216), reads=["gl0"], writes=["gl2"])
                            S.op("dve", lambda e, ys=ys, n=n, t0=t0, hf=hf: e.tensor_tensor(out=gbf[:, hf, t0:t0 + n], in0=c_[:, :n], in1=ys, op=ALU.mult), reads=["gl2", "yT"], writes=["gbf"])
                    SBv = R["SB"].rearrange("(c p) t -> p c t", p=128)
                    for i, (t0, n, v) in enumerate(tiles_of(512) if CUT >= 26 else []):
                        for oc in range(2):
                            pb, pbn = bank()
                            for kc in range(2):
                                S.op("pe", lambda e, oc=oc, kc=kc, pb=pb, t0=t0, n=n: e.matmul(pb[:, :n], lhsT=wglu[:, kc, oc * 128:(oc + 1) * 128], rhs=gbf[:, kc, t0:t0 + n],
                                                                                             start=(kc == 0), stop=(kc == 1)), reads=["wglu", "gbf"], writes=[pbn])
                            S.op("act", lambda e, pb=pb, n=n: e.activation(out=sgt[:, :n], in_=pb[:, :n], func=AF.Sigmoid), reads=[pbn], writes=["sgt"])
                            so = sbo[oc]
                            S.op("dve", lambda e, so=so, oc=oc, t0=t0, n=n: e.tensor_tensor(out=so[:, :n], in0=sgt[:, :n], in1=gbf[:, oc, t0:t0 + n], op=ALU.mult),
                                 reads=["sgt", "gbf"], writes=["sbo%d" % oc])
                            S.dma("sp", lambda e, so=so, oc=oc, t0=t0, n=n: e.dma_start(out=SBv[:, oc, t0:t0 + n], in_=so[:, :n]), reads=["sbo%d" % oc], writes=["SBo"])
                    S.flush()

        def phase3a(l):
            NT = 512
            tl = tiles_of(NT, with_ctx=(l < DEPTH - 1))
            with ExitStack() as ps:
                wa = sb(ps, "wa", [128, 4, D], BF16)
                wbra = sb(ps, "wbra", [128, 2, D], BF16)
                wb = sb(ps, "wb", [128, 2, D], BF16)
                wc = sb(ps, "wc", [128, 4, D], BF16)
                wo = sb(ps, "wo", [128, 8, D], BF16)
                cs64 = sb(ps, "cs64", [128, 2, 128], BF16)
                S.dma("pool", lambda e: e.dma_start(out=cs64[:], in_=I["cs64"].rearrange("a p m -> p a m")), writes=["cs64"])
                S.dma("pool", lambda e: e.dma_start(out=wbra[:], in_=I["w_br_a"][l].rearrange("(k p) n -> p k n", p=128)), writes=["wbra"])
                S.dma("pool", lambda e: e.dma_start(out=wb[:], in_=I["w_br_b"][l].rearrange("(k p) n -> p k n", p=128)), writes=["wb"])
                S.dma("pool", lambda e: e.dma_start(out=wc[:], in_=I["w_br_c"][l].rearrange("(k p) n -> p k n", p=128)), writes=["wc"])
                S.dma("pool", lambda e: e.dma_start(out=wo[:], in_=I["w_out"][l].rearrange("(k p) n -> p k n", p=128)), writes=["wo"])
                for a in range(2 if CUT > -1 else 0):
                    for q in range(2):
                        for hh in range(2):
                            pb, pbn = bank()
                            S.op("pe", lambda e, a=a, q=q, hh=hh, pb=pb: e.matmul(pb[:, :], lhsT=cs64[:, a, :], rhs=wbra[:, q, hh * 512:(hh + 1) * 512],
                                                                                 start=True, stop=True), reads=["cs64", "wbra"], writes=[pbn])
                            eg = evac_eng()
                            S.op(eg, copy_op(eg, wa[:, a * 2 + q, hh * 512:(hh + 1) * 512], pb[:, :]), reads=[pbn], writes=["wa"])
                xTs = [sb(ps, "xT%d" % i, [128, 8, NT], F32) for i in range(2)]
                fsn = [sb(ps, "fsn%d" % i, [128, 10, NT], BF16) for i in range(2)]
                gts = [sb(ps, "gts%d" % i, [128, 24, NT], BF16) for i in range(2)]
                mT = sb(ps, "mT", [128, 8, NT], BF16)
                hT = sb(ps, "hT", [128, 8, NT], BF16)
                sq = sb(ps, "sq", [128, 8, NT], BF16)
                rstd = sb(ps, "rstd", [128, NT], F32)
                tmpa = sb(ps, "tmpa", [128, NT], F32)
                tmps = [sb(ps, "tmps%d" % i, [128, NT], F32) for i in range(2)]
                t1s = [sb(ps, "t1_%d" % i, [128, NT], F32) for i in range(2)]
                t2s = [sb(ps, "t2_%d" % i, [128, NT], F32) for i in range(2)]
                t3s = [sb(ps, "t3_%d" % i, [128, NT], F32) for i in range(2)]
                XTv = R["XT"].rearrange("(c p) t -> p c t", p=128)
                XMv = R["XM"].rearrange("(c p) t -> p c t", p=128)
                H2v = R["H2"].rearrange("(c p) t -> p c t", p=128)
                FAv = R["FA"].rearrange("(c p) t -> p c t", p=128)
                SBv = R["SB"].rearrange("(c p) t -> p c t", p=128)
                NCv = R["NCT"].rearrange("(c p) t -> p c t", p=128)
                GTv = R["GT"].rearrange("(c p) t -> p c t", p=128)

                def load(i):
                    t0, n, v = tl[i]
                    b = i % 2
                    S.dma("sp", lambda e: e.dma_start(out=xTs[b][:, :, :n], in_=XTv[:, :, t0:t0 + n]), writes=["xT%d" % b])
                    S.dma("sp", lambda e: e.dma_start(out=fsn[b][:, 0:4, :n], in_=FAv[:, :, t0:t0 + n]), writes=["fsnA%d" % b])
                    S.dma("sp", lambda e: e.dma_start(out=fsn[b][:, 4:6, :n], in_=SBv[:, :, t0:t0 + n]), writes=["fsnB%d" % b])
                    S.dma("sp", lambda e: e.dma_start(out=fsn[b][:, 6:10, :n], in_=NCv[:, :, t0:t0 + n]), writes=["fsnC%d" % b])
                    for g in range(3):
                        S.dma("sp", lambda e, g=g: e.dma_start(out=gts[b][:, g * 8:(g + 1) * 8, :n], in_=GTv[:, g * 8:(g + 1) * 8, t0:t0 + n]),
                              writes=["gts%d_%d" % (b, g)])

                def compute(i):
                    t0, n, v = tl[i]
                    b = i % 2
                    xT = xTs[b]
                    xn = "xT%d" % b
                    f = fsn[b]
                    g = gts[b]
                    if CUT < 1:
                        return
                    for d in range(8):
                        dd = d % 2
                        ds = slice(d * 128, (d + 1) * 128)
                        pa, pan = bank()
                        for k in range(4):
                            S.op("pe", lambda e, k=k, pa=pa, ds=ds: e.matmul(pa[:, :n], lhsT=wa[:, k, ds], rhs=f[:, k, :n], start=(k == 0), stop=(k == 3)),
                                 reads=["wa", "fsnA%d" % b], writes=[pan])
                        pbk, pbn = bank()
                        for k in range(2):
                            S.op("pe", lambda e, k=k, pbk=pbk, ds=ds: e.matmul(pbk[:, :n], lhsT=wb[:, k, ds], rhs=f[:, 4 + k, :n], start=(k == 0), stop=(k == 1)),
                                 reads=["wb", "fsnB%d" % b], writes=[pbn])
                        pc, pcn = bank()
                        for k in range(4):
                            S.op("pe", lambda e, k=k, pc=pc, ds=ds: e.matmul(pc[:, :n], lhsT=wc[:, k, ds], rhs=f[:, 6 + k, :n], start=(k == 0), stop=(k == 3)),
                                 reads=["wc", "fsnC%d" % b], writes=[pcn])
                        t1, t2, t3 = t1s[dd], t2s[dd], t3s[dd]
                        S.op("dve", lambda e, pa=pa, t1=t1, d=d: e.tensor_tensor(out=t1[:, :n], in0=pa[:, :n], in1=g[:, d, :n], op=ALU.mult),
                             reads=[pan, "gts%d_0" % b], writes=["t1_%d" % dd])
                        S.op("dve", lambda e, pbk=pbk, t2=t2, d=d: e.tensor_tensor(out=t2[:, :n], in0=pbk[:, :n], in1=g[:, 8 + d, :n], op=ALU.mult),
                             reads=[pbn, "gts%d_1" % b], writes=["t2_%d" % dd])
                        S.op("dve", lambda e, pc=pc, t3=t3, d=d: e.tensor_tensor(out=t3[:, :n], in0=pc[:, :n], in1=g[:, 16 + d, :n], op=ALU.mult),
                             reads=[pcn, "gts%d_2" % b], writes=["t3_%d" % dd])
                        S.op("pool", lambda e, t1=t1, t2=t2: e.tensor_tensor(out=t1[:, :n], in0=t1[:, :n], in1=t2[:, :n], op=ALU.add),
                             reads=["t1_%d" % dd, "t2_%d" % dd], writes=["t1_%d" % dd])
                        S.op("pool", lambda e, t1=t1, t3=t3, d=d: e.tensor_tensor(out=mT[:, d, :n], in0=t1[:, :n], in1=t3[:, :n], op=ALU.add),
                             reads=["t1_%d" % dd, "t3_%d" % dd], writes=["mT"])
                    if CUT < 2:
                        return
                    for d in range(8):
                        ds = slice(d * 128, (d + 1) * 128)
                        po, pon = bank()
                        for k in range(8):
                            S.op("pe", lambda e, k=k, po=po, ds=ds: e.matmul(po[:, :n], lhsT=wo[:, k, ds], rhs=mT[:, k, :n], start=(k == 0), stop=(k == 7)),
                                 reads=["wo", "mT"], writes=[pon])
                        S.op("dve", lambda e, po=po, d=d: e.scalar_tensor_tensor(out=xT[:, d, :n], in0=po[:, :n], scalar=mod[:, l, 16 + d, v:v + 1],
                                                                               in1=xT[:, d, :n], op0=ALU.mult, op1=ALU.add),
                             reads=[pon, xn, "mod"], writes=[xn])
                    if CUT < 3:
                        return
                    S.dma("sp", lambda e: e.dma_start(out=XMv[:, :, t0:t0 + n], in_=xT[:, :, :n]), reads=[xn], writes=["XM%d" % i])
                    if CUT < 4:
                        return
                    norm_mod(l, 1, v, xT, n, sq, rstd, tmpa, tmps, hT, xn, "hT", i)
                    S.dma("sp", lambda e: e.dma_start(out=H2v[:, :, t0:t0 + n], in_=hT[:, :, :n]), reads=["hT"], writes=["H2%d" % i])

                load(0)
                for i in range(len(tl)):
                    if i + 1 < len(tl):
                        load(i + 1)
                    compute(i)
                S.flush()

        def phase3b(l):
            NT = 256
            last = (l == DEPTH - 1)
            tl = tiles_of(NT, with_ctx=not last)
            with ExitStack() as ps:
                w1 = sb(ps, "w1", [128, 8, DFF], BF16)
                w2 = sb(ps, "w2", [128, 32, D], BF16)
                w1v = I["w_ff1"][l].rearrange("(k p) n -> p k n", p=128)
                w2v = I["w_ff2"][l].rearrange("(k p) n -> p k n", p=128)
                for j in range(8):
                    S.dma("pool", lambda e, j=j: e.dma_start(out=w1[:, :, j * 512:(j + 1) * 512], in_=w1v[:, :, j * 512:(j + 1) * 512]), writes=["w1_%d" % j])
                for j in range(8):
                    S.dma("pool", lambda e, j=j: e.dma_start(out=w2[:, j * 4:(j + 1) * 4, :], in_=w2v[:, j * 4:(j + 1) * 4, :]), writes=["w2_%d" % j])
                xTs = [sb(ps, "xT%d" % i, [128, 8, NT], F32) for i in range(2)]
                hTs = [sb(ps, "hT%d" % i, [128, 8, NT], BF16) for i in range(2)]
                aT = sb(ps, "aT", [128, 32, NT], BF16)
                rr = [sb(ps, "rr%d" % i, [128, NT], BF16) for i in range(2)]
                XTv = R["XT"].rearrange("(c p) t -> p c t", p=128)
                XMv = R["XM"].rearrange("(c p) t -> p c t", p=128)
                H2v = R["H2"].rearrange("(c p) t -> p c t", p=128)
                if last:
                    sq = sb(ps, "sq", [128, 8, NT], BF16)
                    rstd = sb(ps, "rstd", [128, NT], F32)
                    tmpa = sb(ps, "tmpa", [128, NT], F32)
                    yT = sb(ps, "yT", [128, 8, NT], F32)
                    otok = sb(ps, "otok", [128, 2, D], F32)

                def load(i):
                    t0, n, v = tl[i]
                    b = i % 2
                    S.dma("sp", lambda e: e.dma_start(out=xTs[b][:, :, :n], in_=XMv[:, :, t0:t0 + n]), writes=["xT%d" % b])
                    S.dma("sp", lambda e: e.dma_start(out=hTs[b][:, :, :n], in_=H2v[:, :, t0:t0 + n]), writes=["hT%d" % b])

                def compute(i):
                    t0, n, v = tl[i]
                    b = i % 2
                    xT = xTs[b]
                    xn = "xT%d" % b
                    hT = hTs[b]
                    hn = "hT%d" % b
                    for f in range(32):
                        pb, pbn = bank()
                        for k in range(8):
                            S.op("pe", lambda e, k=k, f=f, pb=pb: e.matmul(pb[:, :n], lhsT=w1[:, k, f * 128:(f + 1) * 128], rhs=hT[:, k, :n],
                                                                         start=(k == 0), stop=(k == 7)), reads=[hn, "w1_%d" % (f // 4)], writes=[pbn])
                        r = rr[f % 2]
                        rn = "rr%d" % (f % 2)
                        S.op("act", lambda e, pb=pb, r=r: e.activation(out=r[:, :n], in_=pb[:, :n], func=AF.Relu), reads=[pbn], writes=[rn])
                        S.op("pool", lambda e, r=r, f=f: e.tensor_tensor(out=aT[:, f, :n], in0=r[:, :n], in1=r[:, :n], op=ALU.mult),
                             reads=[rn], writes=["aT"])
                    for d in range(8):
                        pb, pbn = bank()
                        for f in range(32):
                            S.op("pe", lambda e, f=f, d=d, pb=pb: e.matmul(pb[:, :n], lhsT=w2[:, f, d * 128:(d + 1) * 128], rhs=aT[:, f, :n],
                                                                         start=(f == 0), stop=(f == 31)), reads=["aT", "w2_%d" % (f // 4)], writes=[pbn])
                        S.op("dve", lambda e, pb=pb, d=d: e.scalar_tensor_tensor(out=xT[:, d, :n], in0=pb[:, :n], scalar=mod[:, l, 40 + d, v:v + 1],
                                                                               in1=xT[:, d, :n], op0=ALU.mult, op1=ALU.add),
                             reads=[pbn, xn, "mod"], writes=[xn])
                    if not last:
                        S.dma("sp", lambda e: e.dma_start(out=XTv[:, :, t0:t0 + n], in_=xT[:, :, :n]), reads=[xn], writes=["XT%d" % i])
                        return
                    S.op("act", lambda e: e.activation(out=sq[:, :, :n], in_=xT[:, :, :n], func=AF.Square), reads=[xn], writes=["sq"])
                    pb, pbn = bank()
                    for c in range(8):
                        S.op("pe", lambda e, c=c, pb=pb: e.matmul(pb[:, :n], lhsT=ones_bf[:], rhs=sq[:, c, :n], start=(c == 0), stop=(c == 7)),
                             reads=["sq", "ones"], writes=[pbn])
                    S.op("act", lambda e, pb=pb: e.activation(out=tmpa[:, :n], in_=pb[:, :n], func=AF.Sqrt, scale=1.0 / D, bias=epsb[:, 0:1]),
                         reads=[pbn, "epsb"], writes=["tmpa"])
                    S.op("dve", lambda e: e.reciprocal(out=rstd[:, :n], in_=tmpa[:, :n]), reads=["tmpa"], writes=["rstd"])
                    for c in range(8):
                        S.op("dve", lambda e, c=c: e.scalar_tensor_tensor(out=yT[:, c, :n], in0=xT[:, c, :n], scalar=gfin[:, c:c + 1], in1=rstd[:, :n],
                                                                          op0=ALU.mult, op1=ALU.mult), reads=[xn, "rstd", "gfin"], writes=["yT"])
                    for s in range(n // 128):
                        for half in range(2):
                            pb, pbn = bank()
                            for cc in range(4):
                                c = half * 4 + cc
                                S.op("pe", lambda e, s=s, c=c, cc=cc, pb=pb: e.transpose(out=pb[:, cc * 128:(cc + 1) * 128], in_=yT[:, c, s * 128:(s + 1) * 128],
                                                                                       identity=ident[:]), reads=["yT", "ident"], writes=[pbn])
                            eg = evac_eng()
                            S.op(eg, copy_op(eg, otok[:, s, half * 512:(half + 1) * 512], pb[:, :]), reads=[pbn], writes=["otok"])
                    r0 = t0 - LC
                    S.dma("sp", lambda e: e.dma_start(out=out[r0:r0 + n, :].rearrange("(s p) d -> p s d", p=128), in_=otok[:, :n // 128, :]),
                          reads=["otok"], writes=["out%d" % i])

                load(0)
                for i in range(len(tl)):
                    if i + 1 < len(tl):
                        load(i + 1)
                    compute(i)
                S.flush()

        if "p0" in phases:
            phase0()
        for l in layers:
            if "p1" in phases:
                phase1(l)
            if "p2f" in phases:
                phase2_fnet(l)
            if "p2n" in phases:
                phase2_na(l)
            if "p2s" in phases:
                phase2_s5(l)
            if "p3a" in phases:
                phase3a(l)
            if "p3b" in phases:
                phase3b(l)
        if S.ops["sp"] or S.ops["pe"] or S.ops["pool"]:
            S.flush()
    return nc


ALL_PHASES = ("p0", "p1", "p2f", "p2n", "p2s", "p3a", "p3b")


def kernel(**inputs):
    f32 = lambda a: np.ascontiguousarray(np.asarray(a, dtype=np.float32))
    x = f32(inputs["x"])
    ctx = f32(inputs["ctx"])
    c = f32(inputs["c"])
    c_ctx = f32(inputs["c_ctx"])
    shared = {}
    for k in W_SHAPES:
        if k == "rpbg":
            shared[k] = na_gather_rpb(f32(inputs["na_rpb"]))
        else:
            shared[k] = f32(inputs[k])
    shared.update(_consts())
    nb = x.shape[0]
    in_maps = []
    for core in range(8):
        b = core % nb
        m = dict(shared)
        m["x"] = x[b]
        m["ctx"] = ctx[b]
        m["cvec"] = np.ascontiguousarray(np.stack([c[b], c_ctx]))
        in_maps.append(m)
    nc = build(set(ALL_PHASES))
    res = run_bass_kernel_spmd(nc, in_maps, core_ids=list(range(8)))
    out = np.stack([np.asarray(res.results[b]["out"], dtype=np.float32) for b in range(nb)], axis=0)
    return out
```
